# Optimizing a Trainium2 kernel written in Bass

```python
import math
import jax, jax.numpy as jnp
from jax import lax
import numpy as np

D_MODEL = 1024
BATCH = 16
SEQ = 2048
DEPTH = 2
DEC_BATCH = 32
DEC_SEQ = 2048
PAST_LEN = 128

BRANCH_WIDTH = 512
N_BRANCHES = 5
DIFF_HEADS = 4
DIFF_HEAD_DIM = 64
DIFF_V_DIM = 128
PARTIAL_ROPE_DIM = DIFF_HEAD_DIM // 4
ROPE_THETA = 500000.0
SC_KERNEL = 3
CONF_KERNEL = 31
MLA_HEADS = 4
MLA_NOPE = 64
MLA_ROPE = 32
MLA_V = 128
MLA_Q_RANK = 256
MLA_KV_RANK = 128
MLA_ROPE_THETA = 10000.0
MEM_HEADS = 4
MEM_HEAD_DIM = 128
N_MEM = 256
Q_BLOCK = 128
EPS = 1e-6

IN_SPLITS = (
    DIFF_HEADS * 2 * DIFF_HEAD_DIM, DIFF_HEADS * 2 * DIFF_HEAD_DIM, DIFF_HEADS * DIFF_V_DIM, BRANCH_WIDTH,
    BRANCH_WIDTH, BRANCH_WIDTH, BRANCH_WIDTH, BRANCH_WIDTH,
    2 * BRANCH_WIDTH, BRANCH_WIDTH,
    MLA_Q_RANK, MLA_KV_RANK, MLA_ROPE, BRANCH_WIDTH,
    MEM_HEADS * MEM_HEAD_DIM, BRANCH_WIDTH,
    N_BRANCHES * D_MODEL,
)
IN_COLS = sum(IN_SPLITS)

kernel_name = "hybrid_parallel_gated_encoder"


def _rms_norm(x, g):
    xf = x.astype(jnp.float32)
    y = xf * lax.rsqrt(jnp.mean(xf * xf, axis=-1, keepdims=True) + EPS)
    return (y * g.astype(jnp.float32)).astype(x.dtype)


def _layer_norm(x, g, b):
    xf = x.astype(jnp.float32)
    mu = jnp.mean(xf, axis=-1, keepdims=True)
    var = jnp.mean(jnp.square(xf - mu), axis=-1, keepdims=True)
    y = (xf - mu) * lax.rsqrt(var + EPS)
    return (y * g.astype(jnp.float32) + b.astype(jnp.float32)).astype(x.dtype)


def _rope(x, rd, theta):
    S = x.shape[1]
    inv = jnp.float32(theta) ** (-(jnp.arange(0, rd, 2, dtype=jnp.float32) / rd))
    ang = jnp.arange(S, dtype=jnp.float32)[:, None] * inv[None, :]
    cos = jnp.cos(ang)[None, :, None, :].astype(x.dtype)
    sin = jnp.sin(ang)[None, :, None, :].astype(x.dtype)
    x1 = x[..., : rd // 2]
    x2 = x[..., rd // 2: rd]
    return jnp.concatenate([x1 * cos - x2 * sin, x1 * sin + x2 * cos, x[..., rd:]], axis=-1)


def _dwconv(x, w):
    K, C = w.shape
    return lax.conv_general_dilated(
        x, w[:, None, :].astype(x.dtype), window_strides=(1,),
        padding=[(K // 2, K // 2)], dimension_numbers=("NWC", "WIO", "NWC"),
        feature_group_count=C)


def _to_blocks(t):
    B, S = t.shape[:2]
    nb = S // Q_BLOCK
    return jnp.moveaxis(t.reshape((B, nb, Q_BLOCK) + t.shape[2:]), 1, 0)


def _from_blocks(t):
    nb, B, qb = t.shape[:3]
    return jnp.moveaxis(t, 0, 1).reshape((B, nb * qb) + t.shape[3:])


def _diff_attention(q, k, v, lam):
    scale = q.shape[-1] ** -0.5

    def one(qb):
        s = jnp.einsum("bqhcd,bkhcd->bhcqk", qb, k).astype(jnp.float32) * scale
        p = jax.nn.softmax(s, axis=-1)
        a = p[:, :, 0] - lam * p[:, :, 1]
        return jnp.einsum("bhqk,bkhd->bqhd", a.astype(v.dtype), v)

    return _from_blocks(lax.map(one, _to_blocks(q)))


def _softmax_attention(q, k, v):
    scale = q.shape[-1] ** -0.5

    def one(qb):
        s = jnp.einsum("bqhd,bkhd->bhqk", qb, k).astype(jnp.float32) * scale
        p = jax.nn.softmax(s, axis=-1)
        return jnp.einsum("bhqk,bkhd->bqhd", p.astype(v.dtype), v)

    return _from_blocks(lax.map(one, _to_blocks(q)))


def _layer(x, mem, li, norm_pre, w_in, diff_lambda, diff_subln, sconv_w,
           conf_dw_w, conf_dw_b, conf_ln_g, conf_ln_b, mla_q_norm, mla_w_uq,
           mla_kv_norm, mla_w_ukv, mem_norm, mem_w_kv, w_branch, w_out, norm_post):
    B, S, _ = x.shape
    h = _rms_norm(x, norm_pre)
    proj = h @ w_in
    split_idx = [int(c) for c in np.cumsum(IN_SPLITS)[:-1]]
    (a_q, a_k, a_v, a_z, b_b, b_c, b_x, b_z, c_glu, c_z,
     d_cq, d_ckv, d_kr, d_z, e_q, e_z, gates) = jnp.split(proj, split_idx, axis=-1)

    q = _rope(a_q.reshape(B, S, DIFF_HEADS * 2, DIFF_HEAD_DIM), PARTIAL_ROPE_DIM, ROPE_THETA)
    k = _rope(a_k.reshape(B, S, DIFF_HEADS * 2, DIFF_HEAD_DIM), PARTIAL_ROPE_DIM, ROPE_THETA)
    q = q.reshape(B, S, DIFF_HEADS, 2, DIFF_HEAD_DIM)
    k = k.reshape(B, S, DIFF_HEADS, 2, DIFF_HEAD_DIM)
    v = a_v.reshape(B, S, DIFF_HEADS, DIFF_V_DIM)
    lam_init = 0.8 - 0.6 * math.exp(-0.3 * li)
    dl = diff_lambda.astype(jnp.float32)
    lam = jnp.exp(jnp.sum(dl[0] * dl[1])) - jnp.exp(jnp.sum(dl[2] * dl[3])) + lam_init
    o_a = _diff_attention(q, k, v, lam)
    o_a = (_rms_norm(o_a, diff_subln) * (1.0 - lam_init)).reshape(B, S, BRANCH_WIDTH)

    o_b = b_b * _dwconv(b_c * b_x, sconv_w)

    glu_a, glu_b = jnp.split(c_glu, 2, axis=-1)
    u = _dwconv(glu_a * jax.nn.sigmoid(glu_b), conf_dw_w) + conf_dw_b
    o_c = jax.nn.silu(_layer_norm(u, conf_ln_g, conf_ln_b))

    qd = (_rms_norm(d_cq, mla_q_norm) @ mla_w_uq).reshape(B, S, MLA_HEADS, MLA_NOPE + MLA_ROPE)
    q_nope, q_rot = qd[..., :MLA_NOPE], _rope(qd[..., MLA_NOPE:], MLA_ROPE, MLA_ROPE_THETA)
    kvd = (_rms_norm(d_ckv, mla_kv_norm) @ mla_w_ukv).reshape(B, S, MLA_HEADS, MLA_NOPE + MLA_V)
    k_nope, v_d = kvd[..., :MLA_NOPE], kvd[..., MLA_NOPE:]
    k_rot = _rope(d_kr[:, :, None, :], MLA_ROPE, MLA_ROPE_THETA)
    q_d = jnp.concatenate([q_nope, q_rot], axis=-1)
    k_d = jnp.concatenate([k_nope, jnp.broadcast_to(k_rot, (B, S, MLA_HEADS, MLA_ROPE))], axis=-1)
    o_d = _softmax_attention(q_d, k_d, v_d).reshape(B, S, BRANCH_WIDTH)

    kv_m = (_rms_norm(mem, mem_norm) @ mem_w_kv).reshape(mem.shape[0], N_MEM, 2, MEM_HEADS, MEM_HEAD_DIM)
    q_e = e_q.reshape(B, S, MEM_HEADS, MEM_HEAD_DIM)
    s_e = jnp.einsum("bqhd,bkhd->bhqk", q_e, kv_m[:, :, 0]).astype(jnp.float32) * (MEM_HEAD_DIM ** -0.5)
    p_e = jax.nn.softmax(s_e, axis=-1)
    o_e = jnp.einsum("bhqk,bkhd->bqhd", p_e.astype(x.dtype), kv_m[:, :, 1]).reshape(B, S, BRANCH_WIDTH)

    outs = (o_a * jax.nn.silu(a_z), o_b * jax.nn.silu(b_z), o_c * jax.nn.silu(c_z),
            o_d * jax.nn.silu(d_z), o_e * jax.nn.silu(e_z))
    g = jax.nn.sigmoid(gates.reshape(B, S, N_BRANCHES, D_MODEL))
    y = g[:, :, 0] * (outs[0] @ w_branch[0])
    for i in range(1, N_BRANCHES):
        y = y + g[:, :, i] * (outs[i] @ w_branch[i])
    y = y @ w_out
    return x + _rms_norm(y, norm_post)


def setup_inputs(seed: int = 0) -> dict:
    key = jax.random.key(seed)
    ks = jax.random.split(key, 32)
    f32 = jnp.float32
    n = lambda k, shape, s=1.0: (jax.random.normal(k, shape, f32) * s)
    gain = lambda k, shape: 1.0 + 0.02 * jax.random.normal(k, shape, f32)
    return {
        "x_prompt": n(ks[0], (BATCH, SEQ, D_MODEL)),
        "x_sample": n(ks[1], (DEC_BATCH, DEC_SEQ, D_MODEL)),
        "mem_prompt": n(ks[2], (BATCH, N_MEM, D_MODEL)),
        "mem_sample": n(ks[3], (DEC_BATCH, N_MEM, D_MODEL)),
        "norm_pre": gain(ks[4], (DEPTH, D_MODEL)),
        "w_in": n(ks[5], (DEPTH, D_MODEL, IN_COLS), D_MODEL ** -0.5),
        "diff_lambda": n(ks[6], (DEPTH, 4, DIFF_HEAD_DIM), 0.1),
        "diff_subln": gain(ks[7], (DEPTH, DIFF_V_DIM)),
        "sconv_w": n(ks[8], (DEPTH, SC_KERNEL, BRANCH_WIDTH), SC_KERNEL ** -0.5),
        "conf_dw_w": n(ks[9], (DEPTH, CONF_KERNEL, BRANCH_WIDTH), CONF_KERNEL ** -0.5),
        "conf_dw_b": n(ks[10], (DEPTH, BRANCH_WIDTH), 0.02),
        "conf_ln_g": gain(ks[11], (DEPTH, BRANCH_WIDTH)),
        "conf_ln_b": n(ks[12], (DEPTH, BRANCH_WIDTH), 0.02),
        "mla_q_norm": gain(ks[13], (DEPTH, MLA_Q_RANK)),
        "mla_w_uq": n(ks[14], (DEPTH, MLA_Q_RANK, MLA_HEADS * (MLA_NOPE + MLA_ROPE)), MLA_Q_RANK ** -0.5),
        "mla_kv_norm": gain(ks[15], (DEPTH, MLA_KV_RANK)),
        "mla_w_ukv": n(ks[16], (DEPTH, MLA_KV_RANK, MLA_HEADS * (MLA_NOPE + MLA_V)), MLA_KV_RANK ** -0.5),
        "mem_norm": gain(ks[17], (DEPTH, D_MODEL)),
        "mem_w_kv": n(ks[18], (DEPTH, D_MODEL, 2 * MEM_HEADS * MEM_HEAD_DIM), D_MODEL ** -0.5),
        "w_branch": n(ks[19], (DEPTH, N_BRANCHES, BRANCH_WIDTH, D_MODEL), BRANCH_WIDTH ** -0.5),
        "w_out": n(ks[20], (DEPTH, D_MODEL, D_MODEL), D_MODEL ** -0.5),
        "norm_post": gain(ks[21], (DEPTH, D_MODEL)),
    }


def reference(x_prompt, x_sample, mem_prompt, mem_sample, norm_pre, w_in, diff_lambda,
              diff_subln, sconv_w, conf_dw_w, conf_dw_b, conf_ln_g, conf_ln_b,
              mla_q_norm, mla_w_uq, mla_kv_norm, mla_w_ukv, mem_norm, mem_w_kv,
              w_branch, w_out, norm_post):
    weights = (norm_pre, w_in, diff_lambda, diff_subln, sconv_w, conf_dw_w, conf_dw_b,
               conf_ln_g, conf_ln_b, mla_q_norm, mla_w_uq, mla_kv_norm, mla_w_ukv,
               mem_norm, mem_w_kv, w_branch, w_out, norm_post)
    y_prompt = x_prompt
    y_sample = x_sample
    for li in range(DEPTH):
        layer_w = [w[li] for w in weights]
        y_prompt = _layer(y_prompt, mem_prompt, li, *layer_w)
        y_sample = _layer(y_sample, mem_sample, li, *layer_w)
    return (y_prompt, y_sample)
```

```python
import math
from contextlib import ExitStack

import numpy as np
import concourse.bass as bass
import concourse.mybir as mybir
from concourse.bass_utils import run_bass_kernel_spmd

F32 = mybir.dt.float32
BF16 = mybir.dt.bfloat16
AF = mybir.ActivationFunctionType
ALU = mybir.AluOpType
AX = mybir.AxisListType

D = 1024
S = 2048
DEPTH = 2
NCORES = 8
NSEQ = 6
NMEM = 256
INC = 12704
EPS = 1e-6
O_AQ, O_AK, O_AV, O_AZ = 0, 512, 1024, 1536
O_BB, O_BC, O_BX, O_BZ = 2048, 2560, 3072, 3584
O_GA, O_GB, O_CZ = 4096, 4608, 5120
O_CQ, O_CKV, O_KR, O_DZ = 5632, 5888, 6016, 6048
O_EQ, O_EZ, O_G = 6560, 7072, 7584
VW = 130


class Buf:
    __slots__ = ("name", "w", "r", "sem_ld", "cnt_ld")

    def __init__(self, name):
        self.name = name
        self.w = {}
        self.r = {}
        self.sem_ld = None
        self.cnt_ld = 0


class Rec:
    def __init__(self):
        self.calls = []

    def __getattr__(self, name):
        def f(*a, **k):
            self.calls.append((name, a, k))
            return self
        return f


def _replay(e, calls):
    ins = None
    for name, a, k in calls:
        ins = getattr(e, name)(*a, **k)
    return ins


class Prog:
    ENG = ("pe", "act", "dve", "pool", "sp")

    def __init__(self, nc, es):
        self.nc, self.es = nc, es
        self.q = {k: [] for k in self.ENG}
        self.sems = []
        self.esem = {}
        self.cnt = {}
        for k in ("pe", "act", "dve", "pool"):
            self.esem[k] = self._newsem("e_" + k)
            self.cnt[k] = 0
        self.known = {k: {} for k in self.ENG}
        self.nbuf = 0
        self.nwait = 0

    def _newsem(self, name):
        h = self.es.enter_context(self.nc.semaphore(name))
        self.sems.append(h)
        return len(self.sems) - 1

    def buf(self, name=None):
        self.nbuf += 1
        return Buf(name or f"b{self.nbuf}")

    def _waits(self, eng, reads, writes):
        need = {}
        for b in reads:
            for s, v in b.w.items():
                if need.get(s, 0) < v:
                    need[s] = v
        for b in writes:
            for d in (b.w, b.r):
                for s, v in d.items():
                    if need.get(s, 0) < v:
                        need[s] = v
        out = []
        kn = self.known[eng]
        for s, v in need.items():
            if kn.get(s, 0) < v:
                kn[s] = v
                out.append((s, v))
        self.nwait += len(out)
        return out

    def _commit(self, ev, reads, writes):
        s, v = ev
        for b in writes:
            b.w = {s: v}
            b.r = {}
        for b in reads:
            if b.r.get(s, 0) < v:
                b.r[s] = v

    def op(self, eng, reads, writes, emit):
        waits = self._waits(eng, reads, writes)
        self.cnt[eng] += 1
        si = self.esem[eng]
        sems = self.sems

        rec = Rec()
        emit(rec)
        calls = rec.calls
        assert calls

        def run(e, waits=waits, calls=calls, si=si):
            for s, v in waits:
                e.wait_ge(sems[s], v)
            _replay(e, calls).then_inc(sems[si], 1)

        self.q[eng].append(run)
        self._commit((si, self.cnt[eng]), reads, writes)

    def dma(self, q, reads, writes, sembuf, out_ap, in_ap, slow=False):
        waits = self._waits(q, reads, writes)
        if sembuf.sem_ld is None:
            sembuf.sem_ld = self._newsem("d_" + sembuf.name)
        sembuf.cnt_ld += 16
        si = sembuf.sem_ld
        sems = self.sems

        def run(e, waits=waits, si=si):
            for s, v in waits:
                e.wait_ge(sems[s], v)
            if slow:
                e.dma_start(out=out_ap, in_=in_ap, allow_slow_non_contiguous=True).then_inc(sems[si], 16)
            else:
                e.dma_start(out=out_ap, in_=in_ap).then_inc(sems[si], 16)

        self.q[q].append(run)
        ev = (si, sembuf.cnt_ld)
        self._commit(ev, reads, writes)
        return ev

    def final_wait(self, q, evs):
        sems = self.sems

        def run(e):
            for s, v in evs:
                e.wait_ge(sems[s], v)

        self.q[q].append(run)

    def emit_all(self):
        block = self.es.enter_context(self.nc.Block())
        qs = self.q

        @block.sync
        def _(e):
            for f in qs["sp"]:
                f(e)

        @block.scalar
        def _(e):
            for f in qs["act"]:
                f(e)

        @block.vector
        def _(e):
            for f in qs["dve"]:
                f(e)

        @block.gpsimd
        def _(e):
            for f in qs["pool"]:
                f(e)

        @block.tensor
        def _(e):
            for f in qs["pe"]:
                f(e)


def build_program(nseq=NSEQ, depth=DEPTH, debug=False, skip=()):
    nc = bass.Bass("TRN2", target_bir_lowering=False)
    es = ExitStack()
    P = Prog(nc, es)

    def din(name, shape, dt=F32):
        return nc.dram_tensor(name, list(shape), dt, kind="ExternalInput").ap()

    x_in = din("x_all", [nseq, S, D])
    mem_in = din("mem_all", [nseq, NMEM, D])
    norm_pre = din("norm_pre", [DEPTH, D])
    w_in = din("w_in", [DEPTH, D, INC])
    diff_lambda = din("diff_lambda", [DEPTH, 4, 64])
    diff_subln = din("diff_subln", [DEPTH, 128])
    sconv_w = din("sconv_w", [DEPTH, 3, 512])
    conf_dw_w = din("conf_dw_w", [DEPTH, 31, 512])
    conf_dw_b = din("conf_dw_b", [DEPTH, 512])
    conf_ln_g = din("conf_ln_g", [DEPTH, 512])
    conf_ln_b = din("conf_ln_b", [DEPTH, 512])
    mla_q_norm = din("mla_q_norm", [DEPTH, 256])
    mla_w_uq = din("mla_w_uq", [DEPTH, 256, 384])
    mla_kv_norm = din("mla_kv_norm", [DEPTH, 128])
    mla_w_ukv = din("mla_w_ukv", [DEPTH, 128, 768])
    mem_norm = din("mem_norm", [DEPTH, D])
    mem_w_kv = din("mem_w_kv", [DEPTH, D, D])
    w_branch = din("w_branch", [DEPTH, 5, 512, D])
    w_out = din("w_out", [DEPTH, D, D])
    norm_post = din("norm_post", [DEPTH, D])
    c_ident = din("c_ident", [128, 128])
    c_rope = din("c_rope", [4, 128, S])

    y_out = nc.dram_tensor("y_all", [nseq, S, D], F32, kind="ExternalOutput").ap()
    dbg = nc.dram_tensor("dbg", [5, 128, 4, 512], BF16, kind="ExternalOutput").ap() if debug else None

    def dscr(name, shape, dt):
        return nc.dram_tensor(name, list(shape), dt, kind="Internal").ap()

    xs = dscr("xs", [nseq, S, D], F32)
    wi_b = dscr("wi_b", [DEPTH, D, INC], BF16)
    wsw_b = dscr("wsw_b", [DEPTH, D, 1056], BF16)
    wuq_b = dscr("wuq_b", [DEPTH, 256, 384], BF16)
    wuqs_b = dscr("wuqs_b", [DEPTH, 256, 384], BF16)
    wukv_b = dscr("wukv_b", [DEPTH, 128, 768], BF16)
    wkv_b = dscr("wkv_b", [DEPTH, D, D], BF16)
    wb_b = dscr("wb_b", [DEPTH, 5, 512, D], BF16)
    wo_b = dscr("wo_b", [DEPTH, D, D], BF16)

    def sb(name, shape, dt):
        return es.enter_context(nc.sbuf_tensor(name, list(shape), dt))

    def ps(name, shape, dt):
        return es.enter_context(nc.psum_tensor(name, list(shape), dt))

    hT = sb("hT", [128, 8, S], BF16)
    kTa = sb("kTa", [128, 4, S], BF16)
    v1a = sb("v1a", [128, 16, 4, VW], BF16)
    kdT = sb("kdT", [128, 4, S], BF16)
    vd1 = sb("vd1", [128, 16, 4, VW], BF16)
    kmT = sb("kmT", [128, 4, NMEM], BF16)
    vm1 = sb("vm1", [128, 2, 4, VW], BF16)
    ident = sb("ident", [128, 128], BF16)
    onesf = sb("onesf", [128, 128], F32)
    subln = sb("subln", [128, 128], F32)
    prm_all = sb("prm", [128, DEPTH, 32], F32)
    cw31_all = sb("cw31", [128, DEPTH, 4, 31], F32)
    lam_t = sb("lam_t", [128, 8], F32)
    wring = [sb(f"wr{i}", [128, 8, 512], BF16) for i in range(3)]
    wsm = sb("wsm", [128, 2304], BF16)
    wkr = sb("wkr", [128, 2, 8, 96], BF16)
    rope = sb("rope", [128, 4, 512], F32)
    gvec = rope[:, 2:4, :].rearrange("p a t -> p (a t)")
    dlam = rope[:, 3, 0:256].rearrange("p (a d) -> p a d", a=4)
    big = sb("big", [128, 4096], F32)
    outsT = [sb(f"outsT{i}", [128, 4, 512], BF16) for i in range(2)]
    yacc = sb("yacc", [128, 8, 512], F32)
    identf = yacc[:, 0, 0:128]
    ymT = sb("ymT", [128, 8, 512], BF16)
    tmpA = [sb(f"tmpA{i}", [128, 544], F32) for i in range(6)]
    tmpB = [sb(f"tmpB{i}", [128, 1024], BF16) for i in range(2)]
    st = sb("st", [128, 64], F32)
    zer = sb("zer", [128, 128], BF16)
    xq = big[:, :].rearrange("p (j d) -> p j d", j=4)
    big_bf = big[:, :].bitcast(BF16)
    qT = big_bf[:, 0:2048].rearrange("p (h t) -> p h t", h=4)
    pTs = [big_bf[:, 2048 + i * 512: 2048 + (i + 1) * 512] for i in range(3)]
    uconv = ymT[:, :, :].rearrange("p a b -> p (a b)").bitcast(F32).rearrange("p (c t) -> p c t", c=4)
    otok = big[:, 2048:4096].rearrange("p (j d) -> p j d", j=4)

    pbank = [ps(f"pb{i}", [128, 512], F32) for i in range(3)]
    pacc = [ps(f"pa{i}", [128, 512], F32) for i in range(4)]
    ptr = ps("ptr", [128, 1024], BF16)

    B = {}

    def bf(name):
        if name not in B:
            B[name] = P.buf(name)
        return B[name]

    b_hT = [bf(f"hT{i}") for i in range(4)]
    b_kTa = [bf(f"kTa{i}") for i in range(4)]
    b_v1a = [bf(f"v1a{i}") for i in range(4)]
    b_kdT = [bf(f"kdT{i}") for i in range(4)]
    b_vd1 = [bf(f"vd1{i}") for i in range(4)]
    b_pbank = [bf(f"pb{i}") for i in range(3)]
    b_pacc = [bf(f"pa{i}") for i in range(4)]
    b_wring = [bf(f"wr{i}") for i in range(3)]
    b_tmpA = [bf(f"tmpA{i}") for i in range(6)]
    b_tmpB = [bf(f"tmpB{i}") for i in range(2)]
    b_outsT = [bf(f"outsT{i}") for i in range(2)]
    b_pT = [bf(f"pT{i}") for i in range(3)]
    rr = {"bank": 0, "ring": 0, "pT": 0, "tA": 0, "tB": 0, "nb": 7}
    allbanks = pbank + pacc
    b_allbanks = b_pbank + b_pacc

    def nbank():
        i = rr["bank"] % rr["nb"]
        rr["bank"] = (i + 1) % rr["nb"]
        return allbanks[i], b_allbanks[i]

    def nring():
        i = rr["ring"]
        rr["ring"] = (i + 1) % 3
        return wring[i], b_wring[i]

    def ntA():
        i = rr["tA"]
        rr["tA"] = (i + 1) % 6
        return tmpA[i], b_tmpA[i]

    def ntB():
        i = rr["tB"]
        rr["tB"] = (i + 1) % 2
        return tmpB[i], b_tmpB[i]

    PRM_SC = 0
    PRM_CB = 12
    PRM_LG = 16
    PRM_LB = 20
    PRM_QN = 24
    PRM_KN = 26

    b_const = bf("const")
    P.dma("sp", [], [b_const], b_const, identf[:], c_ident[:, :])
    P.op("dve", [b_const], [bf("ident")], lambda e: e.tensor_copy(out=ident[:], in_=identf[:]))
    P.op("pool", [], [bf("onesf")], lambda e: e.memset(onesf[:], 1.0))
    P.op("pool", [], [bf("zer")], lambda e: e.memset(zer[:], 0.0))
    P.op("pool", [], [b_v1a[0], b_v1a[1], b_v1a[2], b_v1a[3]], lambda e: e.memset(v1a[:], 1.0))
    P.op("pool", [], [b_vd1[0], b_vd1[1], b_vd1[2], b_vd1[3]], lambda e: e.memset(vd1[:], 1.0))
    P.op("pool", [], [bf("vm1")], lambda e: e.memset(vm1[:], 1.0))
    P.op("pool", [], [bf("wkr")], lambda e: e.memset(wkr[:], 0.0))

    b_prm = bf("prm")
    for l in range(depth):
        for c in range(4):
            P.dma("sp", [], [b_prm], b_prm, prm_all[:, l, PRM_SC + c * 3:PRM_SC + c * 3 + 3],
                  sconv_w[l, :, c * 128:(c + 1) * 128].rearrange("k p -> p k"), slow=True)
            P.dma("sp", [], [b_prm], b_prm, cw31_all[:, l, c, :],
                  conf_dw_w[l, :, c * 128:(c + 1) * 128].rearrange("k p -> p k"), slow=True)
        for (off, src, n) in ((PRM_CB, conf_dw_b, 4), (PRM_LG, conf_ln_g, 4), (PRM_LB, conf_ln_b, 4),
                              (PRM_QN, mla_q_norm, 2), (PRM_KN, mla_kv_norm, 1)):
            P.dma("sp", [], [b_prm], b_prm, prm_all[:, l, off:off + n],
                  src[l, :].rearrange("(c p) -> p c", p=128), slow=True)
    P.op("dve", [b_prm], [b_prm], lambda e: e.tensor_scalar(
        out=cw31_all[:, :, :, :].rearrange("p l c k -> p (l c k)"), in0=cw31_all[:, :, :, :].rearrange("p l c k -> p (l c k)"),
        scalar1=0.5, scalar2=None, op0=ALU.mult))
    b_wprep = bf("wprep")
    b_wprep2 = bf("wprep2")

    def castdma(out_ap, in_ap, second=False):
        if second:
            P.dma("pool", [b_wprep], [], b_wprep2, out_ap, in_ap)
        else:
            P.dma("pool", [], [], b_wprep, out_ap, in_ap)

    for l in range(depth):
        for r in range(8):
            castdma(wi_b[l, r * 128:(r + 1) * 128, :], w_in[l, r * 128:(r + 1) * 128, :])
            castdma(wkv_b[l, r * 128:(r + 1) * 128, :], mem_w_kv[l, r * 128:(r + 1) * 128, :])
            castdma(wo_b[l, r * 128:(r + 1) * 128, :], w_out[l, r * 128:(r + 1) * 128, :])
        for i in range(5):
            for r in range(4):
                castdma(wb_b[l, i, r * 128:(r + 1) * 128, :], w_branch[l, i, r * 128:(r + 1) * 128, :])
        castdma(wuq_b[l, :, :], mla_w_uq[l, :, :])
        castdma(wuqs_b[l, :, :], mla_w_uq[l, :, :])
        castdma(wukv_b[l, :, :], mla_w_ukv[l, :, :])
        castdma(wsw_b[l, :, 0:1024], w_in[l, :, 0:1024])
        castdma(wsw_b[l, :, 1024:1056], w_in[l, :, O_KR:O_KR + 32])
    b_wprep.w = {b_wprep.sem_ld: b_wprep.cnt_ld}
    for l in range(depth):
        for rh in range(2):
            rs = slice(rh * 512, (rh + 1) * 512)
            dst = wsw_b[l, rs, 0:1024].rearrange("r (s d) -> r s d", d=64)
            src = w_in[l, rs, 0:1024].rearrange("r (s d) -> r s d", d=64)
            castdma(dst[:, :, 0:8], src[:, :, 8:16], True)
            castdma(dst[:, :, 8:16], src[:, :, 0:8], True)
        castdma(wsw_b[l, :, 1024:1040], w_in[l, :, O_KR + 16:O_KR + 32], True)
        castdma(wsw_b[l, :, 1040:1056], w_in[l, :, O_KR:O_KR + 16], True)
        dq = wuqs_b[l, :, :].rearrange("r (h d) -> r h d", d=96)
        sq = mla_w_uq[l, :, :].rearrange("r (h d) -> r h d", d=96)
        castdma(dq[:, :, 64:80], sq[:, :, 80:96], True)
        castdma(dq[:, :, 80:96], sq[:, :, 64:80], True)
    b_wprep2.w = {b_wprep2.sem_ld: b_wprep2.cnt_ld}
    b_wts = bf("wts")
    P.op("pool", [b_wprep, b_wprep2], [b_wts], lambda e: e.memset(st[:, 61:62], 0.0))

    def load_w(dst_ap, src_ap, b_dst, extra_reads=()):
        P.dma("sp", [b_wts] + list(extra_reads), [b_dst], b_dst, dst_ap, src_ap)

    def wcols(l, c0, n):
        return wi_b[l, :, c0:c0 + n].rearrange("(kc p) n -> p kc n", p=128)

    def ring_load(src_ap, n, kcs=8, slot=None):
        w, bw = (wring[slot], b_wring[slot]) if slot is not None else nring()
        load_w(w[:, 0:kcs, 0:n], src_ap, bw)
        return w, bw

    def run(*gens):
        gens = list(gens)
        while gens:
            for g in list(gens):
                try:
                    next(g)
                except StopIteration:
                    gens.remove(g)

    def mm_fm(out_ap, b_out, w, bw, c0, ncols, tok_lo, ntok, hbufs, kcs=8, rhs_fn=None):
        def emit(e):
            ins = None
            for kc in range(kcs):
                rhs = rhs_fn(kc) if rhs_fn else hT[:, kc, tok_lo:tok_lo + ntok]
                ins = e.matmul(out_ap, lhsT=w[:, kc, c0:c0 + ncols], rhs=rhs,
                               start=(kc == 0), stop=(kc == kcs - 1))
            return ins
        P.op("pe", [bw] + list(hbufs), [b_out], emit)

    def mm_tm(out_ap, b_out, w, bw, c0, ncols, tok_lo, hbufs):
        def emit(e):
            ins = None
            for kc in range(8):
                ins = e.matmul(out_ap, lhsT=hT[:, kc, tok_lo:tok_lo + 128], rhs=w[:, kc, c0:c0 + ncols],
                               start=(kc == 0), stop=(kc == 7))
            return ins
        P.op("pe", [bw] + list(hbufs), [b_out], emit)

    def rsqrt_inplace(ap, b, scale, n_eps=EPS):
        P.op("act", [b], [b], lambda e: e.activation(out=ap, in_=ap, func=AF.Sqrt, bias=float(n_eps), scale=float(scale)))
        P.op("dve", [b], [b], lambda e: e.reciprocal(out=ap, in_=ap))

    def silu2(out_ap, b_out, z_ap, b_z):
        t, bt = ntA()
        n = z_ap.shape[-1] if len(z_ap.shape) == 2 else None
        tv = t[0:z_ap.shape[0], 0:z_ap.shape[1]]
        P.op("act", [b_z], [bt], lambda e: e.activation(out=tv, in_=z_ap, func=AF.Tanh, scale=0.5))
        P.op("dve", [b_z, bt], [b_out], lambda e: e.scalar_tensor_tensor(
            out=out_ap, in0=tv, scalar=1.0, in1=z_ap, op0=ALU.add, op1=ALU.mult))

    def transpose_to(dst_fn, b_dst, src, b_src, nchunks):
        b_ptr = bf("ptr")

        def emit(e):
            ins = None
            for kc in range(nchunks):
                ins = e.transpose(out=ptr[:, kc * 128:(kc + 1) * 128], in_=src[:, kc * 128:(kc + 1) * 128],
                                  identity=ident[:])
            return ins
        P.op("pe", [b_src, bf("ident")], [b_ptr], emit)
        P.op("act", [b_ptr], [b_dst], lambda e: e.copy(
            out=dst_fn(), in_=ptr[:, 0:nchunks * 128].rearrange("p (k t) -> p k t", t=128)))

    def attention(heads, nkc, scale, b_q, b_k, b_v, evac, pre_head=None):
        rr["nb"] = 3
        rr["bank"] = 0
        for hidx, (q_ap, k_fn, v_fn, aset, tag) in enumerate(heads):
            if pre_head is not None:
                pre_head(hidx)
            accs = (pacc[2 * aset], pacc[2 * aset + 1])
            baccs = [b_pacc[2 * aset], b_pacc[2 * aset + 1]]
            steps = []

            def qk(kc):
                bank, bbank = nbank()
                P.op("pe", [b_q] + b_k, [bbank], lambda e: e.matmul(bank[:, :], lhsT=k_fn(kc), rhs=q_ap,
                                                                       start=True, stop=True))
                i = rr["pT"]
                rr["pT"] = (i + 1) % 3
                P.op("act", [bbank], [b_pT[i]], lambda e: e.activation(out=pTs[i], in_=bank[:, :], func=AF.Exp,
                                                                         scale=float(scale)))
                return i

            def zero_acc():
                def emit(e):
                    ins = None
                    for a in accs:
                        ins = e.matmul(a[:, 0:258], lhsT=zer[:, 0:128], rhs=hT[:, 0, 0:258], start=True, stop=False)
                    return ins
                P.op("pe", [bf("zer"), b_hT[0]], baccs, emit)

            def pv(kc, i):
                def emit(e):
                    ins = None
                    for j in range(4):
                        ins = e.matmul(accs[j // 2][:, (j % 2) * 129:(j % 2) * 129 + 129],
                                       lhsT=pTs[i][:, j * 128:(j + 1) * 128], rhs=v_fn(kc),
                                       start=False, stop=(kc == nkc - 1))
                    return ins
                P.op("pe", [b_pT[i]] + b_v, baccs, emit)

            zero_acc()
            pend = [qk(0)]
            if nkc > 1:
                pend.append(qk(1))
            for kc in range(nkc):
                if kc + 2 < nkc:
                    pend.append(qk(kc + 2))
                pv(kc, pend.pop(0))
            evac(tag, accs, baccs)
        rr["nb"] = 7

    out_evs = []
    b_big = bf("big")
    b_rope = bf("ropeD")
    b_ropeA = bf("ropeA")
    b_gvec = b_rope
    b_st = bf("st")
    b_otok = bf("otok")
    b_qT = bf("qT")
    b_yacc = bf("yacc")
    b_ymT = bf("ymT")
    b_uconv = b_ymT
    b_wsm = bf("wsm")
    b_wkr = bf("wkr")
    b_lam = bf("lam")
    b_subln = bf("subln")
    b_kmT = bf("kmT")
    b_vm1 = bf("vm1")
    b_xs = [bf(f"xs{s}") for s in range(nseq)]

    WUQ, WUQS, WUKV = 0, 768, 1536

    def load_gvec(src_row):
        P.dma("sp", [], [b_gvec], b_gvec, gvec[:], src_row.partition_broadcast(128))

    def load_rope(t0, which="AD"):
        if "A" in which:
            P.dma("sp", [], [b_ropeA], b_ropeA, rope[:, 0:2, :], c_rope[0:2, :, t0:t0 + 512].rearrange("i p t -> p i t"))
        if "D" in which:
            P.dma("sp", [], [b_rope], b_rope, rope[:, 2:4, :], c_rope[2:4, :, t0:t0 + 512].rearrange("i p t -> p i t"))

    def rms_rows(x3, bx, ntile, gsrc_loaded, dst_fn, b_dst):
        P.op("dve", [], [b_st], lambda e: e.memset(st[:, 0:ntile], 0.0))
        for j in range(ntile):
            tb_, btb = ntB()
            P.op("act", [bx], [btb, b_st], lambda e: e.activation(out=tb_[:, :], in_=x3[:, j, :], func=AF.Square,
                                                                   accum_out=st[:, j:j + 1]))
        rsqrt_inplace(st[:, 0:ntile], b_st, 1.0 / D)
        for j in range(ntile):
            tb_, btb = ntB()
            P.op("dve", [bx, b_st, b_gvec], [btb], lambda e: e.scalar_tensor_tensor(
                out=tb_[:, :], in0=x3[:, j, :], scalar=st[:, j:j + 1], in1=gvec[:], op0=ALU.mult, op1=ALU.mult))
            transpose_to(lambda: dst_fn(j), b_dst, tb_, btb, 8)

    for s in range(nseq):
        for l in range(depth):
            x_src = x_in if l == 0 else xs
            x_dst = xs if (l == 0 and depth > 1) else y_out
            lam_init = 0.8 - 0.6 * math.exp(-0.3 * l)
            prm = prm_all[:, l, :]
            cw31 = cw31_all[:, l, :, :]
            P.dma("sp", [], [b_subln], b_subln, subln[:, :], diff_subln[l:l + 1, :].partition_broadcast(128))
            P.op("dve", [b_subln], [b_subln], lambda e: e.tensor_scalar(
                out=subln[:], in0=subln[:], scalar1=float((1.0 - lam_init) * 0.5), scalar2=None, op0=ALU.mult))
            P.dma("sp", [], [b_lam, b_rope], b_lam, rope[:, 3, 0:256],
                  diff_lambda[l:l + 1, :, :].rearrange("o a d -> o (a d)").partition_broadcast(128))
            t0_, bt0 = ntA()
            P.op("dve", [b_lam, b_rope], [bt0], lambda e: e.tensor_tensor(out=t0_[:, 0:64], in0=dlam[:, 0, :], in1=dlam[:, 1, :], op=ALU.mult))
            P.op("dve", [b_lam, b_rope], [bt0], lambda e: e.tensor_tensor(out=t0_[:, 64:128], in0=dlam[:, 2, :], in1=dlam[:, 3, :], op=ALU.mult))
            P.op("dve", [bt0], [b_lam], lambda e: e.tensor_reduce(out=lam_t[:, 0:2], in_=t0_[:, 0:128].rearrange("p (a d) -> p a d", a=2), axis=AX.X, op=ALU.add))
            P.op("act", [b_lam], [b_lam], lambda e: e.activation(out=lam_t[:, 2:4], in_=lam_t[:, 0:2], func=AF.Exp))
            P.op("dve", [b_lam], [b_lam], lambda e: e.scalar_tensor_tensor(
                out=lam_t[:, 4:5], in0=lam_t[:, 3:4], scalar=float(-lam_init), in1=lam_t[:, 2:3], op0=ALU.add, op1=ALU.subtract))
            load_gvec(norm_pre[l:l + 1, :])
            for tb in range(4):
                t0 = tb * 512
                rd = [b_xs[s]] if l > 0 else []
                P.dma("sp", rd, [b_big], b_big, xq, x_src[s, t0:t0 + 512, :].rearrange("(j p) d -> p j d", p=128))
                rms_rows(xq, b_big, 4, None, lambda j, t0=t0: hT[:, :, t0 + j * 128:t0 + (j + 1) * 128], b_hT[tb])
            load_w(wsm[:, WUQ:WUQ + 768].rearrange("p (k n) -> p k n", k=2),
                   wuq_b[l, :, :].rearrange("(k p) n -> p k n", p=128), b_wsm)
            load_w(wsm[:, WUQS:WUQS + 768].rearrange("p (k n) -> p k n", k=2),
                   wuqs_b[l, :, :].rearrange("(k p) n -> p k n", p=128), b_wsm)
            load_w(wsm[:, WUKV:WUKV + 768], wukv_b[l, :, :], b_wsm)
            load_w(wkr[:, 0, :, 64:96], wi_b[l, :, O_KR:O_KR + 32].rearrange("(kc p) n -> p kc n", p=128), b_wkr)
            load_w(wkr[:, 1, :, 64:96], wsw_b[l, :, 1024:1056].rearrange("(kc p) n -> p kc n", p=128), b_wkr)

            for tb in range(4):
                t0 = tb * 512
                hb = [b_hT[tb]]
                load_rope(t0)
                wk, bwk = ring_load(wcols(l, O_AK, 512), 512)
                wks, bwks = ring_load(wsw_b[l, :, 512:1024].rearrange("(kc p) n -> p kc n", p=128), 512)
                for c in range(4):
                    bk1, bbk1 = nbank()
                    mm_fm(bk1[:, :], bbk1, wk, bwk, c * 128, 128, t0, 512, hb)
                    bk2, bbk2 = nbank()
                    mm_fm(bk2[:, :], bbk2, wks, bwks, c * 128, 128, t0, 512, hb)
                    ta, bta = ntA()
                    tb2, btb2 = ntA()
                    P.op("dve", [bbk1, b_ropeA], [bta], lambda e: e.tensor_tensor(out=ta[:, 0:512], in0=bk1[:, :], in1=rope[:, 0, :], op=ALU.mult))
                    P.op("dve", [bbk2, b_ropeA], [btb2], lambda e: e.tensor_tensor(out=tb2[:, 0:512], in0=bk2[:, :], in1=rope[:, 1, :], op=ALU.mult))
                    P.op("dve", [bta, btb2], [b_kTa[tb]], lambda e: e.tensor_tensor(out=kTa[:, c, t0:t0 + 512], in0=ta[:, 0:512], in1=tb2[:, 0:512], op=ALU.add))
                wv, bwv = ring_load(wcols(l, O_AV, 512), 512)
                for j in range(4):
                    bk, bbk = nbank()
                    mm_tm(bk[:, :], bbk, wv, bwv, 0, 512, t0 + j * 128, hb)
                    P.op("act", [bbk], [b_v1a[tb]], lambda e: e.copy(out=v1a[:, tb * 4 + j, :, 0:128],
                                                                      in_=bk[:, :].rearrange("p (h d) -> p h d", h=4)))
                wd, bwd = ring_load(wcols(l, O_CKV, 128), 128)
                bk, bbk = nbank()
                mm_fm(bk[:, :], bbk, wd, bwd, 0, 128, t0, 512, hb)
                ckf, bckf = ntA()
                P.op("act", [bbk], [bckf], lambda e: e.copy(out=ckf[:, 0:512], in_=bk[:, :]))
                sqf, bsqf = ntA()
                P.op("act", [bckf], [bsqf], lambda e: e.activation(out=sqf[:, 0:512], in_=ckf[:, 0:512], func=AF.Square))
                bk2, bbk2 = nbank()
                P.op("pe", [bsqf, bf("onesf")], [bbk2], lambda e: e.matmul(bk2[:, :], lhsT=onesf[:], rhs=sqf[:, 0:512], start=True, stop=True))
                rs_, brs = ntA()
                P.op("act", [bbk2], [brs], lambda e: e.activation(out=rs_[:, 0:512], in_=bk2[:, :], func=AF.Sqrt, bias=float(EPS), scale=1.0 / 128))
                P.op("dve", [brs], [brs], lambda e: e.reciprocal(out=rs_[:, 0:512], in_=rs_[:, 0:512]))
                ckn, bckn = ntB()
                P.op("dve", [bckf, brs, b_prm], [bckn], lambda e: e.scalar_tensor_tensor(
                    out=ckn[:, 0:512], in0=ckf[:, 0:512], scalar=prm[:, PRM_KN:PRM_KN + 1], in1=rs_[:, 0:512], op0=ALU.mult, op1=ALU.mult))
                for h in range(4):
                    bk, bbk = nbank()
                    P.op("pe", [bckn, b_wsm], [bbk], lambda e: e.matmul(bk[0:64, :], lhsT=wsm[:, WUKV + h * 192:WUKV + h * 192 + 64],
                                                                         rhs=ckn[:, 0:512], start=True, stop=True))
                    P.op("act", [bbk], [b_kdT[tb]], lambda e: e.copy(out=kdT[0:64, h, t0:t0 + 512], in_=bk[0:64, :]))
                wukv_v = wsm[:, WUKV:WUKV + 768].rearrange("p (h d) -> p h d", h=4)[:, :, 64:192]
                for j in range(4):
                    bk, bbk = nbank()
                    P.op("pe", [bckn, b_wsm], [bbk], lambda e: e.matmul(bk[:, :].rearrange("p (h d) -> p h d", h=4),
                                                                         lhsT=ckn[:, j * 128:(j + 1) * 128], rhs=wukv_v, start=True, stop=True))
                    P.op("act", [bbk], [b_vd1[tb]], lambda e: e.copy(out=vd1[:, tb * 4 + j, :, 0:128],
                                                                      in_=bk[:, :].rearrange("p (h d) -> p h d", h=4)))
                bk1, bbk1 = nbank()
                mm_fm(bk1[0:96, :], bbk1, wkr[:, 0], b_wkr, 0, 96, t0, 512, hb)
                bk2, bbk2 = nbank()
                mm_fm(bk2[0:96, :], bbk2, wkr[:, 1], b_wkr, 0, 96, t0, 512, hb)
                ta, bta = ntA()
                tb2, btb2 = ntA()
                P.op("dve", [bbk1, b_rope], [bta], lambda e: e.tensor_tensor(out=ta[64:96, 0:512], in0=bk1[64:96, :], in1=rope[64:96, 2, :], op=ALU.mult))
                P.op("dve", [bbk2, b_rope], [btb2], lambda e: e.tensor_tensor(out=tb2[64:96, 0:512], in0=bk2[64:96, :], in1=rope[64:96, 3, :], op=ALU.mult))
                for h in range(4):
                    P.op("dve", [bta, btb2], [b_kdT[tb]], lambda e: e.tensor_tensor(out=kdT[64:96, h, t0:t0 + 512], in0=ta[64:96, 0:512], in1=tb2[64:96, 0:512], op=ALU.add))
            load_gvec(mem_norm[l:l + 1, :])
            P.dma("sp", [], [b_big], b_big, xq[:, 0:2, :], mem_in[s, :, :].rearrange("(j p) d -> p j d", p=128))
            memT = outsT[0][:, :, :].rearrange("p a b -> p (a b)").rearrange("p (k t) -> p k t", k=8)
            b_memT = b_outsT[0]
            rms_rows(xq, b_big, 2, None, lambda j: memT[:, :, j * 128:(j + 1) * 128], b_memT)
            wm0, bwm0 = ring_load(wkv_b[l, :, 0:512].rearrange("(kc p) n -> p kc n", p=128), 512)
            for h in range(4):
                bk, bbk = nbank()
                mm_fm(bk[:, 0:NMEM], bbk, wm0, bwm0, h * 128, 128, 0, NMEM, [b_memT], rhs_fn=lambda kc: memT[:, kc, :])
                P.op("act", [bbk], [b_kmT], lambda e: e.copy(out=kmT[:, h, :], in_=bk[:, 0:NMEM]))
            wm1, bwm1 = ring_load(wkv_b[l, :, 512:1024].rearrange("(kc p) n -> p kc n", p=128), 512)
            for j in range(2):
                bk, bbk = nbank()

                def emit(e, bk=bk, j=j):
                    ins = None
                    for kc in range(8):
                        ins = e.matmul(bk[:, :], lhsT=memT[:, kc, j * 128:(j + 1) * 128], rhs=wm1[:, kc, 0:512],
                                       start=(kc == 0), stop=(kc == 7))
                    return ins
                P.op("pe", [b_memT, bwm1], [bbk], emit)
                P.op("act", [bbk], [b_vm1], lambda e: e.copy(out=vm1[:, j, :, 0:128], in_=bk[:, :].rearrange("p (h d) -> p h d", h=4)))

            all_h = list(b_hT)
            for qb in range(4):
                t0 = qb * 512
                hb = [b_hT[qb]]
                load_rope(t0, "A")

                acc_first = [True]

                def branch_accumulate(i, oT, boT, slots=(None, None)):
                    init = acc_first[0]
                    acc_first[0] = False
                    if debug and s == 0 and l == 0 and qb == 0:
                        ev = P.dma("sp", [boT], [], bf(f"dbg{i}"), dbg[i], oT[:, :, :])
                        boT.r[ev[0]] = ev[1]
                        out_evs.append(ev)
                    wbi, bwbi = ring_load(wb_b[l, i, :, 0:512].rearrange("(kc p) n -> p kc n", p=128), 512, kcs=4, slot=slots[0])
                    for half in range(2):
                        if half == 1:
                            wbi, bwbi = ring_load(wb_b[l, i, :, 512:1024].rearrange("(kc p) n -> p kc n", p=128), 512, kcs=4, slot=slots[0])
                        wg, bwg = ring_load(wcols(l, O_G + i * 1024 + half * 512, 512), 512, slot=slots[1])
                        for c4 in range(4):
                            cc = half * 4 + c4
                            bkb, bbkb = nbank()
                            mm_fm(bkb[:, :], bbkb, wbi, bwbi, c4 * 128, 128, 0, 512, [boT], kcs=4,
                                  rhs_fn=lambda kc: oT[:, kc, :])
                            bkg, bbkg = nbank()
                            mm_fm(bkg[:, :], bbkg, wg, bwg, c4 * 128, 128, t0, 512, hb)
                            th, bth = ntA()
                            P.op("act", [bbkg], [bth], lambda e: e.activation(out=th[:, 0:512], in_=bkg[:, :], func=AF.Tanh, scale=0.5))
                            if init:
                                P.op("dve", [bth, bbkb], [b_yacc], lambda e: e.scalar_tensor_tensor(
                                    out=yacc[:, cc, :], in0=th[:, 0:512], scalar=1.0, in1=bkb[:, :], op0=ALU.add, op1=ALU.mult))
                            else:
                                pr, bpr = ntA()
                                P.op("dve", [bth, bbkb], [bpr], lambda e: e.scalar_tensor_tensor(
                                    out=pr[:, 0:512], in0=th[:, 0:512], scalar=1.0, in1=bkb[:, :], op0=ALU.add, op1=ALU.mult))
                                P.op("dve", [bpr, b_yacc], [b_yacc], lambda e: e.tensor_tensor(
                                    out=yacc[:, cc, :], in0=yacc[:, cc, :], in1=pr[:, 0:512], op=ALU.add))
                            yield

                def gate_and_transpose(zoff, oi, post_fn=None, slot=None):
                    wz, bwz = ring_load(wcols(l, zoff, 512), 512, slot=slot)
                    for j in range(4):
                        bk, bbk = nbank()
                        mm_tm(bk[:, :], bbk, wz, bwz, 0, 512, t0 + j * 128, hb)
                        zz, bzz = ntA()
                        silu2(zz[:, 0:512], bzz, bk[:, :], bbk)
                        ob, bob = ntB()
                        if post_fn is None:
                            P.op("dve", [bzz, b_otok], [bob], lambda e: e.scalar_tensor_tensor(
                                out=ob[:, 0:512], in0=otok[:, j, :], scalar=0.5, in1=zz[:, 0:512], op0=ALU.mult, op1=ALU.mult))
                        else:
                            post_fn(j, zz, bzz, ob, bob)
                        transpose_to(lambda: outsT[oi][:, :, j * 128:(j + 1) * 128], b_outsT[oi], ob, bob, 4)
                        yield

                lo = max(0, t0 - 15)
                hi = min(S, t0 + 527)
                nw = hi - lo
                off = lo - (t0 - 15)
                hbc = [b_hT[qb]] + ([b_hT[qb - 1]] if qb > 0 else []) + ([b_hT[qb + 1]] if qb < 3 else [])
                n1 = nw // 2
                n2 = nw - n1
                cw = {}

                def c_setup():
                        cw["wga"], cw["bwga"] = ring_load(wcols(l, O_GA, 512), 512)
                        cw["wgb"], cw["bwgb"] = ring_load(wcols(l, O_GB, 512), 512)

                def c_chunk(c):
                    wga, bwga, wgb, bwgb = cw["wga"], cw["bwga"], cw["wgb"], cw["bwgb"]
                    ue, bue = ntA()
                    if nw < 542:
                        P.op("pool", [], [bue], lambda e: e.memset(ue[:, 0:542], 0.0))
                    for (a, n) in ((0, n1), (n1, n2)):
                        bka, bbka = nbank()
                        mm_fm(bka[:, 0:n], bbka, wga, bwga, c * 128, 128, lo + a, n, hbc)
                        bkb, bbkb = nbank()
                        mm_fm(bkb[:, 0:n], bbkb, wgb, bwgb, c * 128, 128, lo + a, n, hbc)
                        th, bth = ntA()
                        P.op("act", [bbkb], [bth], lambda e: e.activation(out=th[:, 0:n], in_=bkb[:, 0:n], func=AF.Tanh, scale=0.5))
                        P.op("dve", [bth, bbka], [bue], lambda e: e.scalar_tensor_tensor(
                            out=ue[:, off + a:off + a + n], in0=th[:, 0:n], scalar=1.0, in1=bka[:, 0:n], op0=ALU.add, op1=ALU.mult))
                    a1_, ba1 = ntA()
                    NDVE = 31
                    for k in range(31):
                        wk_ = cw31[:, c, k:k + 1]
                        if k < NDVE:
                            if k == 0:
                                P.op("dve", [bue, b_prm], [ba1], lambda e: e.tensor_scalar(out=a1_[:, 0:512], in0=ue[:, k:k + 512], scalar1=wk_, scalar2=None, op0=ALU.mult))
                            else:
                                P.op("dve", [bue, b_prm, ba1], [ba1], lambda e: e.scalar_tensor_tensor(
                                    out=a1_[:, 0:512], in0=ue[:, k:k + 512], scalar=wk_, in1=a1_[:, 0:512], op0=ALU.mult, op1=ALU.add))
                        else:
                            if k == NDVE:
                                P.op("dve", [bue, b_prm], [ba2], lambda e: e.tensor_scalar(out=a2_[:, 0:512], in0=ue[:, k:k + 512], scalar1=wk_, scalar2=None, op0=ALU.mult))
                            else:
                                P.op("dve", [bue, b_prm], [ba3], lambda e: e.tensor_scalar(out=a3_[:, 0:512], in0=ue[:, k:k + 512], scalar1=wk_, scalar2=None, op0=ALU.mult))
                                P.op("dve", [ba3, ba2], [ba2], lambda e: e.tensor_tensor(out=a2_[:, 0:512], in0=a2_[:, 0:512], in1=a3_[:, 0:512], op=ALU.add))
                    if NDVE < 31:
                        P.op("dve", [ba1, ba2], [ba1], lambda e: e.tensor_tensor(out=a1_[:, 0:512], in0=a1_[:, 0:512], in1=a2_[:, 0:512], op=ALU.add))
                    P.op("dve", [ba1, b_prm], [b_uconv], lambda e: e.tensor_scalar(
                        out=uconv[:, c, :], in0=a1_[:, 0:512], scalar1=prm[:, PRM_CB + c:PRM_CB + c + 1], scalar2=None, op0=ALU.add))

                if "A" not in skip:
                    qz = []
                    for h in range(4):
                        reg = big_bf[:, h * 1024:(h + 1) * 1024] if h < 2 else tmpB[h - 2][:, :]
                        qz.append(reg.rearrange("p (s t) -> p s t", s=2))
                    fence_w = [b_big, b_qT, b_otok] + b_pT + b_tmpB
                    for h in range(4):
                        P.op("pool", [], fence_w, lambda e: e.memset(qz[h][64:128, 0, :], 0.0))
                        P.op("pool", [], fence_w, lambda e: e.memset(qz[h][0:64, 1, :], 0.0))
                    wq, bwq = ring_load(wcols(l, O_AQ, 512), 512)
                    wqs, bwqs = ring_load(wsw_b[l, :, 0:512].rearrange("(kc p) n -> p kc n", p=128), 512)
                    for c in range(4):
                        bk1, bbk1 = nbank()
                        mm_fm(bk1[:, :], bbk1, wq, bwq, c * 128, 128, t0, 512, hb)
                        bk2, bbk2 = nbank()
                        mm_fm(bk2[:, :], bbk2, wqs, bwqs, c * 128, 128, t0, 512, hb)
                        ta, bta = ntA()
                        tb2, btb2 = ntA()
                        P.op("dve", [bbk1, b_ropeA], [bta], lambda e: e.tensor_tensor(out=ta[:, 0:512], in0=bk1[:, :], in1=rope[:, 0, :], op=ALU.mult))
                        P.op("dve", [bbk2, b_ropeA], [btb2], lambda e: e.tensor_tensor(out=tb2[:, 0:512], in0=bk2[:, :], in1=rope[:, 1, :], op=ALU.mult))
                        for sub in range(2):
                            rs = slice(sub * 64, sub * 64 + 64)
                            P.op("dve", [bta, btb2], [b_qT] + ([b_tmpB[c - 2]] if c >= 2 else []), lambda e: e.tensor_tensor(
                                out=qz[c][rs, sub, :], in0=ta[rs, 0:512], in1=tb2[rs, 0:512], op=ALU.add))

                    def evac_a(tag, accs, baccs):
                        h, sub = tag
                        if sub == 0:
                            return
                        a0 = (pacc[0], pacc[1])
                        a1 = (pacc[2], pacc[3])
                        for j in range(4):
                            c0 = (j % 2) * 129
                            A0 = a0[j // 2]
                            A1 = a1[j // 2]
                            P.op("dve", [b_pacc[j // 2]], [b_st], lambda e: e.reciprocal(out=st[:, 8:9], in_=A0[:, c0 + 128:c0 + 129]))
                            P.op("dve", [b_pacc[2 + j // 2]], [b_st], lambda e: e.reciprocal(out=st[:, 9:10], in_=A1[:, c0 + 128:c0 + 129]))
                            P.op("dve", [b_st, b_lam], [b_st], lambda e: e.tensor_tensor(out=st[:, 10:11], in0=st[:, 9:10], in1=lam_t[:, 4:5], op=ALU.mult))
                            tt, btt = ntA()
                            P.op("dve", [b_pacc[2 + j // 2], b_st], [btt], lambda e: e.tensor_scalar(
                                out=tt[:, 0:128], in0=A1[:, c0:c0 + 128], scalar1=st[:, 10:11], scalar2=None, op0=ALU.mult))
                            P.op("dve", [b_pacc[j // 2], b_st, btt], [b_otok], lambda e: e.scalar_tensor_tensor(
                                out=otok[:, j, h * 128:(h + 1) * 128], in0=A0[:, c0:c0 + 128], scalar=st[:, 8:9], in1=tt[:, 0:128],
                                op0=ALU.mult, op1=ALU.add))

                    heads = []
                    for h in range(4):
                        for sub in range(2):
                            rs = slice(sub * 64, sub * 64 + 64)
                            heads.append((qz[h][:, sub, :],
                                          (lambda kc, h=h: kTa[:, h, kc * 128:(kc + 1) * 128]),
                                          (lambda kc, h=h: v1a[:, kc, h, 0:129]),
                                          sub, (h, sub)))
                    if "C" not in skip:
                        c_setup()
                    attention(heads, 16, 0.125, b_qT, list(b_kTa) + b_tmpB, list(b_v1a), evac_a,
                              pre_head=(lambda hi_: c_chunk(hi_ // 2) if (hi_ % 2 == 0 and "C" not in skip) else None))

                    def post_a(j, zz, bzz, ob, bob):
                        sq_, bsq = ntA()
                        P.op("act", [b_otok], [bsq], lambda e: e.activation(out=sq_[:, 0:512], in_=otok[:, j, :], func=AF.Square))
                        P.op("dve", [bsq], [b_st], lambda e: e.tensor_reduce(out=st[:, 12:16], in_=sq_[:, 0:512].rearrange("p (h d) -> p h d", h=4), axis=AX.X, op=ALU.add))
                        rsqrt_inplace(st[:, 12:16], b_st, 1.0 / 128)
                        on_, bon = ntA()
                        P.op("dve", [b_otok, b_st], [bon], lambda e: e.tensor_tensor(
                            out=on_[:, 0:512].rearrange("p (h d) -> p h d", h=4), in0=otok[:, j, :].rearrange("p (h d) -> p h d", h=4),
                            in1=st[:, 12:16].unsqueeze(2).broadcast_to([128, 4, 128]), op=ALU.mult))
                        P.op("dve", [bon, b_subln], [bon], lambda e: e.tensor_tensor(out=on_[:, 0:512].rearrange("p (h d) -> p h d", h=4), in0=on_[:, 0:512].rearrange("p (h d) -> p h d", h=4), in1=subln[:, :].unsqueeze(1).broadcast_to([128, 4, 128]), op=ALU.mult))
                        P.op("dve", [bon, bzz], [bob], lambda e: e.tensor_tensor(out=ob[:, 0:512], in0=on_[:, 0:512], in1=zz[:, 0:512], op=ALU.mult))

                if "C" not in skip:
                    oT, boT = outsT[1], b_outsT[1]
                    if "A" in skip:
                        c_setup()
                        for c in range(4):
                            c_chunk(c)
                    bkm, bbkm = nbank()

                    def emit_mean(e, bkm=bkm):
                        ins = None
                        for c in range(4):
                            ins = e.matmul(bkm[:, :], lhsT=onesf[:], rhs=uconv[:, c, :], start=(c == 0), stop=(c == 3))
                        return ins
                    P.op("pe", [b_uconv, bf("onesf")], [bbkm], emit_mean)
                    for c in range(4):
                        P.op("dve", [bbkm, b_uconv], [b_uconv], lambda e: e.scalar_tensor_tensor(
                            out=uconv[:, c, :], in0=bkm[:, :], scalar=-1.0 / 512, in1=uconv[:, c, :], op0=ALU.mult, op1=ALU.add))
                    sqs = []
                    for c in range(4):
                        sq_, bsq = ntA()
                        P.op("act", [b_uconv], [bsq], lambda e: e.activation(out=sq_[:, 0:512], in_=uconv[:, c, :], func=AF.Square))
                        sqs.append((sq_, bsq))
                    bkv, bbkv = nbank()

                    def emit_var(e, bkv=bkv, sqs=sqs):
                        ins = None
                        for c in range(4):
                            ins = e.matmul(bkv[:, :], lhsT=onesf[:], rhs=sqs[c][0][:, 0:512], start=(c == 0), stop=(c == 3))
                        return ins
                    P.op("pe", [x[1] for x in sqs] + [bf("onesf")], [bbkv], emit_var)
                    rs_, brs = ntA()
                    P.op("act", [bbkv], [brs], lambda e: e.activation(out=rs_[:, 0:512], in_=bkv[:, :], func=AF.Sqrt, bias=float(EPS), scale=1.0 / 512))
                    P.op("dve", [brs], [brs], lambda e: e.reciprocal(out=rs_[:, 0:512], in_=rs_[:, 0:512]))
                    wcz, bwcz = ring_load(wcols(l, O_CZ, 512), 512)
                    for c in range(4):
                        P.op("dve", [brs, b_uconv], [b_uconv], lambda e: e.tensor_tensor(out=uconv[:, c, :], in0=uconv[:, c, :], in1=rs_[:, 0:512], op=ALU.mult))
                        P.op("dve", [b_uconv, b_prm], [b_uconv], lambda e: e.tensor_scalar(
                            out=uconv[:, c, :], in0=uconv[:, c, :], scalar1=prm[:, PRM_LG + c:PRM_LG + c + 1],
                            scalar2=prm[:, PRM_LB + c:PRM_LB + c + 1], op0=ALU.mult, op1=ALU.add))
                    for c in range(4):
                        s1, bs1 = ntA()
                        silu2(s1[:, 0:512], bs1, uconv[:, c, :], b_uconv)
                        bk, bbk = nbank()
                        mm_fm(bk[:, :], bbk, wcz, bwcz, c * 128, 128, t0, 512, hb)
                        s2, bs2 = ntA()
                        silu2(s2[:, 0:512], bs2, bk[:, :], bbk)
                        P.op("dve", [bs1, bs2], [boT], lambda e: e.scalar_tensor_tensor(
                            out=oT[:, c, :], in0=s1[:, 0:512], scalar=0.25, in1=s2[:, 0:512], op0=ALU.mult, op1=ALU.mult))
                    run(gate_and_transpose(O_AZ, 0, post_a, slot=2), branch_accumulate(2, outsT[1], b_outsT[1], slots=(0, 1)))
                    run(branch_accumulate(0, outsT[0], b_outsT[0]))

                hal_l = t0 - 1 >= 0
                hal_r = t0 + 512 < S
                hbn = [b_hT[qb]] + ([b_hT[qb - 1]] if hal_l else []) + ([b_hT[qb + 1]] if hal_r else [])

                def b_chunk(c):
                    oT, boT = outsT[1], b_outsT[1]
                    w, bw = nring()
                    for i4 in range(4):
                        P.dma("sp", [b_wts], [bw] if i4 == 0 else [], bw, w[:, :, i4 * 128:(i4 + 1) * 128],
                              wcols(l, O_BB + i4 * 512 + c * 128, 128))
                    bw.w = {bw.sem_ld: bw.cnt_ld}
                    tcm, btcm = ntA()
                    P.op("pool", [], [btcm], lambda e: e.memset(tcm[:, 0:1], 0.0))
                    P.op("pool", [], [btcm], lambda e: e.memset(tcm[:, 513:514], 0.0))
                    bk, bbk = nbank()
                    mm_fm(bk[:, :], bbk, w, bw, 128, 128, t0, 512, hb)
                    P.op("act", [bbk], [btcm], lambda e: e.copy(out=tcm[:, 1:513], in_=bk[:, :]))
                    bkh, bbkh = nbank()
                    if hal_l:
                        mm_fm(bkh[:, 0:1], bbkh, w, bw, 128, 128, t0 - 1, 1, hbn)
                        P.op("act", [bbkh], [btcm], lambda e: e.copy(out=tcm[:, 0:1], in_=bkh[:, 0:1]))
                    if hal_r:
                        mm_fm(bkh[:, 1:2], bbkh, w, bw, 128, 128, t0 + 512, 1, hbn)
                        P.op("act", [bbkh], [btcm], lambda e: e.copy(out=tcm[:, 513:514], in_=bkh[:, 1:2]))
                    bk, bbk = nbank()
                    mm_fm(bk[:, :], bbk, w, bw, 256, 128, t0, 512, hb)
                    P.op("dve", [bbk, btcm], [btcm], lambda e: e.tensor_tensor(out=tcm[:, 1:513], in0=tcm[:, 1:513], in1=bk[:, :], op=ALU.mult))
                    bkh, bbkh = nbank()
                    if hal_l:
                        mm_fm(bkh[:, 0:1], bbkh, w, bw, 256, 128, t0 - 1, 1, hbn)
                        P.op("dve", [bbkh, btcm], [btcm], lambda e: e.tensor_tensor(out=tcm[:, 0:1], in0=tcm[:, 0:1], in1=bkh[:, 0:1], op=ALU.mult))
                    if hal_r:
                        mm_fm(bkh[:, 1:2], bbkh, w, bw, 256, 128, t0 + 512, 1, hbn)
                        P.op("dve", [bbkh, btcm], [btcm], lambda e: e.tensor_tensor(out=tcm[:, 513:514], in0=tcm[:, 513:514], in1=bkh[:, 1:2], op=ALU.mult))
                    w0 = prm[:, PRM_SC + c * 3 + 0:PRM_SC + c * 3 + 1]
                    w1 = prm[:, PRM_SC + c * 3 + 1:PRM_SC + c * 3 + 2]
                    w2 = prm[:, PRM_SC + c * 3 + 2:PRM_SC + c * 3 + 3]
                    P.op("dve", [btcm, b_prm], [b_uconv], lambda e: e.tensor_scalar(out=uconv[:, c, :], in0=tcm[:, 0:512], scalar1=w0, scalar2=None, op0=ALU.mult))
                    P.op("dve", [btcm, b_prm, b_uconv], [b_uconv], lambda e: e.scalar_tensor_tensor(
                        out=uconv[:, c, :], in0=tcm[:, 1:513], scalar=w1, in1=uconv[:, c, :], op0=ALU.mult, op1=ALU.add))
                    P.op("dve", [btcm, b_prm, b_uconv], [b_uconv], lambda e: e.scalar_tensor_tensor(
                        out=uconv[:, c, :], in0=tcm[:, 2:514], scalar=w2, in1=uconv[:, c, :], op0=ALU.mult, op1=ALU.add))
                    bk, bbk = nbank()
                    mm_fm(bk[:, :], bbk, w, bw, 0, 128, t0, 512, hb)
                    P.op("dve", [bbk, b_uconv], [b_uconv], lambda e: e.tensor_tensor(out=uconv[:, c, :], in0=uconv[:, c, :], in1=bk[:, :], op=ALU.mult))
                    bk, bbk = nbank()
                    mm_fm(bk[:, :], bbk, w, bw, 384, 128, t0, 512, hb)
                    zz, bzz = ntA()
                    silu2(zz[:, 0:512], bzz, bk[:, :], bbk)
                    P.op("dve", [bzz, b_uconv], [boT], lambda e: e.scalar_tensor_tensor(
                        out=oT[:, c, :], in0=uconv[:, c, :], scalar=0.5, in1=zz[:, 0:512], op0=ALU.mult, op1=ALU.mult))

                if "B" not in skip and "D" in skip:
                    for c in range(4):
                        b_chunk(c)
                    run(branch_accumulate(1, outsT[1], b_outsT[1]))

                if "D" not in skip:
                    load_rope(t0, "D")
                    wcq, bwcq = ring_load(wcols(l, O_CQ, 256), 256)
                    cqf = []
                    for c in range(2):
                        bk, bbk = nbank()
                        mm_fm(bk[:, :], bbk, wcq, bwcq, c * 128, 128, t0, 512, hb)
                        cf, bcf = ntA()
                        P.op("act", [bbk], [bcf], lambda e: e.copy(out=cf[:, 0:512], in_=bk[:, :]))
                        sq_, bsq = ntA()
                        P.op("act", [bcf], [bsq], lambda e: e.activation(out=sq_[:, 0:512], in_=cf[:, 0:512], func=AF.Square))
                        cqf.append((cf, bcf, sq_, bsq))
                    bk, bbk = nbank()

                    def emit_q(e, bk=bk, cqf=cqf):
                        ins = None
                        for c in range(2):
                            ins = e.matmul(bk[:, :], lhsT=onesf[:], rhs=cqf[c][2][:, 0:512], start=(c == 0), stop=(c == 1))
                        return ins
                    P.op("pe", [cqf[0][3], cqf[1][3], bf("onesf")], [bbk], emit_q)
                    rs_, brs = ntA()
                    P.op("act", [bbk], [brs], lambda e: e.activation(out=rs_[:, 0:512], in_=bk[:, :], func=AF.Sqrt, bias=float(EPS), scale=1.0 / 256))
                    P.op("dve", [brs], [brs], lambda e: e.reciprocal(out=rs_[:, 0:512], in_=rs_[:, 0:512]))
                    cqn, bcqn = ntB()
                    for c in range(2):
                        P.op("dve", [cqf[c][1], brs, b_prm], [bcqn], lambda e: e.scalar_tensor_tensor(
                            out=cqn[:, c * 512:(c + 1) * 512], in0=cqf[c][0][:, 0:512], scalar=prm[:, PRM_QN + c:PRM_QN + c + 1],
                            in1=rs_[:, 0:512], op0=ALU.mult, op1=ALU.mult))
                    for h in range(4):
                        bk1, bbk1 = nbank()
                        bk2, bbk2 = nbank()
                        for (bk_, bbk_, wo_) in ((bk1, bbk1, WUQ), (bk2, bbk2, WUQS)):
                            def emit_uq(e, bk_=bk_, wo_=wo_, h=h):
                                ins = None
                                for c in range(2):
                                    ins = e.matmul(bk_[0:96, :], lhsT=wsm[:, wo_ + c * 384 + h * 96:wo_ + c * 384 + (h + 1) * 96],
                                                   rhs=cqn[:, c * 512:(c + 1) * 512], start=(c == 0), stop=(c == 1))
                                return ins
                            P.op("pe", [bcqn, b_wsm], [bbk_], emit_uq)
                        P.op("act", [bbk1, b_big], [b_qT], lambda e: e.copy(out=qT[0:64, h, :], in_=bk1[0:64, :]))
                        ta, bta = ntA()
                        tb2, btb2 = ntA()
                        P.op("dve", [bbk1, b_rope], [bta], lambda e: e.tensor_tensor(out=ta[64:96, 0:512], in0=bk1[64:96, :], in1=rope[64:96, 2, :], op=ALU.mult))
                        P.op("dve", [bbk2, b_rope], [btb2], lambda e: e.tensor_tensor(out=tb2[64:96, 0:512], in0=bk2[64:96, :], in1=rope[64:96, 3, :], op=ALU.mult))
                        P.op("dve", [bta, btb2, b_big], [b_qT], lambda e: e.tensor_tensor(out=qT[64:96, h, :], in0=ta[64:96, 0:512], in1=tb2[64:96, 0:512], op=ALU.add))

                    def evac_plain(tag, accs, baccs):
                        h = tag
                        for j in range(4):
                            c0 = (j % 2) * 129
                            Aj = accs[j // 2]
                            P.op("dve", [baccs[j // 2]], [b_st], lambda e: e.reciprocal(out=st[:, 8:9], in_=Aj[:, c0 + 128:c0 + 129]))
                            P.op("dve", [baccs[j // 2], b_st], [b_otok], lambda e: e.tensor_scalar(
                                out=otok[:, j, h * 128:(h + 1) * 128], in0=Aj[:, c0:c0 + 128], scalar1=st[:, 8:9], scalar2=None, op0=ALU.mult))

                    heads = []
                    for h in range(4):
                        heads.append((qT[0:96, h, :],
                                      (lambda kc, h=h: kdT[0:96, h, kc * 128:(kc + 1) * 128]),
                                      (lambda kc, h=h: vd1[:, kc, h, 0:129]),
                                      h % 2, h))
                    attention(heads, 16, 96 ** -0.5, b_qT, list(b_kdT), list(b_vd1), evac_plain,
                              pre_head=(lambda hi_: b_chunk(hi_) if "B" not in skip else None))
                    run(gate_and_transpose(O_DZ, 0, slot=2), branch_accumulate(1, outsT[1], b_outsT[1], slots=(0, 1)))

                if "E" not in skip:
                    weq, bweq = ring_load(wcols(l, O_EQ, 512), 512)
                    for h in range(4):
                        bk, bbk = nbank()
                        mm_fm(bk[:, :], bbk, weq, bweq, h * 128, 128, t0, 512, hb)
                        P.op("act", [bbk, b_big], [b_qT], lambda e: e.copy(out=qT[:, h, :], in_=bk[:, :]))
                    heads = []
                    for h in range(4):
                        heads.append((qT[:, h, :],
                                      (lambda kc, h=h: kmT[:, h, kc * 128:(kc + 1) * 128]),
                                      (lambda kc, h=h: vm1[:, kc, h, 0:129]),
                                      h % 2, h))
                    attention(heads, 2, 128 ** -0.5, b_qT, [b_kmT], [b_vm1], evac_plain)
                    run(gate_and_transpose(O_EZ, 1, slot=2), branch_accumulate(3, outsT[0], b_outsT[0], slots=(0, 1)))
                    run(branch_accumulate(4, outsT[1], b_outsT[1]))

                for cc in range(8):
                    P.op("act", [b_yacc], [b_ymT], lambda e: e.activation(out=ymT[:, cc, :], in_=yacc[:, cc, :], func=AF.Copy, scale=0.5))
                rd = [b_xs[s]] if l > 0 else []
                P.dma("sp", rd, [b_big, b_qT, b_otok] + b_pT, b_big, xq, x_src[s, t0:t0 + 512, :].rearrange("(j p) d -> p j d", p=128))
                load_gvec(norm_post[l:l + 1, :])
                yo = yacc[:, :, :].rearrange("p a b -> p (a b)").rearrange("p (j d) -> p j d", j=4)
                for half in range(2):
                    wo_, bwo = ring_load(wo_b[l, :, half * 512:(half + 1) * 512].rearrange("(kc p) n -> p kc n", p=128), 512)
                    for j in range(4):
                        bk, bbk = nbank()

                        def emit_o(e, bk=bk, j=j, wo_=wo_):
                            ins = None
                            for kc in range(8):
                                ins = e.matmul(bk[:, :], lhsT=ymT[:, kc, j * 128:(j + 1) * 128], rhs=wo_[:, kc, 0:512],
                                               start=(kc == 0), stop=(kc == 7))
                            return ins
                        P.op("pe", [b_ymT, bwo], [bbk], emit_o)
                        P.op("act", [bbk, b_ymT], [b_yacc], lambda e: e.copy(out=yo[:, j, half * 512:(half + 1) * 512], in_=bk[:, :]))
                P.op("dve", [], [b_st], lambda e: e.memset(st[:, 16:20], 0.0))
                for j in range(4):
                    tb_, btb = ntB()
                    P.op("act", [b_yacc], [btb, b_st], lambda e: e.activation(out=tb_[:, :], in_=yo[:, j, :], func=AF.Square, accum_out=st[:, 16 + j:17 + j]))
                rsqrt_inplace(st[:, 16:20], b_st, 1.0 / D)
                for j in range(4):
                    P.op("dve", [b_yacc, b_st, b_gvec], [b_yacc], lambda e: e.scalar_tensor_tensor(
                        out=yo[:, j, :], in0=yo[:, j, :], scalar=st[:, 16 + j:17 + j], in1=gvec[:], op0=ALU.mult, op1=ALU.mult))
                    P.op("dve", [b_yacc, b_big, b_qT, b_otok] + b_pT, [b_yacc], lambda e: e.tensor_tensor(out=yo[:, j, :], in0=yo[:, j, :], in1=xq[:, j, :], op=ALU.add))
                wr_ = [b_xs[s]] if x_dst is xs else []
                ev = P.dma("pool", [b_yacc], wr_, b_yacc, x_dst[s, t0:t0 + 512, :].rearrange("(j p) d -> p j d", p=128), yo)
                b_yacc.r[ev[0]] = ev[1]
                if x_dst is not xs:
                    out_evs.append(ev)

    P.final_wait("sp", out_evs)
    P.final_wait("pool", out_evs)
    P.emit_all()
    es.close()
    return nc, P


def _rope_tables():
    t = np.arange(S, dtype=np.float32)
    tab = np.zeros((4, 128, S), np.float32)
    tab[0] = 1.0
    tab[2] = 1.0
    inv_a = (np.float32(500000.0) ** (-(np.arange(0, 16, 2, dtype=np.float32) / np.float32(16)))).astype(np.float32)
    ang_a = (t[:, None] * inv_a[None, :]).astype(np.float32)
    ca, sa = np.cos(ang_a).astype(np.float32).T, np.sin(ang_a).astype(np.float32).T
    for base in (0, 64):
        tab[0, base:base + 8] = ca
        tab[0, base + 8:base + 16] = ca
        tab[1, base:base + 8] = -sa
        tab[1, base + 8:base + 16] = sa
    inv_d = (np.float32(10000.0) ** (-(np.arange(0, 32, 2, dtype=np.float32) / np.float32(32)))).astype(np.float32)
    ang_d = (t[:, None] * inv_d[None, :]).astype(np.float32)
    cd, sd = np.cos(ang_d).astype(np.float32).T, np.sin(ang_d).astype(np.float32).T
    tab[2, 64:80] = cd
    tab[2, 80:96] = cd
    tab[3, 64:80] = -sd
    tab[3, 80:96] = sd
    return tab


_CACHE = {}


def kernel(x_prompt, x_sample, mem_prompt, mem_sample, norm_pre, w_in, diff_lambda,
           diff_subln, sconv_w, conf_dw_w, conf_dw_b, conf_ln_g, conf_ln_b,
           mla_q_norm, mla_w_uq, mla_kv_norm, mla_w_ukv, mem_norm, mem_w_kv,
           w_branch, w_out, norm_post):
    f = lambda a: np.ascontiguousarray(np.asarray(a, dtype=np.float32))
    x_prompt, x_sample, mem_prompt, mem_sample = map(f, (x_prompt, x_sample, mem_prompt, mem_sample))
    if "nc" not in _CACHE:
        _CACHE["nc"] = build_program()[0]
    nc = _CACHE["nc"]
    shared = dict(norm_pre=f(norm_pre), w_in=f(w_in), diff_lambda=f(diff_lambda), diff_subln=f(diff_subln),
                  sconv_w=f(sconv_w), conf_dw_w=f(conf_dw_w), conf_dw_b=f(conf_dw_b), conf_ln_g=f(conf_ln_g),
                  conf_ln_b=f(conf_ln_b), mla_q_norm=f(mla_q_norm), mla_w_uq=f(mla_w_uq),
                  mla_kv_norm=f(mla_kv_norm), mla_w_ukv=f(mla_w_ukv), mem_norm=f(mem_norm),
                  mem_w_kv=f(mem_w_kv), w_branch=f(w_branch), w_out=f(w_out), norm_post=f(norm_post),
                  c_ident=np.eye(128, dtype=np.float32), c_rope=_rope_tables())
    in_maps = []
    for c in range(NCORES):
        xa = np.concatenate([x_prompt[2 * c:2 * c + 2], x_sample[4 * c:4 * c + 4]], axis=0)
        ma = np.concatenate([mem_prompt[2 * c:2 * c + 2], mem_sample[4 * c:4 * c + 4]], axis=0)
        d = dict(shared)
        d["x_all"] = np.ascontiguousarray(xa)
        d["mem_all"] = np.ascontiguousarray(ma)
        in_maps.append(d)
    res = run_bass_kernel_spmd(nc, in_maps, core_ids=list(range(NCORES)))
    yp = np.empty_like(x_prompt)
    ysm = np.empty_like(x_sample)
    for c in range(NCORES):
        ya = res.results[c]["y_all"]
        yp[2 * c:2 * c + 2] = ya[0:2]
        ysm[4 * c:4 * c + 4] = ya[2:6]
    return (yp, ysm)
```

```python
import math
from contextlib import ExitStack

import numpy as np
import concourse.bass as bass
import concourse.mybir as mybir
from concourse.bass_utils import run_bass_kernel_spmd

F32 = mybir.dt.float32
BF16 = mybir.dt.bfloat16
AF = mybir.ActivationFunctionType
ALU = mybir.AluOpType
AX = mybir.AxisListType

D = 1024
S = 2048
DEPTH = 2
NCORES = 8
NSEQ = 6
NMEM = 256
INC = 12704
EPS = 1e-6
O_AQ, O_AK, O_AV, O_AZ = 0, 512, 1024, 1536
O_BB, O_BC, O_BX, O_BZ = 2048, 2560, 3072, 3584
O_GA, O_GB, O_CZ = 4096, 4608, 5120
O_CQ, O_CKV, O_KR, O_DZ = 5632, 5888, 6016, 6048
O_EQ, O_EZ, O_G = 6560, 7072, 7584
VW = 130


class Buf:
    __slots__ = ("name", "w", "r", "sem_ld", "cnt_ld")

    def __init__(self, name):
        self.name = name
        self.w = {}
        self.r = {}
        self.sem_ld = None
        self.cnt_ld = 0


class Rec:
    def __init__(self):
        self.calls = []

    def __getattr__(self, name):
        def f(*a, **k):
            self.calls.append((name, a, k))
            return self
        return f


def _replay(e, calls):
    ins = None
    for name, a, k in calls:
        ins = getattr(e, name)(*a, **k)
    return ins


class Prog:
    ENG = ("pe", "act", "dve", "pool", "sp")

    def __init__(self, nc, es):
        self.nc, self.es = nc, es
        self.q = {k: [] for k in self.ENG}
        self.sems = []
        self.esem = {}
        self.cnt = {}
        for k in ("pe", "act", "dve", "pool"):
            self.esem[k] = self._newsem("e_" + k)
            self.cnt[k] = 0
        self.known = {k: {} for k in self.ENG}
        self.nbuf = 0
        self.nwait = 0

    def _newsem(self, name):
        h = self.es.enter_context(self.nc.semaphore(name))
        self.sems.append(h)
        return len(self.sems) - 1

    def buf(self, name=None):
        self.nbuf += 1
        return Buf(name or f"b{self.nbuf}")

    def _waits(self, eng, reads, writes):
        need = {}
        for b in reads:
            for s, v in b.w.items():
                if need.get(s, 0) < v:
                    need[s] = v
        for b in writes:
            for d in (b.w, b.r):
                for s, v in d.items():
                    if need.get(s, 0) < v:
                        need[s] = v
        out = []
        kn = self.known[eng]
        for s, v in need.items():
            if kn.get(s, 0) < v:
                kn[s] = v
                out.append((s, v))
        self.nwait += len(out)
        return out

    def _commit(self, ev, reads, writes):
        s, v = ev
        for b in writes:
            b.w = {s: v}
            b.r = {}
        for b in reads:
            if b.r.get(s, 0) < v:
                b.r[s] = v

    def op(self, eng, reads, writes, emit):
        waits = self._waits(eng, reads, writes)
        self.cnt[eng] += 1
        si = self.esem[eng]
        sems = self.sems

        rec = Rec()
        emit(rec)
        calls = rec.calls
        assert calls

        def run(e, waits=waits, calls=calls, si=si):
            for s, v in waits:
                e.wait_ge(sems[s], v)
            _replay(e, calls).then_inc(sems[si], 1)

        self.q[eng].append(run)
        self._commit((si, self.cnt[eng]), reads, writes)

    def dma(self, q, reads, writes, sembuf, out_ap, in_ap, slow=False, accum=False):
        waits = self._waits(q, reads, writes)
        if sembuf.sem_ld is None:
            sembuf.sem_ld = self._newsem("d_" + sembuf.name)
        sembuf.cnt_ld += 16
        si = sembuf.sem_ld
        sems = self.sems

        def run(e, waits=waits, si=si):
            for s, v in waits:
                e.wait_ge(sems[s], v)
            if accum:
                e.dma_start(out=out_ap, in_=in_ap, accum_op=ALU.add).then_inc(sems[si], 16)
            elif slow:
                e.dma_start(out=out_ap, in_=in_ap, allow_slow_non_contiguous=True).then_inc(sems[si], 16)
            else:
                e.dma_start(out=out_ap, in_=in_ap).then_inc(sems[si], 16)

        self.q[q].append(run)
        ev = (si, sembuf.cnt_ld)
        self._commit(ev, reads, writes)
        return ev

    def final_wait(self, q, evs):
        sems = self.sems

        def run(e):
            for s, v in evs:
                e.wait_ge(sems[s], v)

        self.q[q].append(run)

    def emit_all(self):
        block = self.es.enter_context(self.nc.Block())
        qs = self.q

        @block.sync
        def _(e):
            for f in qs["sp"]:
                f(e)

        @block.scalar
        def _(e):
            for f in qs["act"]:
                f(e)

        @block.vector
        def _(e):
            for f in qs["dve"]:
                f(e)

        @block.gpsimd
        def _(e):
            for f in qs["pool"]:
                f(e)

        @block.tensor
        def _(e):
            for f in qs["pe"]:
                f(e)


def build_program(nseq=NSEQ, depth=DEPTH, debug=False, skip=()):
    nc = bass.Bass("TRN2", target_bir_lowering=False)
    es = ExitStack()
    P = Prog(nc, es)

    def din(name, shape, dt=F32):
        return nc.dram_tensor(name, list(shape), dt, kind="ExternalInput").ap()

    x_in = din("x_all", [nseq, S, D])
    mem_in = din("mem_all", [nseq, NMEM, D])
    norm_pre = din("norm_pre", [DEPTH, D])
    w_in = din("w_in", [DEPTH, D, INC])
    diff_lambda = din("diff_lambda", [DEPTH, 4, 64])
    diff_subln = din("diff_subln", [DEPTH, 128])
    sconv_w = din("sconv_w", [DEPTH, 3, 512])
    conf_dw_w = din("conf_dw_w", [DEPTH, 31, 512])
    conf_dw_b = din("conf_dw_b", [DEPTH, 512])
    conf_ln_g = din("conf_ln_g", [DEPTH, 512])
    conf_ln_b = din("conf_ln_b", [DEPTH, 512])
    mla_q_norm = din("mla_q_norm", [DEPTH, 256])
    mla_w_uq = din("mla_w_uq", [DEPTH, 256, 384])
    mla_kv_norm = din("mla_kv_norm", [DEPTH, 128])
    mla_w_ukv = din("mla_w_ukv", [DEPTH, 128, 768])
    mem_norm = din("mem_norm", [DEPTH, D])
    mem_w_kv = din("mem_w_kv", [DEPTH, D, D])
    w_branch = din("w_branch", [DEPTH, 5, 512, D])
    w_out = din("w_out", [DEPTH, D, D])
    norm_post = din("norm_post", [DEPTH, D])
    c_ident = din("c_ident", [128, 128])
    c_rope = din("c_rope", [4, 128, S])

    y_out = nc.dram_tensor("y_all", [nseq, S, D], F32, kind="ExternalOutput").ap()
    dbg = nc.dram_tensor("dbg", [5, 128, 4, 512], BF16, kind="ExternalOutput").ap() if debug else None

    def dscr(name, shape, dt):
        return nc.dram_tensor(name, list(shape), dt, kind="Internal").ap()

    xs = dscr("xs", [nseq, S, D], F32)
    wi_b = dscr("wi_b", [DEPTH, D, INC], BF16)
    wsw_b = dscr("wsw_b", [DEPTH, D, 1056], BF16)
    wuq_b = dscr("wuq_b", [DEPTH, 256, 384], BF16)
    wuqs_b = dscr("wuqs_b", [DEPTH, 256, 384], BF16)
    wukv_b = dscr("wukv_b", [DEPTH, 128, 768], BF16)
    wkv_b = dscr("wkv_b", [DEPTH, D, D], BF16)
    wb_b = dscr("wb_b", [DEPTH, 5, 512, D], BF16)
    wo_b = dscr("wo_b", [DEPTH, D, D], BF16)

    def sb(name, shape, dt):
        return es.enter_context(nc.sbuf_tensor(name, list(shape), dt))

    def ps(name, shape, dt):
        return es.enter_context(nc.psum_tensor(name, list(shape), dt))

    hT = sb("hT", [128, 8, S], BF16)
    kTa = sb("kTa", [128, 4, S], BF16)
    v1a = sb("v1a", [128, 16, 4, VW], BF16)
    kdT = sb("kdT", [128, 4, S], BF16)
    vd1 = sb("vd1", [128, 16, 4, VW], BF16)
    kmT = sb("kmT", [128, 4, NMEM], BF16)
    vm1 = sb("vm1", [128, 2, 4, VW], BF16)
    ident = sb("ident", [128, 128], BF16)
    onesf = sb("onesf", [128, 128], F32)
    subln = sb("subln", [128, 128], F32)
    prm_all = sb("prm", [128, DEPTH, 32], F32)
    cw31_all = sb("cw31", [128, DEPTH, 4, 31], F32)
    lam_t = sb("lam_t", [128, 8], F32)
    wring = [sb(f"wr{i}", [128, 8, 512], BF16) for i in range(3)]
    wsm = sb("wsm", [128, 2304], BF16)
    wkr = sb("wkr", [128, 2, 8, 96], BF16)
    rope = sb("rope", [128, 4, 512], F32)
    gvec = rope[:, 2:4, :].rearrange("p a t -> p (a t)")
    dlam = rope[:, 3, 0:256].rearrange("p (a d) -> p a d", a=4)
    big = sb("big", [128, 4096], F32)
    outsT = [sb(f"outsT{i}", [128, 4, 512], BF16) for i in range(2)]
    yacc = sb("yacc", [128, 8, 512], F32)
    identf = yacc[:, 0, 0:128]
    ymT = sb("ymT", [128, 8, 512], BF16)
    tmpA = [sb(f"tmpA{i}", [128, 544], F32) for i in range(6)]
    tmpB = [sb(f"tmpB{i}", [128, 1024], BF16) for i in range(2)]
    st = sb("st", [128, 64], F32)
    zer = sb("zer", [128, 128], BF16)
    xq = big[:, :].rearrange("p (j d) -> p j d", j=4)
    big_bf = big[:, :].bitcast(BF16)
    qT = big_bf[:, 0:2048].rearrange("p (h t) -> p h t", h=4)
    pTs = [big_bf[:, 2048 + i * 512: 2048 + (i + 1) * 512] for i in range(3)]
    uconv = ymT[:, :, :].rearrange("p a b -> p (a b)").bitcast(F32).rearrange("p (c t) -> p c t", c=4)
    otok = big[:, 2048:4096].rearrange("p (j d) -> p j d", j=4)

    pbank = [ps(f"pb{i}", [128, 512], F32) for i in range(3)]
    pacc = [ps(f"pa{i}", [128, 512], F32) for i in range(4)]
    ptr = ps("ptr", [128, 1024], BF16)

    B = {}

    def bf(name):
        if name not in B:
            B[name] = P.buf(name)
        return B[name]

    b_hT = [bf(f"hT{i}") for i in range(4)]
    b_kTa = [bf(f"kTa{i}") for i in range(4)]
    b_v1a = [bf(f"v1a{i}") for i in range(4)]
    b_kdT = [bf(f"kdT{i}") for i in range(4)]
    b_vd1 = [bf(f"vd1{i}") for i in range(4)]
    b_pbank = [bf(f"pb{i}") for i in range(3)]
    b_pacc = [bf(f"pa{i}") for i in range(4)]
    b_wring = [bf(f"wr{i}") for i in range(3)]
    b_tmpA = [bf(f"tmpA{i}") for i in range(6)]
    b_tmpB = [bf(f"tmpB{i}") for i in range(2)]
    b_outsT = [bf(f"outsT{i}") for i in range(2)]
    b_pT = [bf(f"pT{i}") for i in range(3)]
    rr = {"bank": 0, "ring": 0, "pT": 0, "tA": 0, "tB": 0, "nb": 7}
    allbanks = pbank + pacc
    b_allbanks = b_pbank + b_pacc

    def nbank():
        i = rr["bank"] % rr["nb"]
        rr["bank"] = (i + 1) % rr["nb"]
        return allbanks[i], b_allbanks[i]

    def nring():
        i = rr["ring"]
        rr["ring"] = (i + 1) % 3
        return wring[i], b_wring[i]

    def ntA():
        i = rr["tA"]
        rr["tA"] = (i + 1) % 6
        return tmpA[i], b_tmpA[i]

    def ntB():
        i = rr["tB"]
        rr["tB"] = (i + 1) % 2
        return tmpB[i], b_tmpB[i]

    PRM_SC = 0
    PRM_CB = 12
    PRM_LG = 16
    PRM_LB = 20
    PRM_QN = 24
    PRM_KN = 26

    b_const = bf("const")
    P.dma("sp", [], [b_const], b_const, identf[:], c_ident[:, :])
    P.op("dve", [b_const], [bf("ident")], lambda e: e.tensor_copy(out=ident[:], in_=identf[:]))
    P.op("pool", [], [bf("onesf")], lambda e: e.memset(onesf[:], 1.0))
    P.op("pool", [], [bf("zer")], lambda e: e.memset(zer[:], 0.0))
    P.op("pool", [], [b_v1a[0], b_v1a[1], b_v1a[2], b_v1a[3]], lambda e: e.memset(v1a[:], 1.0))
    P.op("pool", [], [b_vd1[0], b_vd1[1], b_vd1[2], b_vd1[3]], lambda e: e.memset(vd1[:], 1.0))
    P.op("pool", [], [bf("vm1")], lambda e: e.memset(vm1[:], 1.0))
    P.op("pool", [], [bf("wkr")], lambda e: e.memset(wkr[:], 0.0))

    b_prm = bf("prm")
    for l in range(depth):
        for c in range(4):
            P.dma("sp", [], [b_prm], b_prm, prm_all[:, l, PRM_SC + c * 3:PRM_SC + c * 3 + 3],
                  sconv_w[l, :, c * 128:(c + 1) * 128].rearrange("k p -> p k"), slow=True)
            P.dma("sp", [], [b_prm], b_prm, cw31_all[:, l, c, :],
                  conf_dw_w[l, :, c * 128:(c + 1) * 128].rearrange("k p -> p k"), slow=True)
        for (off, src, n) in ((PRM_CB, conf_dw_b, 4), (PRM_LG, conf_ln_g, 4), (PRM_LB, conf_ln_b, 4),
                              (PRM_QN, mla_q_norm, 2), (PRM_KN, mla_kv_norm, 1)):
            P.dma("sp", [], [b_prm], b_prm, prm_all[:, l, off:off + n],
                  src[l, :].rearrange("(c p) -> p c", p=128), slow=True)
    P.op("dve", [b_prm], [b_prm], lambda e: e.tensor_scalar(
        out=cw31_all[:, :, :, :].rearrange("p l c k -> p (l c k)"), in0=cw31_all[:, :, :, :].rearrange("p l c k -> p (l c k)"),
        scalar1=0.5, scalar2=None, op0=ALU.mult))
    b_wprep = bf("wprep")
    b_wprep2 = bf("wprep2")

    def castdma(out_ap, in_ap, second=False):
        if second:
            P.dma("pool", [b_wprep], [], b_wprep2, out_ap, in_ap)
        else:
            P.dma("pool", [], [], b_wprep, out_ap, in_ap)

    for l in range(depth):
        for r in range(8):
            castdma(wi_b[l, r * 128:(r + 1) * 128, :], w_in[l, r * 128:(r + 1) * 128, :])
            castdma(wkv_b[l, r * 128:(r + 1) * 128, :], mem_w_kv[l, r * 128:(r + 1) * 128, :])
            castdma(wo_b[l, r * 128:(r + 1) * 128, :], w_out[l, r * 128:(r + 1) * 128, :])
        for i in range(5):
            for r in range(4):
                castdma(wb_b[l, i, r * 128:(r + 1) * 128, :], w_branch[l, i, r * 128:(r + 1) * 128, :])
        castdma(wuq_b[l, :, :], mla_w_uq[l, :, :])
        castdma(wuqs_b[l, :, :], mla_w_uq[l, :, :])
        castdma(wukv_b[l, :, :], mla_w_ukv[l, :, :])
        castdma(wsw_b[l, :, 0:1024], w_in[l, :, 0:1024])
        castdma(wsw_b[l, :, 1024:1056], w_in[l, :, O_KR:O_KR + 32])
    b_wprep.w = {b_wprep.sem_ld: b_wprep.cnt_ld}
    for l in range(depth):
        for rh in range(2):
            rs = slice(rh * 512, (rh + 1) * 512)
            dst = wsw_b[l, rs, 0:1024].rearrange("r (s d) -> r s d", d=64)
            src = w_in[l, rs, 0:1024].rearrange("r (s d) -> r s d", d=64)
            castdma(dst[:, :, 0:8], src[:, :, 8:16], True)
            castdma(dst[:, :, 8:16], src[:, :, 0:8], True)
        castdma(wsw_b[l, :, 1024:1040], w_in[l, :, O_KR + 16:O_KR + 32], True)
        castdma(wsw_b[l, :, 1040:1056], w_in[l, :, O_KR:O_KR + 16], True)
        dq = wuqs_b[l, :, :].rearrange("r (h d) -> r h d", d=96)
        sq = mla_w_uq[l, :, :].rearrange("r (h d) -> r h d", d=96)
        castdma(dq[:, :, 64:80], sq[:, :, 80:96], True)
        castdma(dq[:, :, 80:96], sq[:, :, 64:80], True)
    b_wprep2.w = {b_wprep2.sem_ld: b_wprep2.cnt_ld}
    b_wts = bf("wts")
    P.op("pool", [b_wprep, b_wprep2], [b_wts], lambda e: e.memset(st[:, 61:62], 0.0))

    def load_w(dst_ap, src_ap, b_dst, extra_reads=()):
        P.dma("sp", [b_wts] + list(extra_reads), [b_dst], b_dst, dst_ap, src_ap)

    def wcols(l, c0, n):
        return wi_b[l, :, c0:c0 + n].rearrange("(kc p) n -> p kc n", p=128)

    def ring_load(src_ap, n, kcs=8, slot=None):
        w, bw = (wring[slot], b_wring[slot]) if slot is not None else nring()
        load_w(w[:, 0:kcs, 0:n], src_ap, bw)
        return w, bw

    def run(*gens):
        gens = list(gens)
        while gens:
            for g in list(gens):
                try:
                    next(g)
                except StopIteration:
                    gens.remove(g)

    def mm_fm(out_ap, b_out, w, bw, c0, ncols, tok_lo, ntok, hbufs, kcs=8, rhs_fn=None):
        def emit(e):
            ins = None
            for kc in range(kcs):
                rhs = rhs_fn(kc) if rhs_fn else hT[:, kc, tok_lo:tok_lo + ntok]
                ins = e.matmul(out_ap, lhsT=w[:, kc, c0:c0 + ncols], rhs=rhs,
                               start=(kc == 0), stop=(kc == kcs - 1))
            return ins
        P.op("pe", [bw] + list(hbufs), [b_out], emit)

    def mm_tm(out_ap, b_out, w, bw, c0, ncols, tok_lo, hbufs):
        def emit(e):
            ins = None
            for kc in range(8):
                ins = e.matmul(out_ap, lhsT=hT[:, kc, tok_lo:tok_lo + 128], rhs=w[:, kc, c0:c0 + ncols],
                               start=(kc == 0), stop=(kc == 7))
            return ins
        P.op("pe", [bw] + list(hbufs), [b_out], emit)

    def rsqrt_inplace(ap, b, scale, n_eps=EPS):
        P.op("act", [b], [b], lambda e: e.activation(out=ap, in_=ap, func=AF.Sqrt, bias=float(n_eps), scale=float(scale)))
        P.op("dve", [b], [b], lambda e: e.reciprocal(out=ap, in_=ap))

    def silu2(out_ap, b_out, z_ap, b_z):
        t, bt = ntA()
        n = z_ap.shape[-1] if len(z_ap.shape) == 2 else None
        tv = t[0:z_ap.shape[0], 0:z_ap.shape[1]]
        P.op("act", [b_z], [bt], lambda e: e.activation(out=tv, in_=z_ap, func=AF.Tanh, scale=0.5))
        P.op("dve", [b_z, bt], [b_out], lambda e: e.scalar_tensor_tensor(
            out=out_ap, in0=tv, scalar=1.0, in1=z_ap, op0=ALU.add, op1=ALU.mult))

    def transpose_to(dst_fn, b_dst, src, b_src, nchunks):
        b_ptr = bf("ptr")

        def emit(e):
            ins = None
            for kc in range(nchunks):
                ins = e.transpose(out=ptr[:, kc * 128:(kc + 1) * 128], in_=src[:, kc * 128:(kc + 1) * 128],
                                  identity=ident[:])
            return ins
        P.op("pe", [b_src, bf("ident")], [b_ptr], emit)
        P.op("act", [b_ptr], [b_dst], lambda e: e.copy(
            out=dst_fn(), in_=ptr[:, 0:nchunks * 128].rearrange("p (k t) -> p k t", t=128)))

    def attention(heads, nkc, scale, b_q, b_k, b_v, evac, pre_head=None):
        rr["nb"] = 3
        rr["bank"] = 0
        for hidx, (q_ap, k_fn, v_fn, aset, tag) in enumerate(heads):
            if pre_head is not None:
                pre_head(hidx)
            accs = (pacc[2 * aset], pacc[2 * aset + 1])
            baccs = [b_pacc[2 * aset], b_pacc[2 * aset + 1]]
            steps = []

            def qk(kc):
                bank, bbank = nbank()
                P.op("pe", [b_q] + b_k, [bbank], lambda e: e.matmul(bank[:, :], lhsT=k_fn(kc), rhs=q_ap,
                                                                       start=True, stop=True))
                i = rr["pT"]
                rr["pT"] = (i + 1) % 3
                P.op("act", [bbank], [b_pT[i]], lambda e: e.activation(out=pTs[i], in_=bank[:, :], func=AF.Exp,
                                                                         scale=float(scale)))
                return i

            def zero_acc():
                def emit(e):
                    ins = None
                    for a in accs:
                        ins = e.matmul(a[:, 0:258], lhsT=zer[:, 0:128], rhs=hT[:, 0, 0:258], start=True, stop=False)
                    return ins
                P.op("pe", [bf("zer"), b_hT[0]], baccs, emit)

            def pv(kc, i):
                def emit(e):
                    ins = None
                    for j in range(4):
                        ins = e.matmul(accs[j // 2][:, (j % 2) * 129:(j % 2) * 129 + 129],
                                       lhsT=pTs[i][:, j * 128:(j + 1) * 128], rhs=v_fn(kc),
                                       start=False, stop=(kc == nkc - 1 and j % 2 == 1))
                    return ins
                P.op("pe", [b_pT[i]] + b_v, baccs, emit)

            zero_acc()
            pend = [qk(0)]
            if nkc > 1:
                pend.append(qk(1))
            for kc in range(nkc):
                if kc + 2 < nkc:
                    pend.append(qk(kc + 2))
                pv(kc, pend.pop(0))
            evac(tag, accs, baccs)
        rr["nb"] = 7

    out_evs = []
    b_big = bf("big")
    b_rope = bf("ropeD")
    b_ropeA = bf("ropeA")
    b_gvec = b_rope
    b_st = bf("st")
    b_otok = bf("otok")
    b_qT = bf("qT")
    b_yacc = bf("yacc")
    b_ymT = bf("ymT")
    b_uconv = b_ymT
    b_wsm = bf("wsm")
    b_wkr = bf("wkr")
    b_lam = bf("lam")
    b_subln = bf("subln")
    b_kmT = bf("kmT")
    b_vm1 = bf("vm1")
    b_xs = [bf(f"xs{s}") for s in range(nseq)]
    b_yo = [bf(f"yo{s}") for s in range(nseq)]

    WUQ, WUQS, WUKV = 0, 768, 1536

    def load_gvec(src_row):
        P.dma("sp", [], [b_gvec], b_gvec, gvec[:], src_row.partition_broadcast(128))

    def load_rope(t0, which="AD"):
        if "A" in which:
            P.dma("sp", [], [b_ropeA], b_ropeA, rope[:, 0:2, :], c_rope[0:2, :, t0:t0 + 512].rearrange("i p t -> p i t"))
        if "D" in which:
            P.dma("sp", [], [b_rope], b_rope, rope[:, 2:4, :], c_rope[2:4, :, t0:t0 + 512].rearrange("i p t -> p i t"))

    def rms_rows(x3, bx, ntile, gsrc_loaded, dst_fn, b_dst):
        P.op("dve", [], [b_st], lambda e: e.memset(st[:, 0:ntile], 0.0))
        for j in range(ntile):
            tb_, btb = ntB()
            P.op("act", [bx], [btb, b_st], lambda e: e.activation(out=tb_[:, :], in_=x3[:, j, :], func=AF.Square,
                                                                   accum_out=st[:, j:j + 1]))
        rsqrt_inplace(st[:, 0:ntile], b_st, 1.0 / D)
        for j in range(ntile):
            tb_, btb = ntB()
            P.op("dve", [bx, b_st, b_gvec], [btb], lambda e: e.scalar_tensor_tensor(
                out=tb_[:, :], in0=x3[:, j, :], scalar=st[:, j:j + 1], in1=gvec[:], op0=ALU.mult, op1=ALU.mult))
            transpose_to(lambda: dst_fn(j), b_dst, tb_, btb, 8)

    for s in range(nseq):
        for l in range(depth):
            x_src = x_in if l == 0 else xs
            x_dst = xs if (l == 0 and depth > 1) else y_out
            lam_init = 0.8 - 0.6 * math.exp(-0.3 * l)
            prm = prm_all[:, l, :]
            cw31 = cw31_all[:, l, :, :]
            P.dma("sp", [], [b_subln], b_subln, subln[:, :], diff_subln[l:l + 1, :].partition_broadcast(128))
            P.op("dve", [b_subln], [b_subln], lambda e: e.tensor_scalar(
                out=subln[:], in0=subln[:], scalar1=float((1.0 - lam_init) * 0.5), scalar2=None, op0=ALU.mult))
            P.dma("sp", [], [b_lam, b_rope], b_lam, rope[:, 3, 0:256],
                  diff_lambda[l:l + 1, :, :].rearrange("o a d -> o (a d)").partition_broadcast(128))
            t0_, bt0 = ntA()
            P.op("dve", [b_lam, b_rope], [bt0], lambda e: e.tensor_tensor(out=t0_[:, 0:64], in0=dlam[:, 0, :], in1=dlam[:, 1, :], op=ALU.mult))
            P.op("dve", [b_lam, b_rope], [bt0], lambda e: e.tensor_tensor(out=t0_[:, 64:128], in0=dlam[:, 2, :], in1=dlam[:, 3, :], op=ALU.mult))
            P.op("dve", [bt0], [b_lam], lambda e: e.tensor_reduce(out=lam_t[:, 0:2], in_=t0_[:, 0:128].rearrange("p (a d) -> p a d", a=2), axis=AX.X, op=ALU.add))
            P.op("act", [b_lam], [b_lam], lambda e: e.activation(out=lam_t[:, 2:4], in_=lam_t[:, 0:2], func=AF.Exp))
            P.op("dve", [b_lam], [b_lam], lambda e: e.scalar_tensor_tensor(
                out=lam_t[:, 4:5], in0=lam_t[:, 3:4], scalar=float(-lam_init), in1=lam_t[:, 2:3], op0=ALU.add, op1=ALU.subtract))
            b_dst = b_xs[s] if x_dst is xs else b_yo[s]
            for half in range(2):
                rs_ = slice(half * 1024, (half + 1) * 1024)
                P.dma("sp", [b_xs[s]] if l > 0 else [], [b_dst] if half == 0 else [], b_dst, x_dst[s, rs_, :], x_src[s, rs_, :])
            b_dst.w = {b_dst.sem_ld: b_dst.cnt_ld}
            load_gvec(norm_pre[l:l + 1, :])
            for tb in range(4):
                t0 = tb * 512
                rd = [b_xs[s]] if l > 0 else []
                P.dma("sp", rd, [b_big], b_big, xq, x_src[s, t0:t0 + 512, :].rearrange("(j p) d -> p j d", p=128))
                rms_rows(xq, b_big, 4, None, lambda j, t0=t0: hT[:, :, t0 + j * 128:t0 + (j + 1) * 128], b_hT[tb])
            load_w(wsm[:, WUQ:WUQ + 768].rearrange("p (k n) -> p k n", k=2),
                   wuq_b[l, :, :].rearrange("(k p) n -> p k n", p=128), b_wsm)
            load_w(wsm[:, WUQS:WUQS + 768].rearrange("p (k n) -> p k n", k=2),
                   wuqs_b[l, :, :].rearrange("(k p) n -> p k n", p=128), b_wsm)
            load_w(wsm[:, WUKV:WUKV + 768], wukv_b[l, :, :], b_wsm)
            load_w(wkr[:, 0, :, 64:96], wi_b[l, :, O_KR:O_KR + 32].rearrange("(kc p) n -> p kc n", p=128), b_wkr)
            load_w(wkr[:, 1, :, 64:96], wsw_b[l, :, 1024:1056].rearrange("(kc p) n -> p kc n", p=128), b_wkr)

            for tb in range(4):
                t0 = tb * 512
                hb = [b_hT[tb]]
                load_rope(t0)
                wk, bwk = ring_load(wcols(l, O_AK, 512), 512)
                wks, bwks = ring_load(wsw_b[l, :, 512:1024].rearrange("(kc p) n -> p kc n", p=128), 512)
                for c in range(4):
                    bk1, bbk1 = nbank()
                    mm_fm(bk1[:, :], bbk1, wk, bwk, c * 128, 128, t0, 512, hb)
                    bk2, bbk2 = nbank()
                    mm_fm(bk2[:, :], bbk2, wks, bwks, c * 128, 128, t0, 512, hb)
                    ta, bta = ntA()
                    tb2, btb2 = ntA()
                    P.op("dve", [bbk1, b_ropeA], [bta], lambda e: e.tensor_tensor(out=ta[:, 0:512], in0=bk1[:, :], in1=rope[:, 0, :], op=ALU.mult))
                    P.op("dve", [bbk2, b_ropeA], [btb2], lambda e: e.tensor_tensor(out=tb2[:, 0:512], in0=bk2[:, :], in1=rope[:, 1, :], op=ALU.mult))
                    P.op("dve", [bta, btb2], [b_kTa[tb]], lambda e: e.tensor_tensor(out=kTa[:, c, t0:t0 + 512], in0=ta[:, 0:512], in1=tb2[:, 0:512], op=ALU.add))
                wv, bwv = ring_load(wcols(l, O_AV, 512), 512)
                for j in range(4):
                    bk, bbk = nbank()
                    mm_tm(bk[:, :], bbk, wv, bwv, 0, 512, t0 + j * 128, hb)
                    P.op("act", [bbk], [b_v1a[tb]], lambda e: e.copy(out=v1a[:, tb * 4 + j, :, 0:128],
                                                                      in_=bk[:, :].rearrange("p (h d) -> p h d", h=4)))
                wd, bwd = ring_load(wcols(l, O_CKV, 128), 128)
                bk, bbk = nbank()
                mm_fm(bk[:, :], bbk, wd, bwd, 0, 128, t0, 512, hb)
                ckf, bckf = ntA()
                P.op("act", [bbk], [bckf], lambda e: e.copy(out=ckf[:, 0:512], in_=bk[:, :]))
                sqf, bsqf = ntA()
                P.op("act", [bckf], [bsqf], lambda e: e.activation(out=sqf[:, 0:512], in_=ckf[:, 0:512], func=AF.Square))
                bk2, bbk2 = nbank()
                P.op("pe", [bsqf, bf("onesf")], [bbk2], lambda e: e.matmul(bk2[:, :], lhsT=onesf[:], rhs=sqf[:, 0:512], start=True, stop=True))
                rs_, brs = ntA()
                P.op("act", [bbk2], [brs], lambda e: e.activation(out=rs_[:, 0:512], in_=bk2[:, :], func=AF.Sqrt, bias=float(EPS), scale=1.0 / 128))
                P.op("dve", [brs], [brs], lambda e: e.reciprocal(out=rs_[:, 0:512], in_=rs_[:, 0:512]))
                ckn, bckn = ntB()
                P.op("dve", [bckf, brs, b_prm], [bckn], lambda e: e.scalar_tensor_tensor(
                    out=ckn[:, 0:512], in0=ckf[:, 0:512], scalar=prm[:, PRM_KN:PRM_KN + 1], in1=rs_[:, 0:512], op0=ALU.mult, op1=ALU.mult))
                for h in range(4):
                    bk, bbk = nbank()
                    P.op("pe", [bckn, b_wsm], [bbk], lambda e: e.matmul(bk[0:64, :], lhsT=wsm[:, WUKV + h * 192:WUKV + h * 192 + 64],
                                                                         rhs=ckn[:, 0:512], start=True, stop=True))
                    P.op("act", [bbk], [b_kdT[tb]], lambda e: e.copy(out=kdT[0:64, h, t0:t0 + 512], in_=bk[0:64, :]))
                wukv_v = wsm[:, WUKV:WUKV + 768].rearrange("p (h d) -> p h d", h=4)[:, :, 64:192]
                for j in range(4):
                    bk, bbk = nbank()
                    P.op("pe", [bckn, b_wsm], [bbk], lambda e: e.matmul(bk[:, :].rearrange("p (h d) -> p h d", h=4),
                                                                         lhsT=ckn[:, j * 128:(j + 1) * 128], rhs=wukv_v, start=True, stop=True))
                    P.op("act", [bbk], [b_vd1[tb]], lambda e: e.copy(out=vd1[:, tb * 4 + j, :, 0:128],
                                                                      in_=bk[:, :].rearrange("p (h d) -> p h d", h=4)))
                bk1, bbk1 = nbank()
                mm_fm(bk1[0:96, :], bbk1, wkr[:, 0], b_wkr, 0, 96, t0, 512, hb)
                bk2, bbk2 = nbank()
                mm_fm(bk2[0:96, :], bbk2, wkr[:, 1], b_wkr, 0, 96, t0, 512, hb)
                ta, bta = ntA()
                tb2, btb2 = ntA()
                P.op("dve", [bbk1, b_rope], [bta], lambda e: e.tensor_tensor(out=ta[64:96, 0:512], in0=bk1[64:96, :], in1=rope[64:96, 2, :], op=ALU.mult))
                P.op("dve", [bbk2, b_rope], [btb2], lambda e: e.tensor_tensor(out=tb2[64:96, 0:512], in0=bk2[64:96, :], in1=rope[64:96, 3, :], op=ALU.mult))
                for h in range(4):
                    P.op("dve", [bta, btb2], [b_kdT[tb]], lambda e: e.tensor_tensor(out=kdT[64:96, h, t0:t0 + 512], in0=ta[64:96, 0:512], in1=tb2[64:96, 0:512], op=ALU.add))
            load_gvec(mem_norm[l:l + 1, :])
            P.dma("sp", [], [b_big], b_big, xq[:, 0:2, :], mem_in[s, :, :].rearrange("(j p) d -> p j d", p=128))
            memT = outsT[0][:, :, :].rearrange("p a b -> p (a b)").rearrange("p (k t) -> p k t", k=8)
            b_memT = b_outsT[0]
            rms_rows(xq, b_big, 2, None, lambda j: memT[:, :, j * 128:(j + 1) * 128], b_memT)
            wm0, bwm0 = ring_load(wkv_b[l, :, 0:512].rearrange("(kc p) n -> p kc n", p=128), 512)
            for h in range(4):
                bk, bbk = nbank()
                mm_fm(bk[:, 0:NMEM], bbk, wm0, bwm0, h * 128, 128, 0, NMEM, [b_memT], rhs_fn=lambda kc: memT[:, kc, :])
                P.op("act", [bbk], [b_kmT], lambda e: e.copy(out=kmT[:, h, :], in_=bk[:, 0:NMEM]))
            wm1, bwm1 = ring_load(wkv_b[l, :, 512:1024].rearrange("(kc p) n -> p kc n", p=128), 512)
            for j in range(2):
                bk, bbk = nbank()

                def emit(e, bk=bk, j=j):
                    ins = None
                    for kc in range(8):
                        ins = e.matmul(bk[:, :], lhsT=memT[:, kc, j * 128:(j + 1) * 128], rhs=wm1[:, kc, 0:512],
                                       start=(kc == 0), stop=(kc == 7))
                    return ins
                P.op("pe", [b_memT, bwm1], [bbk], emit)
                P.op("act", [bbk], [b_vm1], lambda e: e.copy(out=vm1[:, j, :, 0:128], in_=bk[:, :].rearrange("p (h d) -> p h d", h=4)))

            all_h = list(b_hT)
            for qb in range(4):
                t0 = qb * 512
                hb = [b_hT[qb]]
                load_rope(t0, "A")

                acc_first = [True]

                def branch_accumulate(i, oT, boT, slots=(None, None)):
                    init = acc_first[0]
                    acc_first[0] = False
                    if debug and s == 0 and l == 0 and qb == 0:
                        ev = P.dma("sp", [boT], [], bf(f"dbg{i}"), dbg[i], oT[:, :, :])
                        boT.r[ev[0]] = ev[1]
                        out_evs.append(ev)
                    wbi, bwbi = ring_load(wb_b[l, i, :, 0:512].rearrange("(kc p) n -> p kc n", p=128), 512, kcs=4, slot=slots[0])
                    for half in range(2):
                        if half == 1:
                            wbi, bwbi = ring_load(wb_b[l, i, :, 512:1024].rearrange("(kc p) n -> p kc n", p=128), 512, kcs=4, slot=slots[0])
                        wg, bwg = ring_load(wcols(l, O_G + i * 1024 + half * 512, 512), 512, slot=slots[1])
                        for c4 in range(4):
                            cc = half * 4 + c4
                            bkb, bbkb = nbank()
                            mm_fm(bkb[:, :], bbkb, wbi, bwbi, c4 * 128, 128, 0, 512, [boT], kcs=4,
                                  rhs_fn=lambda kc: oT[:, kc, :])
                            bkg, bbkg = nbank()
                            mm_fm(bkg[:, :], bbkg, wg, bwg, c4 * 128, 128, t0, 512, hb)
                            th, bth = ntA()
                            P.op("act", [bbkg], [bth], lambda e: e.activation(out=th[:, 0:512], in_=bkg[:, :], func=AF.Tanh, scale=0.5))
                            if init:
                                P.op("dve", [bth, bbkb], [b_yacc], lambda e: e.scalar_tensor_tensor(
                                    out=yacc[:, cc, :], in0=th[:, 0:512], scalar=1.0, in1=bkb[:, :], op0=ALU.add, op1=ALU.mult))
                            else:
                                pr, bpr = ntA()
                                P.op("dve", [bth, bbkb], [bpr], lambda e: e.scalar_tensor_tensor(
                                    out=pr[:, 0:512], in0=th[:, 0:512], scalar=1.0, in1=bkb[:, :], op0=ALU.add, op1=ALU.mult))
                                P.op("dve", [bpr, b_yacc], [b_yacc], lambda e: e.tensor_tensor(
                                    out=yacc[:, cc, :], in0=yacc[:, cc, :], in1=pr[:, 0:512], op=ALU.add))
                            yield

                def gate_and_transpose(zoff, oi, post_fn=None, slot=None):
                    wz, bwz = ring_load(wcols(l, zoff, 512), 512, slot=slot)
                    for j in range(4):
                        bk, bbk = nbank()
                        mm_tm(bk[:, :], bbk, wz, bwz, 0, 512, t0 + j * 128, hb)
                        zz, bzz = ntA()
                        silu2(zz[:, 0:512], bzz, bk[:, :], bbk)
                        ob, bob = ntB()
                        if post_fn is None:
                            P.op("dve", [bzz, b_otok], [bob], lambda e: e.scalar_tensor_tensor(
                                out=ob[:, 0:512], in0=otok[:, j, :], scalar=0.5, in1=zz[:, 0:512], op0=ALU.mult, op1=ALU.mult))
                        else:
                            post_fn(j, zz, bzz, ob, bob)
                        transpose_to(lambda: outsT[oi][:, :, j * 128:(j + 1) * 128], b_outsT[oi], ob, bob, 4)
                        yield

                lo = max(0, t0 - 15)
                hi = min(S, t0 + 527)
                nw = hi - lo
                off = lo - (t0 - 15)
                hbc = [b_hT[qb]] + ([b_hT[qb - 1]] if qb > 0 else []) + ([b_hT[qb + 1]] if qb < 3 else [])
                n1 = nw // 2
                n2 = nw - n1
                cw = {}

                def c_setup():
                        cw["wga"], cw["bwga"] = ring_load(wcols(l, O_GA, 512), 512)
                        cw["wgb"], cw["bwgb"] = ring_load(wcols(l, O_GB, 512), 512)

                def c_chunk(c):
                    wga, bwga, wgb, bwgb = cw["wga"], cw["bwga"], cw["wgb"], cw["bwgb"]
                    ue, bue = ntA()
                    if nw < 542:
                        P.op("pool", [], [bue], lambda e: e.memset(ue[:, 0:542], 0.0))
                    for (a, n) in ((0, n1), (n1, n2)):
                        bka, bbka = nbank()
                        mm_fm(bka[:, 0:n], bbka, wga, bwga, c * 128, 128, lo + a, n, hbc)
                        bkb, bbkb = nbank()
                        mm_fm(bkb[:, 0:n], bbkb, wgb, bwgb, c * 128, 128, lo + a, n, hbc)
                        th, bth = ntA()
                        P.op("act", [bbkb], [bth], lambda e: e.activation(out=th[:, 0:n], in_=bkb[:, 0:n], func=AF.Tanh, scale=0.5))
                        P.op("dve", [bth, bbka], [bue], lambda e: e.scalar_tensor_tensor(
                            out=ue[:, off + a:off + a + n], in0=th[:, 0:n], scalar=1.0, in1=bka[:, 0:n], op0=ALU.add, op1=ALU.mult))
                    a1_, ba1 = ntA()
                    a2_, ba2 = ntA()
                    accs2 = ((a1_, ba1), (a2_, ba2))
                    for k in range(31):
                        wk_ = cw31[:, c, k:k + 1]
                        acc_, bacc_ = accs2[k % 2]
                        if k < 2:
                            P.op("dve", [bue, b_prm], [bacc_], lambda e: e.tensor_scalar(out=acc_[:, 0:512], in0=ue[:, k:k + 512], scalar1=wk_, scalar2=None, op0=ALU.mult))
                        else:
                            P.op("dve", [bue, b_prm, bacc_], [bacc_], lambda e: e.scalar_tensor_tensor(
                                out=acc_[:, 0:512], in0=ue[:, k:k + 512], scalar=wk_, in1=acc_[:, 0:512], op0=ALU.mult, op1=ALU.add))
                    P.op("dve", [ba1, ba2], [ba1], lambda e: e.tensor_tensor(out=a1_[:, 0:512], in0=a1_[:, 0:512], in1=a2_[:, 0:512], op=ALU.add))
                    P.op("dve", [ba1, b_prm], [b_uconv], lambda e: e.tensor_scalar(
                        out=uconv[:, c, :], in0=a1_[:, 0:512], scalar1=prm[:, PRM_CB + c:PRM_CB + c + 1], scalar2=None, op0=ALU.add))

                if "A" not in skip:
                    qz = []
                    for h in range(4):
                        reg = big_bf[:, h * 1024:(h + 1) * 1024] if h < 2 else tmpB[h - 2][:, :]
                        qz.append(reg.rearrange("p (s t) -> p s t", s=2))
                    fence_w = [b_big, b_qT, b_otok] + b_pT + b_tmpB
                    for h in range(4):
                        P.op("pool", [], fence_w, lambda e: e.memset(qz[h][64:128, 0, :], 0.0))
                        P.op("pool", [], fence_w, lambda e: e.memset(qz[h][0:64, 1, :], 0.0))
                    wq, bwq = ring_load(wcols(l, O_AQ, 512), 512)
                    wqs, bwqs = ring_load(wsw_b[l, :, 0:512].rearrange("(kc p) n -> p kc n", p=128), 512)
                    for c in range(4):
                        bk1, bbk1 = nbank()
                        mm_fm(bk1[:, :], bbk1, wq, bwq, c * 128, 128, t0, 512, hb)
                        bk2, bbk2 = nbank()
                        mm_fm(bk2[:, :], bbk2, wqs, bwqs, c * 128, 128, t0, 512, hb)
                        ta, bta = ntA()
                        tb2, btb2 = ntA()
                        P.op("dve", [bbk1, b_ropeA], [bta], lambda e: e.tensor_tensor(out=ta[:, 0:512], in0=bk1[:, :], in1=rope[:, 0, :], op=ALU.mult))
                        P.op("dve", [bbk2, b_ropeA], [btb2], lambda e: e.tensor_tensor(out=tb2[:, 0:512], in0=bk2[:, :], in1=rope[:, 1, :], op=ALU.mult))
                        for sub in range(2):
                            rs = slice(sub * 64, sub * 64 + 64)
                            P.op("dve", [bta, btb2], [b_qT] + ([b_tmpB[c - 2]] if c >= 2 else []), lambda e: e.tensor_tensor(
                                out=qz[c][rs, sub, :], in0=ta[rs, 0:512], in1=tb2[rs, 0:512], op=ALU.add))

                    def evac_a(tag, accs, baccs):
                        h, sub = tag
                        if sub == 0:
                            return
                        a0 = (pacc[0], pacc[1])
                        a1 = (pacc[2], pacc[3])
                        for j in range(4):
                            c0 = (j % 2) * 129
                            A0 = a0[j // 2]
                            A1 = a1[j // 2]
                            P.op("dve", [b_pacc[j // 2]], [b_st], lambda e: e.reciprocal(out=st[:, 8:9], in_=A0[:, c0 + 128:c0 + 129]))
                            P.op("dve", [b_pacc[2 + j // 2]], [b_st], lambda e: e.reciprocal(out=st[:, 9:10], in_=A1[:, c0 + 128:c0 + 129]))
                            P.op("dve", [b_st, b_lam], [b_st], lambda e: e.tensor_tensor(out=st[:, 10:11], in0=st[:, 9:10], in1=lam_t[:, 4:5], op=ALU.mult))
                            tt, btt = ntA()
                            P.op("dve", [b_pacc[2 + j // 2], b_st], [btt], lambda e: e.tensor_scalar(
                                out=tt[:, 0:128], in0=A1[:, c0:c0 + 128], scalar1=st[:, 10:11], scalar2=None, op0=ALU.mult))
                            P.op("dve", [b_pacc[j // 2], b_st, btt], [b_otok], lambda e: e.scalar_tensor_tensor(
                                out=otok[:, j, h * 128:(h + 1) * 128], in0=A0[:, c0:c0 + 128], scalar=st[:, 8:9], in1=tt[:, 0:128],
                                op0=ALU.mult, op1=ALU.add))

                    heads = []
                    for h in range(4):
                        for sub in range(2):
                            rs = slice(sub * 64, sub * 64 + 64)
                            heads.append((qz[h][:, sub, :],
                                          (lambda kc, h=h: kTa[:, h, kc * 128:(kc + 1) * 128]),
                                          (lambda kc, h=h: v1a[:, kc, h, 0:129]),
                                          sub, (h, sub)))
                    if "C" not in skip:
                        c_setup()
                    attention(heads, 16, 0.125, b_qT, list(b_kTa) + b_tmpB, list(b_v1a), evac_a,
                              pre_head=(lambda hi_: c_chunk(hi_ // 2) if (hi_ % 2 == 0 and "C" not in skip) else None))

                    def post_a(j, zz, bzz, ob, bob):
                        sq_, bsq = ntA()
                        P.op("act", [b_otok], [bsq], lambda e: e.activation(out=sq_[:, 0:512], in_=otok[:, j, :], func=AF.Square))
                        P.op("dve", [bsq], [b_st], lambda e: e.tensor_reduce(out=st[:, 12:16], in_=sq_[:, 0:512].rearrange("p (h d) -> p h d", h=4), axis=AX.X, op=ALU.add))
                        rsqrt_inplace(st[:, 12:16], b_st, 1.0 / 128)
                        on_, bon = ntA()
                        P.op("dve", [b_otok, b_st], [bon], lambda e: e.tensor_tensor(
                            out=on_[:, 0:512].rearrange("p (h d) -> p h d", h=4), in0=otok[:, j, :].rearrange("p (h d) -> p h d", h=4),
                            in1=st[:, 12:16].unsqueeze(2).broadcast_to([128, 4, 128]), op=ALU.mult))
                        P.op("dve", [bon, b_subln], [bon], lambda e: e.tensor_tensor(out=on_[:, 0:512].rearrange("p (h d) -> p h d", h=4), in0=on_[:, 0:512].rearrange("p (h d) -> p h d", h=4), in1=subln[:, :].unsqueeze(1).broadcast_to([128, 4, 128]), op=ALU.mult))
                        P.op("dve", [bon, bzz], [bob], lambda e: e.tensor_tensor(out=ob[:, 0:512], in0=on_[:, 0:512], in1=zz[:, 0:512], op=ALU.mult))

                if "C" not in skip:
                    oT, boT = outsT[1], b_outsT[1]
                    if "A" in skip:
                        c_setup()
                        for c in range(4):
                            c_chunk(c)
                    bkm, bbkm = nbank()

                    def emit_mean(e, bkm=bkm):
                        ins = None
                        for c in range(4):
                            ins = e.matmul(bkm[:, :], lhsT=onesf[:], rhs=uconv[:, c, :], start=(c == 0), stop=(c == 3))
                        return ins
                    P.op("pe", [b_uconv, bf("onesf")], [bbkm], emit_mean)
                    for c in range(4):
                        P.op("dve", [bbkm, b_uconv], [b_uconv], lambda e: e.scalar_tensor_tensor(
                            out=uconv[:, c, :], in0=bkm[:, :], scalar=-1.0 / 512, in1=uconv[:, c, :], op0=ALU.mult, op1=ALU.add))
                    sqs = []
                    for c in range(4):
                        sq_, bsq = ntA()
                        P.op("act", [b_uconv], [bsq], lambda e: e.activation(out=sq_[:, 0:512], in_=uconv[:, c, :], func=AF.Square))
                        sqs.append((sq_, bsq))
                    bkv, bbkv = nbank()

                    def emit_var(e, bkv=bkv, sqs=sqs):
                        ins = None
                        for c in range(4):
                            ins = e.matmul(bkv[:, :], lhsT=onesf[:], rhs=sqs[c][0][:, 0:512], start=(c == 0), stop=(c == 3))
                        return ins
                    P.op("pe", [x[1] for x in sqs] + [bf("onesf")], [bbkv], emit_var)
                    rs_, brs = ntA()
                    P.op("act", [bbkv], [brs], lambda e: e.activation(out=rs_[:, 0:512], in_=bkv[:, :], func=AF.Sqrt, bias=float(EPS), scale=1.0 / 512))
                    P.op("dve", [brs], [brs], lambda e: e.reciprocal(out=rs_[:, 0:512], in_=rs_[:, 0:512]))
                    wcz, bwcz = ring_load(wcols(l, O_CZ, 512), 512)
                    for c in range(4):
                        P.op("dve", [brs, b_uconv], [b_uconv], lambda e: e.tensor_tensor(out=uconv[:, c, :], in0=uconv[:, c, :], in1=rs_[:, 0:512], op=ALU.mult))
                        P.op("dve", [b_uconv, b_prm], [b_uconv], lambda e: e.tensor_scalar(
                            out=uconv[:, c, :], in0=uconv[:, c, :], scalar1=prm[:, PRM_LG + c:PRM_LG + c + 1],
                            scalar2=prm[:, PRM_LB + c:PRM_LB + c + 1], op0=ALU.mult, op1=ALU.add))
                    for c in range(4):
                        s1, bs1 = ntA()
                        silu2(s1[:, 0:512], bs1, uconv[:, c, :], b_uconv)
                        bk, bbk = nbank()
                        mm_fm(bk[:, :], bbk, wcz, bwcz, c * 128, 128, t0, 512, hb)
                        s2, bs2 = ntA()
                        silu2(s2[:, 0:512], bs2, bk[:, :], bbk)
                        P.op("dve", [bs1, bs2], [boT], lambda e: e.scalar_tensor_tensor(
                            out=oT[:, c, :], in0=s1[:, 0:512], scalar=0.25, in1=s2[:, 0:512], op0=ALU.mult, op1=ALU.mult))
                    run(gate_and_transpose(O_AZ, 0, post_a, slot=2), branch_accumulate(2, outsT[1], b_outsT[1], slots=(0, 1)))
                    run(branch_accumulate(0, outsT[0], b_outsT[0]))

                hal_l = t0 - 1 >= 0
                hal_r = t0 + 512 < S
                hbn = [b_hT[qb]] + ([b_hT[qb - 1]] if hal_l else []) + ([b_hT[qb + 1]] if hal_r else [])

                def b_chunk(c):
                    oT, boT = outsT[1], b_outsT[1]
                    w, bw = nring()
                    for i4 in range(4):
                        P.dma("sp", [b_wts], [bw] if i4 == 0 else [], bw, w[:, :, i4 * 128:(i4 + 1) * 128],
                              wcols(l, O_BB + i4 * 512 + c * 128, 128))
                    bw.w = {bw.sem_ld: bw.cnt_ld}
                    tcm, btcm = ntA()
                    P.op("pool", [], [btcm], lambda e: e.memset(tcm[:, 0:1], 0.0))
                    P.op("pool", [], [btcm], lambda e: e.memset(tcm[:, 513:514], 0.0))
                    bk, bbk = nbank()
                    mm_fm(bk[:, :], bbk, w, bw, 128, 128, t0, 512, hb)
                    P.op("act", [bbk], [btcm], lambda e: e.copy(out=tcm[:, 1:513], in_=bk[:, :]))
                    bkh, bbkh = nbank()
                    if hal_l:
                        mm_fm(bkh[:, 0:1], bbkh, w, bw, 128, 128, t0 - 1, 1, hbn)
                        P.op("act", [bbkh], [btcm], lambda e: e.copy(out=tcm[:, 0:1], in_=bkh[:, 0:1]))
                    if hal_r:
                        mm_fm(bkh[:, 1:2], bbkh, w, bw, 128, 128, t0 + 512, 1, hbn)
                        P.op("act", [bbkh], [btcm], lambda e: e.copy(out=tcm[:, 513:514], in_=bkh[:, 1:2]))
                    bk, bbk = nbank()
                    mm_fm(bk[:, :], bbk, w, bw, 256, 128, t0, 512, hb)
                    P.op("dve", [bbk, btcm], [btcm], lambda e: e.tensor_tensor(out=tcm[:, 1:513], in0=tcm[:, 1:513], in1=bk[:, :], op=ALU.mult))
                    bkh, bbkh = nbank()
                    if hal_l:
                        mm_fm(bkh[:, 0:1], bbkh, w, bw, 256, 128, t0 - 1, 1, hbn)
                        P.op("dve", [bbkh, btcm], [btcm], lambda e: e.tensor_tensor(out=tcm[:, 0:1], in0=tcm[:, 0:1], in1=bkh[:, 0:1], op=ALU.mult))
                    if hal_r:
                        mm_fm(bkh[:, 1:2], bbkh, w, bw, 256, 128, t0 + 512, 1, hbn)
                        P.op("dve", [bbkh, btcm], [btcm], lambda e: e.tensor_tensor(out=tcm[:, 513:514], in0=tcm[:, 513:514], in1=bkh[:, 1:2], op=ALU.mult))
                    w0 = prm[:, PRM_SC + c * 3 + 0:PRM_SC + c * 3 + 1]
                    w1 = prm[:, PRM_SC + c * 3 + 1:PRM_SC + c * 3 + 2]
                    w2 = prm[:, PRM_SC + c * 3 + 2:PRM_SC + c * 3 + 3]
                    P.op("dve", [btcm, b_prm], [b_uconv], lambda e: e.tensor_scalar(out=uconv[:, c, :], in0=tcm[:, 0:512], scalar1=w0, scalar2=None, op0=ALU.mult))
                    P.op("dve", [btcm, b_prm, b_uconv], [b_uconv], lambda e: e.scalar_tensor_tensor(
                        out=uconv[:, c, :], in0=tcm[:, 1:513], scalar=w1, in1=uconv[:, c, :], op0=ALU.mult, op1=ALU.add))
                    P.op("dve", [btcm, b_prm, b_uconv], [b_uconv], lambda e: e.scalar_tensor_tensor(
                        out=uconv[:, c, :], in0=tcm[:, 2:514], scalar=w2, in1=uconv[:, c, :], op0=ALU.mult, op1=ALU.add))
                    bk, bbk = nbank()
                    mm_fm(bk[:, :], bbk, w, bw, 0, 128, t0, 512, hb)
                    P.op("dve", [bbk, b_uconv], [b_uconv], lambda e: e.tensor_tensor(out=uconv[:, c, :], in0=uconv[:, c, :], in1=bk[:, :], op=ALU.mult))
                    bk, bbk = nbank()
                    mm_fm(bk[:, :], bbk, w, bw, 384, 128, t0, 512, hb)
                    zz, bzz = ntA()
                    silu2(zz[:, 0:512], bzz, bk[:, :], bbk)
                    P.op("dve", [bzz, b_uconv], [boT], lambda e: e.scalar_tensor_tensor(
                        out=oT[:, c, :], in0=uconv[:, c, :], scalar=0.5, in1=zz[:, 0:512], op0=ALU.mult, op1=ALU.mult))

                if "B" not in skip and "D" in skip:
                    for c in range(4):
                        b_chunk(c)
                    run(branch_accumulate(1, outsT[1], b_outsT[1]))

                if "D" not in skip:
                    load_rope(t0, "D")
                    wcq, bwcq = ring_load(wcols(l, O_CQ, 256), 256)
                    cqf = []
                    for c in range(2):
                        bk, bbk = nbank()
                        mm_fm(bk[:, :], bbk, wcq, bwcq, c * 128, 128, t0, 512, hb)
                        cf, bcf = ntA()
                        P.op("act", [bbk], [bcf], lambda e: e.copy(out=cf[:, 0:512], in_=bk[:, :]))
                        sq_, bsq = ntA()
                        P.op("act", [bcf], [bsq], lambda e: e.activation(out=sq_[:, 0:512], in_=cf[:, 0:512], func=AF.Square))
                        cqf.append((cf, bcf, sq_, bsq))
                    bk, bbk = nbank()

                    def emit_q(e, bk=bk, cqf=cqf):
                        ins = None
                        for c in range(2):
                            ins = e.matmul(bk[:, :], lhsT=onesf[:], rhs=cqf[c][2][:, 0:512], start=(c == 0), stop=(c == 1))
                        return ins
                    P.op("pe", [cqf[0][3], cqf[1][3], bf("onesf")], [bbk], emit_q)
                    rs_, brs = ntA()
                    P.op("act", [bbk], [brs], lambda e: e.activation(out=rs_[:, 0:512], in_=bk[:, :], func=AF.Sqrt, bias=float(EPS), scale=1.0 / 256))
                    P.op("dve", [brs], [brs], lambda e: e.reciprocal(out=rs_[:, 0:512], in_=rs_[:, 0:512]))
                    cqn, bcqn = ntB()
                    for c in range(2):
                        P.op("dve", [cqf[c][1], brs, b_prm], [bcqn], lambda e: e.scalar_tensor_tensor(
                            out=cqn[:, c * 512:(c + 1) * 512], in0=cqf[c][0][:, 0:512], scalar=prm[:, PRM_QN + c:PRM_QN + c + 1],
                            in1=rs_[:, 0:512], op0=ALU.mult, op1=ALU.mult))
                    for h in range(4):
                        bk1, bbk1 = nbank()
                        bk2, bbk2 = nbank()
                        for (bk_, bbk_, wo_) in ((bk1, bbk1, WUQ), (bk2, bbk2, WUQS)):
                            def emit_uq(e, bk_=bk_, wo_=wo_, h=h):
                                ins = None
                                for c in range(2):
                                    ins = e.matmul(bk_[0:96, :], lhsT=wsm[:, wo_ + c * 384 + h * 96:wo_ + c * 384 + (h + 1) * 96],
                                                   rhs=cqn[:, c * 512:(c + 1) * 512], start=(c == 0), stop=(c == 1))
                                return ins
                            P.op("pe", [bcqn, b_wsm], [bbk_], emit_uq)
                        P.op("act", [bbk1, b_big], [b_qT], lambda e: e.copy(out=qT[0:64, h, :], in_=bk1[0:64, :]))
                        ta, bta = ntA()
                        tb2, btb2 = ntA()
                        P.op("dve", [bbk1, b_rope], [bta], lambda e: e.tensor_tensor(out=ta[64:96, 0:512], in0=bk1[64:96, :], in1=rope[64:96, 2, :], op=ALU.mult))
                        P.op("dve", [bbk2, b_rope], [btb2], lambda e: e.tensor_tensor(out=tb2[64:96, 0:512], in0=bk2[64:96, :], in1=rope[64:96, 3, :], op=ALU.mult))
                        P.op("dve", [bta, btb2, b_big], [b_qT], lambda e: e.tensor_tensor(out=qT[64:96, h, :], in0=ta[64:96, 0:512], in1=tb2[64:96, 0:512], op=ALU.add))

                    def evac_plain(tag, accs, baccs):
                        h = tag
                        for j in range(4):
                            c0 = (j % 2) * 129
                            Aj = accs[j // 2]
                            P.op("dve", [baccs[j // 2]], [b_st], lambda e: e.reciprocal(out=st[:, 8:9], in_=Aj[:, c0 + 128:c0 + 129]))
                            P.op("dve", [baccs[j // 2], b_st], [b_otok], lambda e: e.tensor_scalar(
                                out=otok[:, j, h * 128:(h + 1) * 128], in0=Aj[:, c0:c0 + 128], scalar1=st[:, 8:9], scalar2=None, op0=ALU.mult))

                    heads = []
                    for h in range(4):
                        heads.append((qT[0:96, h, :],
                                      (lambda kc, h=h: kdT[0:96, h, kc * 128:(kc + 1) * 128]),
                                      (lambda kc, h=h: vd1[:, kc, h, 0:129]),
                                      h % 2, h))
                    attention(heads, 16, 96 ** -0.5, b_qT, list(b_kdT), list(b_vd1), evac_plain,
                              pre_head=(lambda hi_: b_chunk(hi_) if "B" not in skip else None))
                    run(gate_and_transpose(O_DZ, 0, slot=2), branch_accumulate(1, outsT[1], b_outsT[1], slots=(0, 1)))

                if "E" not in skip:
                    weq, bweq = ring_load(wcols(l, O_EQ, 512), 512)
                    for h in range(4):
                        bk, bbk = nbank()
                        mm_fm(bk[:, :], bbk, weq, bweq, h * 128, 128, t0, 512, hb)
                        P.op("act", [bbk, b_big], [b_qT], lambda e: e.copy(out=qT[:, h, :], in_=bk[:, :]))
                    heads = []
                    for h in range(4):
                        heads.append((qT[:, h, :],
                                      (lambda kc, h=h: kmT[:, h, kc * 128:(kc + 1) * 128]),
                                      (lambda kc, h=h: vm1[:, kc, h, 0:129]),
                                      h % 2, h))
                    attention(heads, 2, 128 ** -0.5, b_qT, [b_kmT], [b_vm1], evac_plain)
                    run(gate_and_transpose(O_EZ, 1, slot=2), branch_accumulate(3, outsT[0], b_outsT[0], slots=(0, 1)))
                    run(branch_accumulate(4, outsT[1], b_outsT[1]))

                for cc in range(8):
                    P.op("act", [b_yacc], [b_ymT], lambda e: e.activation(out=ymT[:, cc, :], in_=yacc[:, cc, :], func=AF.Copy, scale=0.5))
                load_gvec(norm_post[l:l + 1, :])
                yo = yacc[:, :, :].rearrange("p a b -> p (a b)").rearrange("p (j d) -> p j d", j=4)
                for half in range(2):
                    wo_, bwo = ring_load(wo_b[l, :, half * 512:(half + 1) * 512].rearrange("(kc p) n -> p kc n", p=128), 512)
                    for j in range(4):
                        bk, bbk = nbank()

                        def emit_o(e, bk=bk, j=j, wo_=wo_):
                            ins = None
                            for kc in range(8):
                                ins = e.matmul(bk[:, :], lhsT=ymT[:, kc, j * 128:(j + 1) * 128], rhs=wo_[:, kc, 0:512],
                                               start=(kc == 0), stop=(kc == 7))
                            return ins
                        P.op("pe", [b_ymT, bwo], [bbk], emit_o)
                        P.op("act", [bbk, b_ymT], [b_yacc], lambda e: e.copy(out=yo[:, j, half * 512:(half + 1) * 512], in_=bk[:, :]))
                P.op("dve", [], [b_st], lambda e: e.memset(st[:, 16:20], 0.0))
                for j in range(4):
                    tb_, btb = ntB()
                    P.op("act", [b_yacc], [btb, b_st], lambda e: e.activation(out=tb_[:, :], in_=yo[:, j, :], func=AF.Square, accum_out=st[:, 16 + j:17 + j]))
                rsqrt_inplace(st[:, 16:20], b_st, 1.0 / D)
                for j in range(4):
                    P.op("dve", [b_yacc, b_st, b_gvec], [b_yacc], lambda e: e.scalar_tensor_tensor(
                        out=yo[:, j, :], in0=yo[:, j, :], scalar=st[:, 16 + j:17 + j], in1=gvec[:], op0=ALU.mult, op1=ALU.mult))
                ev = P.dma("pool", [b_yacc], [b_dst], b_yacc, x_dst[s, t0:t0 + 512, :].rearrange("(j p) d -> p j d", p=128), yo, accum=True)
                b_yacc.r[ev[0]] = ev[1]
                if x_dst is not xs:
                    out_evs.append(ev)

    P.final_wait("sp", out_evs)
    P.final_wait("pool", out_evs)
    P.emit_all()
    es.close()
    return nc, P


def _rope_tables():
    t = np.arange(S, dtype=np.float32)
    tab = np.zeros((4, 128, S), np.float32)
    tab[0] = 1.0
    tab[2] = 1.0
    inv_a = (np.float32(500000.0) ** (-(np.arange(0, 16, 2, dtype=np.float32) / np.float32(16)))).astype(np.float32)
    ang_a = (t[:, None] * inv_a[None, :]).astype(np.float32)
    ca, sa = np.cos(ang_a).astype(np.float32).T, np.sin(ang_a).astype(np.float32).T
    for base in (0, 64):
        tab[0, base:base + 8] = ca
        tab[0, base + 8:base + 16] = ca
        tab[1, base:base + 8] = -sa
        tab[1, base + 8:base + 16] = sa
    inv_d = (np.float32(10000.0) ** (-(np.arange(0, 32, 2, dtype=np.float32) / np.float32(32)))).astype(np.float32)
    ang_d = (t[:, None] * inv_d[None, :]).astype(np.float32)
    cd, sd = np.cos(ang_d).astype(np.float32).T, np.sin(ang_d).astype(np.float32).T
    tab[2, 64:80] = cd
    tab[2, 80:96] = cd
    tab[3, 64:80] = -sd
    tab[3, 80:96] = sd
    return tab


_CACHE = {}


def kernel(x_prompt, x_sample, mem_prompt, mem_sample, norm_pre, w_in, diff_lambda,
           diff_subln, sconv_w, conf_dw_w, conf_dw_b, conf_ln_g, conf_ln_b,
           mla_q_norm, mla_w_uq, mla_kv_norm, mla_w_ukv, mem_norm, mem_w_kv,
           w_branch, w_out, norm_post):
    f = lambda a: np.ascontiguousarray(np.asarray(a, dtype=np.float32))
    x_prompt, x_sample, mem_prompt, mem_sample = map(f, (x_prompt, x_sample, mem_prompt, mem_sample))
    if "nc" not in _CACHE:
        _CACHE["nc"] = build_program()[0]
    nc = _CACHE["nc"]
    shared = dict(norm_pre=f(norm_pre), w_in=f(w_in), diff_lambda=f(diff_lambda), diff_subln=f(diff_subln),
                  sconv_w=f(sconv_w), conf_dw_w=f(conf_dw_w), conf_dw_b=f(conf_dw_b), conf_ln_g=f(conf_ln_g),
                  conf_ln_b=f(conf_ln_b), mla_q_norm=f(mla_q_norm), mla_w_uq=f(mla_w_uq),
                  mla_kv_norm=f(mla_kv_norm), mla_w_ukv=f(mla_w_ukv), mem_norm=f(mem_norm),
                  mem_w_kv=f(mem_w_kv), w_branch=f(w_branch), w_out=f(w_out), norm_post=f(norm_post),
                  c_ident=np.eye(128, dtype=np.float32), c_rope=_rope_tables())
    in_maps = []
    for c in range(NCORES):
        xa = np.concatenate([x_prompt[2 * c:2 * c + 2], x_sample[4 * c:4 * c + 4]], axis=0)
        ma = np.concatenate([mem_prompt[2 * c:2 * c + 2], mem_sample[4 * c:4 * c + 4]], axis=0)
        d = dict(shared)
        d["x_all"] = np.ascontiguousarray(xa)
        d["mem_all"] = np.ascontiguousarray(ma)
        in_maps.append(d)
    res = run_bass_kernel_spmd(nc, in_maps, core_ids=list(range(NCORES)))
    yp = np.empty_like(x_prompt)
    ysm = np.empty_like(x_sample)
    for c in range(NCORES):
        ya = res.results[c]["y_all"]
        yp[2 * c:2 * c + 2] = ya[0:2]
        ysm[4 * c:4 * c + 4] = ya[2:6]
    return (yp, ysm)
```

```python
import math
from contextlib import ExitStack

import numpy as np
import concourse.bass as bass
import concourse.mybir as mybir
from concourse.bass_utils import run_bass_kernel_spmd

F32 = mybir.dt.float32
BF16 = mybir.dt.bfloat16
AF = mybir.ActivationFunctionType
ALU = mybir.AluOpType
AX = mybir.AxisListType

D = 1024
S = 2048
DEPTH = 2
NCORES = 8
NSEQ = 6
NMEM = 256
INC = 12704
EPS = 1e-6
O_AQ, O_AK, O_AV, O_AZ = 0, 512, 1024, 1536
O_BB, O_BC, O_BX, O_BZ = 2048, 2560, 3072, 3584
O_GA, O_GB, O_CZ = 4096, 4608, 5120
O_CQ, O_CKV, O_KR, O_DZ = 5632, 5888, 6016, 6048
O_EQ, O_EZ, O_G = 6560, 7072, 7584
VW = 130


class Buf:
    __slots__ = ("name", "w", "r", "sem_ld", "cnt_ld")

    def __init__(self, name):
        self.name = name
        self.w = {}
        self.r = {}
        self.sem_ld = None
        self.cnt_ld = 0


class Rec:
    def __init__(self):
        self.calls = []

    def __getattr__(self, name):
        def f(*a, **k):
            self.calls.append((name, a, k))
            return self
        return f


def _replay(e, calls):
    ins = None
    for name, a, k in calls:
        ins = getattr(e, name)(*a, **k)
    return ins


class Prog:
    ENG = ("pe", "act", "dve", "pool", "sp")

    def __init__(self, nc, es):
        self.nc, self.es = nc, es
        self.q = {k: [] for k in self.ENG}
        self.sems = []
        self.esem = {}
        self.cnt = {}
        for k in ("pe", "act", "dve", "pool"):
            self.esem[k] = self._newsem("e_" + k)
            self.cnt[k] = 0
        self.known = {k: {} for k in self.ENG}
        self.nbuf = 0
        self.nwait = 0

    def _newsem(self, name):
        h = self.es.enter_context(self.nc.semaphore(name))
        self.sems.append(h)
        return len(self.sems) - 1

    def buf(self, name=None):
        self.nbuf += 1
        return Buf(name or f"b{self.nbuf}")

    def _waits(self, eng, reads, writes):
        need = {}
        for b in reads:
            for s, v in b.w.items():
                if need.get(s, 0) < v:
                    need[s] = v
        for b in writes:
            for d in (b.w, b.r):
                for s, v in d.items():
                    if need.get(s, 0) < v:
                        need[s] = v
        out = []
        kn = self.known[eng]
        for s, v in need.items():
            if kn.get(s, 0) < v:
                kn[s] = v
                out.append((s, v))
        self.nwait += len(out)
        return out

    def _commit(self, ev, reads, writes):
        s, v = ev
        for b in writes:
            b.w = {s: v}
            b.r = {}
        for b in reads:
            if b.r.get(s, 0) < v:
                b.r[s] = v

    def op(self, eng, reads, writes, emit):
        waits = self._waits(eng, reads, writes)
        self.cnt[eng] += 1
        si = self.esem[eng]
        sems = self.sems

        rec = Rec()
        emit(rec)
        calls = rec.calls
        assert calls

        def run(e, waits=waits, calls=calls, si=si):
            for s, v in waits:
                e.wait_ge(sems[s], v)
            _replay(e, calls).then_inc(sems[si], 1)

        self.q[eng].append(run)
        self._commit((si, self.cnt[eng]), reads, writes)

    def dma(self, q, reads, writes, sembuf, out_ap, in_ap, slow=False, accum=False):
        waits = self._waits(q, reads, writes)
        if sembuf.sem_ld is None:
            sembuf.sem_ld = self._newsem("d_" + sembuf.name)
        sembuf.cnt_ld += 16
        si = sembuf.sem_ld
        sems = self.sems

        def run(e, waits=waits, si=si):
            for s, v in waits:
                e.wait_ge(sems[s], v)
            if accum:
                e.dma_start(out=out_ap, in_=in_ap, accum_op=ALU.add).then_inc(sems[si], 16)
            elif slow:
                e.dma_start(out=out_ap, in_=in_ap, allow_slow_non_contiguous=True).then_inc(sems[si], 16)
            else:
                e.dma_start(out=out_ap, in_=in_ap).then_inc(sems[si], 16)

        self.q[q].append(run)
        ev = (si, sembuf.cnt_ld)
        self._commit(ev, reads, writes)
        return ev

    def final_wait(self, q, evs):
        sems = self.sems

        def run(e):
            for s, v in evs:
                e.wait_ge(sems[s], v)

        self.q[q].append(run)

    def emit_all(self):
        block = self.es.enter_context(self.nc.Block())
        qs = self.q

        @block.sync
        def _(e):
            for f in qs["sp"]:
                f(e)

        @block.scalar
        def _(e):
            for f in qs["act"]:
                f(e)

        @block.vector
        def _(e):
            for f in qs["dve"]:
                f(e)

        @block.gpsimd
        def _(e):
            for f in qs["pool"]:
                f(e)

        @block.tensor
        def _(e):
            for f in qs["pe"]:
                f(e)


def build_program(nseq=NSEQ, depth=DEPTH, debug=False, skip=()):
    nc = bass.Bass("TRN2", target_bir_lowering=False)
    es = ExitStack()
    P = Prog(nc, es)

    def din(name, shape, dt=F32):
        return nc.dram_tensor(name, list(shape), dt, kind="ExternalInput").ap()

    x_in = din("x_all", [nseq, S, D])
    mem_in = din("mem_all", [nseq, NMEM, D])
    norm_pre = din("norm_pre", [DEPTH, D])
    w_in = din("w_in", [DEPTH, D, INC])
    diff_lambda = din("diff_lambda", [DEPTH, 4, 64])
    diff_subln = din("diff_subln", [DEPTH, 128])
    sconv_w = din("sconv_w", [DEPTH, 3, 512])
    conf_dw_w = din("conf_dw_w", [DEPTH, 31, 512])
    conf_dw_b = din("conf_dw_b", [DEPTH, 512])
    conf_ln_g = din("conf_ln_g", [DEPTH, 512])
    conf_ln_b = din("conf_ln_b", [DEPTH, 512])
    mla_q_norm = din("mla_q_norm", [DEPTH, 256])
    mla_w_uq = din("mla_w_uq", [DEPTH, 256, 384])
    mla_kv_norm = din("mla_kv_norm", [DEPTH, 128])
    mla_w_ukv = din("mla_w_ukv", [DEPTH, 128, 768])
    mem_norm = din("mem_norm", [DEPTH, D])
    mem_w_kv = din("mem_w_kv", [DEPTH, D, D])
    w_branch = din("w_branch", [DEPTH, 5, 512, D])
    w_out = din("w_out", [DEPTH, D, D])
    norm_post = din("norm_post", [DEPTH, D])
    c_ident = din("c_ident", [128, 128])
    c_rope = din("c_rope", [4, 128, S])

    y_out = nc.dram_tensor("y_all", [nseq, S, D], F32, kind="ExternalOutput").ap()
    dbg = nc.dram_tensor("dbg", [5, 128, 4, 512], BF16, kind="ExternalOutput").ap() if debug else None

    def dscr(name, shape, dt):
        return nc.dram_tensor(name, list(shape), dt, kind="Internal").ap()

    xs = dscr("xs", [nseq, S, D], F32)
    wi_b = dscr("wi_b", [DEPTH, D, INC], BF16)
    wsw_b = dscr("wsw_b", [DEPTH, D, 1056], BF16)
    wuq_b = dscr("wuq_b", [DEPTH, 256, 384], BF16)
    wuqs_b = dscr("wuqs_b", [DEPTH, 256, 384], BF16)
    wukv_b = dscr("wukv_b", [DEPTH, 128, 768], BF16)
    wkv_b = dscr("wkv_b", [DEPTH, D, D], BF16)
    wb_b = dscr("wb_b", [DEPTH, 5, 512, D], BF16)
    wo_b = dscr("wo_b", [DEPTH, D, D], BF16)

    def sb(name, shape, dt):
        return es.enter_context(nc.sbuf_tensor(name, list(shape), dt))

    def ps(name, shape, dt):
        return es.enter_context(nc.psum_tensor(name, list(shape), dt))

    hT = sb("hT", [128, 8, S], BF16)
    kTa = sb("kTa", [128, 4, S], BF16)
    v1a = sb("v1a", [128, 16, 4, VW], BF16)
    kdT = sb("kdT", [128, 4, S], BF16)
    vd1 = sb("vd1", [128, 16, 4, VW], BF16)
    kmT = sb("kmT", [128, 4, NMEM], BF16)
    vm1 = sb("vm1", [128, 2, 4, VW], BF16)
    ident = sb("ident", [128, 128], BF16)
    onesf = sb("onesf", [128, 128], F32)
    subln = sb("subln", [128, 128], F32)
    prm_all = sb("prm", [128, DEPTH, 32], F32)
    cw31_all = sb("cw31", [128, DEPTH, 4, 31], F32)
    lam_t = sb("lam_t", [128, 8], F32)
    wring = [sb(f"wr{i}", [128, 8, 512], BF16) for i in range(3)]
    wsm = sb("wsm", [128, 2304], BF16)
    wkr = sb("wkr", [128, 2, 8, 96], BF16)
    rope = sb("rope", [128, 4, 512], F32)
    gvec = rope[:, 2:4, :].rearrange("p a t -> p (a t)")
    dlam = rope[:, 3, 0:256].rearrange("p (a d) -> p a d", a=4)
    big = sb("big", [128, 4096], F32)
    outsT = [sb(f"outsT{i}", [128, 4, 512], BF16) for i in range(2)]
    yacc = sb("yacc", [128, 8, 512], F32)
    identf = yacc[:, 0, 0:128]
    ymT = sb("ymT", [128, 8, 512], BF16)
    tmpA = [sb(f"tmpA{i}", [128, 544], F32) for i in range(6)]
    tmpB = [sb(f"tmpB{i}", [128, 1024], BF16) for i in range(2)]
    st = sb("st", [128, 64], F32)
    zer = sb("zer", [128, 128], BF16)
    xq = big[:, :].rearrange("p (j d) -> p j d", j=4)
    big_bf = big[:, :].bitcast(BF16)
    qT = big_bf[:, 0:2048].rearrange("p (h t) -> p h t", h=4)
    pTs = [big_bf[:, 2048 + i * 512: 2048 + (i + 1) * 512] for i in range(3)]
    uconv = ymT[:, :, :].rearrange("p a b -> p (a b)").bitcast(F32).rearrange("p (c t) -> p c t", c=4)
    otok = big[:, 2048:4096].rearrange("p (j d) -> p j d", j=4)

    pbank = [ps(f"pb{i}", [128, 512], F32) for i in range(3)]
    pacc = [ps(f"pa{i}", [128, 512], F32) for i in range(4)]
    ptr = ps("ptr", [128, 1024], BF16)

    B = {}

    def bf(name):
        if name not in B:
            B[name] = P.buf(name)
        return B[name]

    b_hT = [bf(f"hT{i}") for i in range(4)]
    b_kTa = [bf(f"kTa{i}") for i in range(4)]
    b_v1a = [bf(f"v1a{i}") for i in range(4)]
    b_kdT = [bf(f"kdT{i}") for i in range(4)]
    b_vd1 = [bf(f"vd1{i}") for i in range(4)]
    b_pbank = [bf(f"pb{i}") for i in range(3)]
    b_pacc = [bf(f"pa{i}") for i in range(4)]
    b_wring = [bf(f"wr{i}") for i in range(3)]
    b_tmpA = [bf(f"tmpA{i}") for i in range(6)]
    b_tmpB = [bf(f"tmpB{i}") for i in range(2)]
    b_outsT = [bf(f"outsT{i}") for i in range(2)]
    b_pT = [bf(f"pT{i}") for i in range(3)]
    rr = {"bank": 0, "ring": 0, "pT": 0, "tA": 0, "tB": 0, "nb": 7}
    allbanks = pbank + pacc
    b_allbanks = b_pbank + b_pacc

    def nbank():
        i = rr["bank"] % rr["nb"]
        rr["bank"] = (i + 1) % rr["nb"]
        return allbanks[i], b_allbanks[i]

    def nring():
        i = rr["ring"]
        rr["ring"] = (i + 1) % 3
        return wring[i], b_wring[i]

    def ntA():
        i = rr["tA"]
        rr["tA"] = (i + 1) % 6
        return tmpA[i], b_tmpA[i]

    def ntB():
        i = rr["tB"]
        rr["tB"] = (i + 1) % 2
        return tmpB[i], b_tmpB[i]

    PRM_SC = 0
    PRM_CB = 12
    PRM_LG = 16
    PRM_LB = 20
    PRM_QN = 24
    PRM_KN = 26

    b_const = bf("const")
    P.dma("sp", [], [b_const], b_const, identf[:], c_ident[:, :])
    P.op("dve", [b_const], [bf("ident")], lambda e: e.tensor_copy(out=ident[:], in_=identf[:]))
    P.op("pool", [], [bf("onesf")], lambda e: e.memset(onesf[:], 1.0))
    P.op("pool", [], [bf("zer")], lambda e: e.memset(zer[:], 0.0))
    P.op("pool", [], [b_v1a[0], b_v1a[1], b_v1a[2], b_v1a[3]], lambda e: e.memset(v1a[:], 1.0))
    P.op("pool", [], [b_vd1[0], b_vd1[1], b_vd1[2], b_vd1[3]], lambda e: e.memset(vd1[:], 1.0))
    P.op("pool", [], [bf("vm1")], lambda e: e.memset(vm1[:], 1.0))
    P.op("pool", [], [bf("wkr")], lambda e: e.memset(wkr[:], 0.0))

    b_prm = bf("prm")
    for l in range(depth):
        for c in range(4):
            P.dma("sp", [], [b_prm], b_prm, prm_all[:, l, PRM_SC + c * 3:PRM_SC + c * 3 + 3],
                  sconv_w[l, :, c * 128:(c + 1) * 128].rearrange("k p -> p k"), slow=True)
            P.dma("sp", [], [b_prm], b_prm, cw31_all[:, l, c, :],
                  conf_dw_w[l, :, c * 128:(c + 1) * 128].rearrange("k p -> p k"), slow=True)
        for (off, src, n) in ((PRM_CB, conf_dw_b, 4), (PRM_LG, conf_ln_g, 4), (PRM_LB, conf_ln_b, 4),
                              (PRM_QN, mla_q_norm, 2), (PRM_KN, mla_kv_norm, 1)):
            P.dma("sp", [], [b_prm], b_prm, prm_all[:, l, off:off + n],
                  src[l, :].rearrange("(c p) -> p c", p=128), slow=True)
    P.op("dve", [b_prm], [b_prm], lambda e: e.tensor_scalar(
        out=cw31_all[:, :, :, :].rearrange("p l c k -> p (l c k)"), in0=cw31_all[:, :, :, :].rearrange("p l c k -> p (l c k)"),
        scalar1=0.5, scalar2=None, op0=ALU.mult))
    b_wprep = bf("wprep")
    b_wprep2 = bf("wprep2")

    thr = P.buf("thr")

    def castdma(out_ap, in_ap, second=False):
        if second:
            if b_wprep2.sem_ld is not None and b_wprep2.cnt_ld > 0:
                thr.w = {b_wprep2.sem_ld: b_wprep2.cnt_ld}
            else:
                thr.w = {}
            P.dma("pool", [b_wprep, thr], [], b_wprep2, out_ap, in_ap)
        else:
            if b_wprep.sem_ld is not None and b_wprep.cnt_ld >= 16 * 6:
                thr.w = {b_wprep.sem_ld: b_wprep.cnt_ld - 16 * 5}
            else:
                thr.w = {}
            P.dma("pool", [thr], [], b_wprep, out_ap, in_ap)

    for l in range(depth):
        for r in range(8):
            castdma(wi_b[l, r * 128:(r + 1) * 128, :], w_in[l, r * 128:(r + 1) * 128, :])
            castdma(wkv_b[l, r * 128:(r + 1) * 128, :], mem_w_kv[l, r * 128:(r + 1) * 128, :])
            castdma(wo_b[l, r * 128:(r + 1) * 128, :], w_out[l, r * 128:(r + 1) * 128, :])
        for i in range(5):
            for r in range(4):
                castdma(wb_b[l, i, r * 128:(r + 1) * 128, :], w_branch[l, i, r * 128:(r + 1) * 128, :])
        castdma(wuq_b[l, :, :], mla_w_uq[l, :, :])
        castdma(wuqs_b[l, :, :], mla_w_uq[l, :, :])
        castdma(wukv_b[l, :, :], mla_w_ukv[l, :, :])
        castdma(wsw_b[l, :, 0:1024], w_in[l, :, 0:1024])
        castdma(wsw_b[l, :, 1024:1056], w_in[l, :, O_KR:O_KR + 32])
    b_wprep.w = {b_wprep.sem_ld: b_wprep.cnt_ld}
    for l in range(depth):
        for rh in range(4):
            rs = slice(rh * 256, (rh + 1) * 256)
            dst = wsw_b[l, rs, 0:1024].rearrange("r (s d) -> r s d", d=64)
            src = w_in[l, rs, 0:1024].rearrange("r (s d) -> r s d", d=64)
            castdma(dst[:, :, 0:8], src[:, :, 8:16], True)
            castdma(dst[:, :, 8:16], src[:, :, 0:8], True)
        castdma(wsw_b[l, :, 1024:1040], w_in[l, :, O_KR + 16:O_KR + 32], True)
        castdma(wsw_b[l, :, 1040:1056], w_in[l, :, O_KR:O_KR + 16], True)
        dq = wuqs_b[l, :, :].rearrange("r (h d) -> r h d", d=96)
        sq = mla_w_uq[l, :, :].rearrange("r (h d) -> r h d", d=96)
        castdma(dq[:, :, 64:80], sq[:, :, 80:96], True)
        castdma(dq[:, :, 80:96], sq[:, :, 64:80], True)
    b_wprep2.w = {b_wprep2.sem_ld: b_wprep2.cnt_ld}
    b_wts = bf("wts")
    P.op("pool", [b_wprep, b_wprep2], [b_wts], lambda e: e.memset(st[:, 61:62], 0.0))

    def load_w(dst_ap, src_ap, b_dst, extra_reads=()):
        P.dma("sp", [b_wts] + list(extra_reads), [b_dst], b_dst, dst_ap, src_ap)

    def wcols(l, c0, n):
        return wi_b[l, :, c0:c0 + n].rearrange("(kc p) n -> p kc n", p=128)

    def ring_load(src_ap, n, kcs=8, slot=None):
        w, bw = (wring[slot], b_wring[slot]) if slot is not None else nring()
        load_w(w[:, 0:kcs, 0:n], src_ap, bw)
        return w, bw

    def run(*gens):
        gens = list(gens)
        while gens:
            for g in list(gens):
                try:
                    next(g)
                except StopIteration:
                    gens.remove(g)

    def mm_fm(out_ap, b_out, w, bw, c0, ncols, tok_lo, ntok, hbufs, kcs=8, rhs_fn=None):
        def emit(e):
            ins = None
            for kc in range(kcs):
                rhs = rhs_fn(kc) if rhs_fn else hT[:, kc, tok_lo:tok_lo + ntok]
                ins = e.matmul(out_ap, lhsT=w[:, kc, c0:c0 + ncols], rhs=rhs,
                               start=(kc == 0), stop=(kc == kcs - 1))
            return ins
        P.op("pe", [bw] + list(hbufs), [b_out], emit)

    def mm_tm(out_ap, b_out, w, bw, c0, ncols, tok_lo, hbufs):
        def emit(e):
            ins = None
            for kc in range(8):
                ins = e.matmul(out_ap, lhsT=hT[:, kc, tok_lo:tok_lo + 128], rhs=w[:, kc, c0:c0 + ncols],
                               start=(kc == 0), stop=(kc == 7))
            return ins
        P.op("pe", [bw] + list(hbufs), [b_out], emit)

    def rsqrt_inplace(ap, b, scale, n_eps=EPS):
        P.op("act", [b], [b], lambda e: e.activation(out=ap, in_=ap, func=AF.Sqrt, bias=float(n_eps), scale=float(scale)))
        P.op("dve", [b], [b], lambda e: e.reciprocal(out=ap, in_=ap))

    def silu2(out_ap, b_out, z_ap, b_z):
        t, bt = ntA()
        n = z_ap.shape[-1] if len(z_ap.shape) == 2 else None
        tv = t[0:z_ap.shape[0], 0:z_ap.shape[1]]
        P.op("act", [b_z], [bt], lambda e: e.activation(out=tv, in_=z_ap, func=AF.Tanh, scale=0.5))
        P.op("dve", [b_z, bt], [b_out], lambda e: e.scalar_tensor_tensor(
            out=out_ap, in0=tv, scalar=1.0, in1=z_ap, op0=ALU.add, op1=ALU.mult))

    def transpose_to(dst_fn, b_dst, src, b_src, nchunks):
        b_ptr = bf("ptr")

        def emit(e):
            ins = None
            for kc in range(nchunks):
                ins = e.transpose(out=ptr[:, kc * 128:(kc + 1) * 128], in_=src[:, kc * 128:(kc + 1) * 128],
                                  identity=ident[:])
            return ins
        P.op("pe", [b_src, bf("ident")], [b_ptr], emit)
        P.op("act", [b_ptr], [b_dst], lambda e: e.copy(
            out=dst_fn(), in_=ptr[:, 0:nchunks * 128].rearrange("p (k t) -> p k t", t=128)))

    def attention(heads, nkc, scale, b_q, b_k, b_v, evac, pre_head=None):
        rr["nb"] = 3
        rr["bank"] = 0
        for hidx, (q_ap, k_fn, v_fn, aset, tag) in enumerate(heads):
            if pre_head is not None:
                pre_head(hidx)
            accs = (pacc[2 * aset], pacc[2 * aset + 1])
            baccs = [b_pacc[2 * aset], b_pacc[2 * aset + 1]]
            steps = []

            def qk(kc):
                bank, bbank = nbank()
                P.op("pe", [b_q] + b_k, [bbank], lambda e: e.matmul(bank[:, :], lhsT=k_fn(kc), rhs=q_ap,
                                                                       start=True, stop=True))
                i = rr["pT"]
                rr["pT"] = (i + 1) % 3
                P.op("act", [bbank], [b_pT[i]], lambda e: e.activation(out=pTs[i], in_=bank[:, :], func=AF.Exp,
                                                                         scale=float(scale)))
                return i

            def zero_acc():
                def emit(e):
                    ins = None
                    for a in accs:
                        ins = e.matmul(a[:, 0:258], lhsT=zer[:, 0:128], rhs=hT[:, 0, 0:258], start=True, stop=False)
                    return ins
                P.op("pe", [bf("zer"), b_hT[0]], baccs, emit)

            def pv(kc, i):
                def emit(e):
                    ins = None
                    for j in range(4):
                        ins = e.matmul(accs[j // 2][:, (j % 2) * 129:(j % 2) * 129 + 129],
                                       lhsT=pTs[i][:, j * 128:(j + 1) * 128], rhs=v_fn(kc),
                                       start=False, stop=(kc == nkc - 1 and j % 2 == 1))
                    return ins
                P.op("pe", [b_pT[i]] + b_v, baccs, emit)

            zero_acc()
            pend = [qk(0)]
            if nkc > 1:
                pend.append(qk(1))
            for kc in range(nkc):
                if kc + 2 < nkc:
                    pend.append(qk(kc + 2))
                pv(kc, pend.pop(0))
            evac(tag, accs, baccs)
        rr["nb"] = 7

    out_evs = []
    b_big = bf("big")
    b_rope = bf("ropeD")
    b_ropeA = bf("ropeA")
    b_gvec = b_rope
    b_st = bf("st")
    b_otok = bf("otok")
    b_qT = bf("qT")
    b_yacc = bf("yacc")
    b_ymT = bf("ymT")
    b_uconv = b_ymT
    b_wsm = bf("wsm")
    b_wkr = bf("wkr")
    b_lam = bf("lam")
    b_subln = bf("subln")
    b_kmT = bf("kmT")
    b_vm1 = bf("vm1")
    b_xs = [bf(f"xs{s}") for s in range(nseq)]
    b_yo = [bf(f"yo{s}") for s in range(nseq)]

    WUQ, WUQS, WUKV = 0, 768, 1536

    def load_gvec(src_row):
        P.dma("sp", [], [b_gvec], b_gvec, gvec[:], src_row.partition_broadcast(128))

    def load_rope(t0, which="AD"):
        if "A" in which:
            P.dma("sp", [], [b_ropeA], b_ropeA, rope[:, 0:2, :], c_rope[0:2, :, t0:t0 + 512].rearrange("i p t -> p i t"))
        if "D" in which:
            P.dma("sp", [], [b_rope], b_rope, rope[:, 2:4, :], c_rope[2:4, :, t0:t0 + 512].rearrange("i p t -> p i t"))

    def rms_rows(x3, bx, ntile, gsrc_loaded, dst_fn, b_dst):
        P.op("dve", [], [b_st], lambda e: e.memset(st[:, 0:ntile], 0.0))
        for j in range(ntile):
            tb_, btb = ntB()
            P.op("act", [bx], [btb, b_st], lambda e: e.activation(out=tb_[:, :], in_=x3[:, j, :], func=AF.Square,
                                                                   accum_out=st[:, j:j + 1]))
        rsqrt_inplace(st[:, 0:ntile], b_st, 1.0 / D)
        for j in range(ntile):
            tb_, btb = ntB()
            P.op("dve", [bx, b_st, b_gvec], [btb], lambda e: e.scalar_tensor_tensor(
                out=tb_[:, :], in0=x3[:, j, :], scalar=st[:, j:j + 1], in1=gvec[:], op0=ALU.mult, op1=ALU.mult))
            transpose_to(lambda: dst_fn(j), b_dst, tb_, btb, 8)

    for s in range(nseq):
        for l in range(depth):
            x_src = x_in if l == 0 else xs
            x_dst = xs if (l == 0 and depth > 1) else y_out
            lam_init = 0.8 - 0.6 * math.exp(-0.3 * l)
            prm = prm_all[:, l, :]
            cw31 = cw31_all[:, l, :, :]
            P.dma("sp", [], [b_subln], b_subln, subln[:, :], diff_subln[l:l + 1, :].partition_broadcast(128))
            P.op("dve", [b_subln], [b_subln], lambda e: e.tensor_scalar(
                out=subln[:], in0=subln[:], scalar1=float((1.0 - lam_init) * 0.5), scalar2=None, op0=ALU.mult))
            P.dma("sp", [], [b_lam, b_rope], b_lam, rope[:, 3, 0:256],
                  diff_lambda[l:l + 1, :, :].rearrange("o a d -> o (a d)").partition_broadcast(128))
            t0_, bt0 = ntA()
            P.op("dve", [b_lam, b_rope], [bt0], lambda e: e.tensor_tensor(out=t0_[:, 0:64], in0=dlam[:, 0, :], in1=dlam[:, 1, :], op=ALU.mult))
            P.op("dve", [b_lam, b_rope], [bt0], lambda e: e.tensor_tensor(out=t0_[:, 64:128], in0=dlam[:, 2, :], in1=dlam[:, 3, :], op=ALU.mult))
            P.op("dve", [bt0], [b_lam], lambda e: e.tensor_reduce(out=lam_t[:, 0:2], in_=t0_[:, 0:128].rearrange("p (a d) -> p a d", a=2), axis=AX.X, op=ALU.add))
            P.op("act", [b_lam], [b_lam], lambda e: e.activation(out=lam_t[:, 2:4], in_=lam_t[:, 0:2], func=AF.Exp))
            P.op("dve", [b_lam], [b_lam], lambda e: e.scalar_tensor_tensor(
                out=lam_t[:, 4:5], in0=lam_t[:, 3:4], scalar=float(-lam_init), in1=lam_t[:, 2:3], op0=ALU.add, op1=ALU.subtract))
            b_dst = b_xs[s] if x_dst is xs else b_yo[s]
            for half in range(2):
                rs_ = slice(half * 1024, (half + 1) * 1024)
                P.dma("sp", [b_xs[s]] if l > 0 else [], [b_dst] if half == 0 else [], b_dst, x_dst[s, rs_, :], x_src[s, rs_, :])
            b_dst.w = {b_dst.sem_ld: b_dst.cnt_ld}
            load_gvec(norm_pre[l:l + 1, :])
            for tb in range(4):
                t0 = tb * 512
                rd = [b_xs[s]] if l > 0 else []
                P.dma("sp", rd, [b_big], b_big, xq, x_src[s, t0:t0 + 512, :].rearrange("(j p) d -> p j d", p=128))
                rms_rows(xq, b_big, 4, None, lambda j, t0=t0: hT[:, :, t0 + j * 128:t0 + (j + 1) * 128], b_hT[tb])
            load_w(wsm[:, WUQ:WUQ + 768].rearrange("p (k n) -> p k n", k=2),
                   wuq_b[l, :, :].rearrange("(k p) n -> p k n", p=128), b_wsm)
            load_w(wsm[:, WUQS:WUQS + 768].rearrange("p (k n) -> p k n", k=2),
                   wuqs_b[l, :, :].rearrange("(k p) n -> p k n", p=128), b_wsm)
            load_w(wsm[:, WUKV:WUKV + 768], wukv_b[l, :, :], b_wsm)
            load_w(wkr[:, 0, :, 64:96], wi_b[l, :, O_KR:O_KR + 32].rearrange("(kc p) n -> p kc n", p=128), b_wkr)
            load_w(wkr[:, 1, :, 64:96], wsw_b[l, :, 1024:1056].rearrange("(kc p) n -> p kc n", p=128), b_wkr)

            for tb in range(4):
                t0 = tb * 512
                hb = [b_hT[tb]]
                load_rope(t0)
                wk, bwk = ring_load(wcols(l, O_AK, 512), 512)
                wks, bwks = ring_load(wsw_b[l, :, 512:1024].rearrange("(kc p) n -> p kc n", p=128), 512)
                for c in range(4):
                    bk1, bbk1 = nbank()
                    mm_fm(bk1[:, :], bbk1, wk, bwk, c * 128, 128, t0, 512, hb)
                    bk2, bbk2 = nbank()
                    mm_fm(bk2[:, :], bbk2, wks, bwks, c * 128, 128, t0, 512, hb)
                    ta, bta = ntA()
                    tb2, btb2 = ntA()
                    P.op("dve", [bbk1, b_ropeA], [bta], lambda e: e.tensor_tensor(out=ta[:, 0:512], in0=bk1[:, :], in1=rope[:, 0, :], op=ALU.mult))
                    P.op("dve", [bbk2, b_ropeA], [btb2], lambda e: e.tensor_tensor(out=tb2[:, 0:512], in0=bk2[:, :], in1=rope[:, 1, :], op=ALU.mult))
                    P.op("dve", [bta, btb2], [b_kTa[tb]], lambda e: e.tensor_tensor(out=kTa[:, c, t0:t0 + 512], in0=ta[:, 0:512], in1=tb2[:, 0:512], op=ALU.add))
                wv, bwv = ring_load(wcols(l, O_AV, 512), 512)
                for j in range(4):
                    bk, bbk = nbank()
                    mm_tm(bk[:, :], bbk, wv, bwv, 0, 512, t0 + j * 128, hb)
                    P.op("act", [bbk], [b_v1a[tb]], lambda e: e.copy(out=v1a[:, tb * 4 + j, :, 0:128],
                                                                      in_=bk[:, :].rearrange("p (h d) -> p h d", h=4)))
                wd, bwd = ring_load(wcols(l, O_CKV, 128), 128)
                bk, bbk = nbank()
                mm_fm(bk[:, :], bbk, wd, bwd, 0, 128, t0, 512, hb)
                ckf, bckf = ntA()
                P.op("act", [bbk], [bckf], lambda e: e.copy(out=ckf[:, 0:512], in_=bk[:, :]))
                sqf, bsqf = ntA()
                P.op("act", [bckf], [bsqf], lambda e: e.activation(out=sqf[:, 0:512], in_=ckf[:, 0:512], func=AF.Square))
                bk2, bbk2 = nbank()
                P.op("pe", [bsqf, bf("onesf")], [bbk2], lambda e: e.matmul(bk2[:, :], lhsT=onesf[:], rhs=sqf[:, 0:512], start=True, stop=True))
                rs_, brs = ntA()
                P.op("act", [bbk2], [brs], lambda e: e.activation(out=rs_[:, 0:512], in_=bk2[:, :], func=AF.Sqrt, bias=float(EPS), scale=1.0 / 128))
                P.op("dve", [brs], [brs], lambda e: e.reciprocal(out=rs_[:, 0:512], in_=rs_[:, 0:512]))
                ckn, bckn = ntB()
                P.op("dve", [bckf, brs, b_prm], [bckn], lambda e: e.scalar_tensor_tensor(
                    out=ckn[:, 0:512], in0=ckf[:, 0:512], scalar=prm[:, PRM_KN:PRM_KN + 1], in1=rs_[:, 0:512], op0=ALU.mult, op1=ALU.mult))
                for h in range(4):
                    bk, bbk = nbank()
                    P.op("pe", [bckn, b_wsm], [bbk], lambda e: e.matmul(bk[0:64, :], lhsT=wsm[:, WUKV + h * 192:WUKV + h * 192 + 64],
                                                                         rhs=ckn[:, 0:512], start=True, stop=True))
                    P.op("act", [bbk], [b_kdT[tb]], lambda e: e.copy(out=kdT[0:64, h, t0:t0 + 512], in_=bk[0:64, :]))
                wukv_v = wsm[:, WUKV:WUKV + 768].rearrange("p (h d) -> p h d", h=4)[:, :, 64:192]
                for j in range(4):
                    bk, bbk = nbank()
                    P.op("pe", [bckn, b_wsm], [bbk], lambda e: e.matmul(bk[:, :].rearrange("p (h d) -> p h d", h=4),
                                                                         lhsT=ckn[:, j * 128:(j + 1) * 128], rhs=wukv_v, start=True, stop=True))
                    P.op("act", [bbk], [b_vd1[tb]], lambda e: e.copy(out=vd1[:, tb * 4 + j, :, 0:128],
                                                                      in_=bk[:, :].rearrange("p (h d) -> p h d", h=4)))
                bk1, bbk1 = nbank()
                mm_fm(bk1[0:96, :], bbk1, wkr[:, 0], b_wkr, 0, 96, t0, 512, hb)
                bk2, bbk2 = nbank()
                mm_fm(bk2[0:96, :], bbk2, wkr[:, 1], b_wkr, 0, 96, t0, 512, hb)
                ta, bta = ntA()
                tb2, btb2 = ntA()
                P.op("dve", [bbk1, b_rope], [bta], lambda e: e.tensor_tensor(out=ta[64:96, 0:512], in0=bk1[64:96, :], in1=rope[64:96, 2, :], op=ALU.mult))
                P.op("dve", [bbk2, b_rope], [btb2], lambda e: e.tensor_tensor(out=tb2[64:96, 0:512], in0=bk2[64:96, :], in1=rope[64:96, 3, :], op=ALU.mult))
                for h in range(4):
                    P.op("dve", [bta, btb2], [b_kdT[tb]], lambda e: e.tensor_tensor(out=kdT[64:96, h, t0:t0 + 512], in0=ta[64:96, 0:512], in1=tb2[64:96, 0:512], op=ALU.add))
            load_gvec(mem_norm[l:l + 1, :])
            P.dma("sp", [], [b_big], b_big, xq[:, 0:2, :], mem_in[s, :, :].rearrange("(j p) d -> p j d", p=128))
            memT = outsT[0][:, :, :].rearrange("p a b -> p (a b)").rearrange("p (k t) -> p k t", k=8)
            b_memT = b_outsT[0]
            rms_rows(xq, b_big, 2, None, lambda j: memT[:, :, j * 128:(j + 1) * 128], b_memT)
            wm0, bwm0 = ring_load(wkv_b[l, :, 0:512].rearrange("(kc p) n -> p kc n", p=128), 512)
            for h in range(4):
                bk, bbk = nbank()
                mm_fm(bk[:, 0:NMEM], bbk, wm0, bwm0, h * 128, 128, 0, NMEM, [b_memT], rhs_fn=lambda kc: memT[:, kc, :])
                P.op("act", [bbk], [b_kmT], lambda e: e.copy(out=kmT[:, h, :], in_=bk[:, 0:NMEM]))
            wm1, bwm1 = ring_load(wkv_b[l, :, 512:1024].rearrange("(kc p) n -> p kc n", p=128), 512)
            for j in range(2):
                bk, bbk = nbank()

                def emit(e, bk=bk, j=j):
                    ins = None
                    for kc in range(8):
                        ins = e.matmul(bk[:, :], lhsT=memT[:, kc, j * 128:(j + 1) * 128], rhs=wm1[:, kc, 0:512],
                                       start=(kc == 0), stop=(kc == 7))
                    return ins
                P.op("pe", [b_memT, bwm1], [bbk], emit)
                P.op("act", [bbk], [b_vm1], lambda e: e.copy(out=vm1[:, j, :, 0:128], in_=bk[:, :].rearrange("p (h d) -> p h d", h=4)))

            all_h = list(b_hT)
            for qb in range(4):
                t0 = qb * 512
                hb = [b_hT[qb]]
                load_rope(t0, "A")

                acc_first = [True]

                def branch_accumulate(i, oT, boT, slots=(None, None)):
                    init = acc_first[0]
                    acc_first[0] = False
                    if debug and s == 0 and l == 0 and qb == 0:
                        ev = P.dma("sp", [boT], [], bf(f"dbg{i}"), dbg[i], oT[:, :, :])
                        boT.r[ev[0]] = ev[1]
                        out_evs.append(ev)
                    wbi, bwbi = ring_load(wb_b[l, i, :, 0:512].rearrange("(kc p) n -> p kc n", p=128), 512, kcs=4, slot=slots[0])
                    for half in range(2):
                        if half == 1:
                            wbi, bwbi = ring_load(wb_b[l, i, :, 512:1024].rearrange("(kc p) n -> p kc n", p=128), 512, kcs=4, slot=slots[0])
                        wg, bwg = ring_load(wcols(l, O_G + i * 1024 + half * 512, 512), 512, slot=slots[1])
                        for c4 in range(4):
                            cc = half * 4 + c4
                            bkb, bbkb = nbank()
                            mm_fm(bkb[:, :], bbkb, wbi, bwbi, c4 * 128, 128, 0, 512, [boT], kcs=4,
                                  rhs_fn=lambda kc: oT[:, kc, :])
                            bkg, bbkg = nbank()
                            mm_fm(bkg[:, :], bbkg, wg, bwg, c4 * 128, 128, t0, 512, hb)
                            th, bth = ntA()
                            P.op("act", [bbkg], [bth], lambda e: e.activation(out=th[:, 0:512], in_=bkg[:, :], func=AF.Tanh, scale=0.5))
                            if init:
                                P.op("dve", [bth, bbkb], [b_yacc], lambda e: e.scalar_tensor_tensor(
                                    out=yacc[:, cc, :], in0=th[:, 0:512], scalar=1.0, in1=bkb[:, :], op0=ALU.add, op1=ALU.mult))
                            else:
                                pr, bpr = ntA()
                                P.op("dve", [bth, bbkb], [bpr], lambda e: e.scalar_tensor_tensor(
                                    out=pr[:, 0:512], in0=th[:, 0:512], scalar=1.0, in1=bkb[:, :], op0=ALU.add, op1=ALU.mult))
                                P.op("dve", [bpr, b_yacc], [b_yacc], lambda e: e.tensor_tensor(
                                    out=yacc[:, cc, :], in0=yacc[:, cc, :], in1=pr[:, 0:512], op=ALU.add))
                            yield

                def gate_and_transpose(zoff, oi, post_fn=None, slot=None):
                    wz, bwz = ring_load(wcols(l, zoff, 512), 512, slot=slot)
                    for j in range(4):
                        bk, bbk = nbank()
                        mm_tm(bk[:, :], bbk, wz, bwz, 0, 512, t0 + j * 128, hb)
                        zz, bzz = ntA()
                        silu2(zz[:, 0:512], bzz, bk[:, :], bbk)
                        ob, bob = ntB()
                        if post_fn is None:
                            P.op("dve", [bzz, b_otok], [bob], lambda e: e.scalar_tensor_tensor(
                                out=ob[:, 0:512], in0=otok[:, j, :], scalar=0.5, in1=zz[:, 0:512], op0=ALU.mult, op1=ALU.mult))
                        else:
                            post_fn(j, zz, bzz, ob, bob)
                        yield
                        transpose_to(lambda: outsT[oi][:, :, j * 128:(j + 1) * 128], b_outsT[oi], ob, bob, 4)
                        yield

                lo = max(0, t0 - 15)
                hi = min(S, t0 + 527)
                nw = hi - lo
                off = lo - (t0 - 15)
                hbc = [b_hT[qb]] + ([b_hT[qb - 1]] if qb > 0 else []) + ([b_hT[qb + 1]] if qb < 3 else [])
                n1 = nw // 2
                n2 = nw - n1
                cw = {}

                def c_setup():
                        cw["wga"], cw["bwga"] = ring_load(wcols(l, O_GA, 512), 512)
                        cw["wgb"], cw["bwgb"] = ring_load(wcols(l, O_GB, 512), 512)

                def c_chunk(c):
                    wga, bwga, wgb, bwgb = cw["wga"], cw["bwga"], cw["wgb"], cw["bwgb"]
                    ue, bue = ntA()
                    if nw < 542:
                        P.op("pool", [], [bue], lambda e: e.memset(ue[:, 0:542], 0.0))
                    for (a, n) in ((0, n1), (n1, n2)):
                        bka, bbka = nbank()
                        mm_fm(bka[:, 0:n], bbka, wga, bwga, c * 128, 128, lo + a, n, hbc)
                        bkb, bbkb = nbank()
                        mm_fm(bkb[:, 0:n], bbkb, wgb, bwgb, c * 128, 128, lo + a, n, hbc)
                        th, bth = ntA()
                        P.op("act", [bbkb], [bth], lambda e: e.activation(out=th[:, 0:n], in_=bkb[:, 0:n], func=AF.Tanh, scale=0.5))
                        P.op("dve", [bth, bbka], [bue], lambda e: e.scalar_tensor_tensor(
                            out=ue[:, off + a:off + a + n], in0=th[:, 0:n], scalar=1.0, in1=bka[:, 0:n], op0=ALU.add, op1=ALU.mult))
                    a1_, ba1 = ntA()
                    a2_, ba2 = ntA()
                    accs2 = ((a1_, ba1), (a2_, ba2))
                    for k in range(31):
                        wk_ = cw31[:, c, k:k + 1]
                        acc_, bacc_ = accs2[k % 2]
                        if k < 2:
                            P.op("dve", [bue, b_prm], [bacc_], lambda e: e.tensor_scalar(out=acc_[:, 0:512], in0=ue[:, k:k + 512], scalar1=wk_, scalar2=None, op0=ALU.mult))
                        else:
                            P.op("dve", [bue, b_prm, bacc_], [bacc_], lambda e: e.scalar_tensor_tensor(
                                out=acc_[:, 0:512], in0=ue[:, k:k + 512], scalar=wk_, in1=acc_[:, 0:512], op0=ALU.mult, op1=ALU.add))
                    P.op("dve", [ba1, ba2], [ba1], lambda e: e.tensor_tensor(out=a1_[:, 0:512], in0=a1_[:, 0:512], in1=a2_[:, 0:512], op=ALU.add))
                    P.op("dve", [ba1, b_prm], [b_uconv], lambda e: e.tensor_scalar(
                        out=uconv[:, c, :], in0=a1_[:, 0:512], scalar1=prm[:, PRM_CB + c:PRM_CB + c + 1], scalar2=None, op0=ALU.add))

                if "A" not in skip:
                    qz = []
                    for h in range(4):
                        reg = big_bf[:, h * 1024:(h + 1) * 1024] if h < 2 else tmpB[h - 2][:, :]
                        qz.append(reg.rearrange("p (s t) -> p s t", s=2))
                    fence_w = [b_big, b_qT, b_otok] + b_pT + b_tmpB
                    for h in range(4):
                        P.op("pool", [], fence_w, lambda e: e.memset(qz[h][64:128, 0, :], 0.0))
                        P.op("pool", [], fence_w, lambda e: e.memset(qz[h][0:64, 1, :], 0.0))
                    wq, bwq = ring_load(wcols(l, O_AQ, 512), 512)
                    wqs, bwqs = ring_load(wsw_b[l, :, 0:512].rearrange("(kc p) n -> p kc n", p=128), 512)
                    for c in range(4):
                        bk1, bbk1 = nbank()
                        mm_fm(bk1[:, :], bbk1, wq, bwq, c * 128, 128, t0, 512, hb)
                        bk2, bbk2 = nbank()
                        mm_fm(bk2[:, :], bbk2, wqs, bwqs, c * 128, 128, t0, 512, hb)
                        ta, bta = ntA()
                        tb2, btb2 = ntA()
                        P.op("dve", [bbk1, b_ropeA], [bta], lambda e: e.tensor_tensor(out=ta[:, 0:512], in0=bk1[:, :], in1=rope[:, 0, :], op=ALU.mult))
                        P.op("dve", [bbk2, b_ropeA], [btb2], lambda e: e.tensor_tensor(out=tb2[:, 0:512], in0=bk2[:, :], in1=rope[:, 1, :], op=ALU.mult))
                        for sub in range(2):
                            rs = slice(sub * 64, sub * 64 + 64)
                            P.op("dve", [bta, btb2], [b_qT] + ([b_tmpB[c - 2]] if c >= 2 else []), lambda e: e.tensor_tensor(
                                out=qz[c][rs, sub, :], in0=ta[rs, 0:512], in1=tb2[rs, 0:512], op=ALU.add))

                    def evac_a(tag, accs, baccs):
                        h, sub = tag
                        if sub == 0:
                            return
                        a0 = (pacc[0], pacc[1])
                        a1 = (pacc[2], pacc[3])
                        for j in range(4):
                            c0 = (j % 2) * 129
                            A0 = a0[j // 2]
                            A1 = a1[j // 2]
                            P.op("dve", [b_pacc[j // 2]], [b_st], lambda e: e.reciprocal(out=st[:, 8:9], in_=A0[:, c0 + 128:c0 + 129]))
                            P.op("dve", [b_pacc[2 + j // 2]], [b_st], lambda e: e.reciprocal(out=st[:, 9:10], in_=A1[:, c0 + 128:c0 + 129]))
                            P.op("dve", [b_st, b_lam], [b_st], lambda e: e.tensor_tensor(out=st[:, 10:11], in0=st[:, 9:10], in1=lam_t[:, 4:5], op=ALU.mult))
                            tt, btt = ntA()
                            P.op("dve", [b_pacc[2 + j // 2], b_st], [btt], lambda e: e.tensor_scalar(
                                out=tt[:, 0:128], in0=A1[:, c0:c0 + 128], scalar1=st[:, 10:11], scalar2=None, op0=ALU.mult))
                            P.op("dve", [b_pacc[j // 2], b_st, btt], [b_otok], lambda e: e.scalar_tensor_tensor(
                                out=otok[:, j, h * 128:(h + 1) * 128], in0=A0[:, c0:c0 + 128], scalar=st[:, 8:9], in1=tt[:, 0:128],
                                op0=ALU.mult, op1=ALU.add))

                    heads = []
                    for h in range(4):
                        for sub in range(2):
                            rs = slice(sub * 64, sub * 64 + 64)
                            heads.append((qz[h][:, sub, :],
                                          (lambda kc, h=h: kTa[:, h, kc * 128:(kc + 1) * 128]),
                                          (lambda kc, h=h: v1a[:, kc, h, 0:129]),
                                          sub, (h, sub)))
                    if "C" not in skip:
                        c_setup()
                    attention(heads, 16, 0.125, b_qT, list(b_kTa) + b_tmpB, list(b_v1a), evac_a,
                              pre_head=(lambda hi_: c_chunk(hi_ // 2) if (hi_ % 2 == 0 and "C" not in skip) else None))

                    b_st2 = bf("st2")

                    def post_a_pre():
                        for j in range(4):
                            sq_, bsq = ntA()
                            P.op("act", [b_otok], [bsq], lambda e: e.activation(out=sq_[:, 0:512], in_=otok[:, j, :], func=AF.Square))
                            P.op("dve", [bsq], [b_st2], lambda e: e.tensor_reduce(out=st[:, 20 + 4 * j:24 + 4 * j], in_=sq_[:, 0:512].rearrange("p (h d) -> p h d", h=4), axis=AX.X, op=ALU.add))
                        rsqrt_inplace(st[:, 20:36], b_st2, 1.0 / 128)

                    def post_a(j, zz, bzz, ob, bob):
                        on_, bon = ntA()
                        P.op("dve", [b_otok, b_st2], [bon], lambda e: e.tensor_tensor(
                            out=on_[:, 0:512].rearrange("p (h d) -> p h d", h=4), in0=otok[:, j, :].rearrange("p (h d) -> p h d", h=4),
                            in1=st[:, 20 + 4 * j:24 + 4 * j].unsqueeze(2).broadcast_to([128, 4, 128]), op=ALU.mult))
                        P.op("dve", [bon, b_subln], [bon], lambda e: e.tensor_tensor(out=on_[:, 0:512].rearrange("p (h d) -> p h d", h=4), in0=on_[:, 0:512].rearrange("p (h d) -> p h d", h=4), in1=subln[:, :].unsqueeze(1).broadcast_to([128, 4, 128]), op=ALU.mult))
                        P.op("dve", [bon, bzz], [bob], lambda e: e.tensor_tensor(out=ob[:, 0:512], in0=on_[:, 0:512], in1=zz[:, 0:512], op=ALU.mult))

                if "C" not in skip:
                    oT, boT = outsT[1], b_outsT[1]
                    if "A" in skip:
                        c_setup()
                        for c in range(4):
                            c_chunk(c)
                    def c_rest():
                        bkm, bbkm = nbank()

                        def emit_mean(e, bkm=bkm):
                            ins = None
                            for c in range(4):
                                ins = e.matmul(bkm[:, :], lhsT=onesf[:], rhs=uconv[:, c, :], start=(c == 0), stop=(c == 3))
                            return ins
                        P.op("pe", [b_uconv, bf("onesf")], [bbkm], emit_mean)
                        yield
                        for c in range(4):
                            P.op("dve", [bbkm, b_uconv], [b_uconv], lambda e: e.scalar_tensor_tensor(
                                out=uconv[:, c, :], in0=bkm[:, :], scalar=-1.0 / 512, in1=uconv[:, c, :], op0=ALU.mult, op1=ALU.add))
                        sqs = []
                        for c in range(4):
                            sq_, bsq = ntA()
                            P.op("act", [b_uconv], [bsq], lambda e: e.activation(out=sq_[:, 0:512], in_=uconv[:, c, :], func=AF.Square))
                            sqs.append((sq_, bsq))
                        yield
                        bkv, bbkv = nbank()

                        def emit_var(e, bkv=bkv, sqs=sqs):
                            ins = None
                            for c in range(4):
                                ins = e.matmul(bkv[:, :], lhsT=onesf[:], rhs=sqs[c][0][:, 0:512], start=(c == 0), stop=(c == 3))
                            return ins
                        P.op("pe", [x[1] for x in sqs] + [bf("onesf")], [bbkv], emit_var)
                        rs_, brs = ntA()
                        P.op("act", [bbkv], [brs], lambda e: e.activation(out=rs_[:, 0:512], in_=bkv[:, :], func=AF.Sqrt, bias=float(EPS), scale=1.0 / 512))
                        P.op("dve", [brs], [brs], lambda e: e.reciprocal(out=rs_[:, 0:512], in_=rs_[:, 0:512]))
                        yield
                        wcz, bwcz = ring_load(wcols(l, O_CZ, 512), 512, slot=2)
                        for c in range(4):
                            P.op("dve", [brs, b_uconv], [b_uconv], lambda e: e.tensor_tensor(out=uconv[:, c, :], in0=uconv[:, c, :], in1=rs_[:, 0:512], op=ALU.mult))
                            P.op("dve", [b_uconv, b_prm], [b_uconv], lambda e: e.tensor_scalar(
                                out=uconv[:, c, :], in0=uconv[:, c, :], scalar1=prm[:, PRM_LG + c:PRM_LG + c + 1],
                                scalar2=prm[:, PRM_LB + c:PRM_LB + c + 1], op0=ALU.mult, op1=ALU.add))
                        yield
                        for c in range(4):
                            s1, bs1 = ntA()
                            silu2(s1[:, 0:512], bs1, uconv[:, c, :], b_uconv)
                            bk, bbk = nbank()
                            mm_fm(bk[:, :], bbk, wcz, bwcz, c * 128, 128, t0, 512, hb)
                            s2, bs2 = ntA()
                            silu2(s2[:, 0:512], bs2, bk[:, :], bbk)
                            P.op("dve", [bs1, bs2], [boT], lambda e: e.scalar_tensor_tensor(
                                out=oT[:, c, :], in0=s1[:, 0:512], scalar=0.25, in1=s2[:, 0:512], op0=ALU.mult, op1=ALU.mult))
                            yield
                    run(c_rest())
                    post_a_pre()
                    run(gate_and_transpose(O_AZ, 0, post_a, slot=2), branch_accumulate(2, outsT[1], b_outsT[1], slots=(0, 1)))

                hal_l = t0 - 1 >= 0
                hal_r = t0 + 512 < S
                hbn = [b_hT[qb]] + ([b_hT[qb - 1]] if hal_l else []) + ([b_hT[qb + 1]] if hal_r else [])

                def b_chunk(c):
                    oT, boT = outsT[1], b_outsT[1]
                    w, bw = nring()
                    for i4 in range(4):
                        P.dma("sp", [b_wts], [bw] if i4 == 0 else [], bw, w[:, :, i4 * 128:(i4 + 1) * 128],
                              wcols(l, O_BB + i4 * 512 + c * 128, 128))
                    bw.w = {bw.sem_ld: bw.cnt_ld}
                    tcm, btcm = ntA()
                    P.op("pool", [], [btcm], lambda e: e.memset(tcm[:, 0:1], 0.0))
                    P.op("pool", [], [btcm], lambda e: e.memset(tcm[:, 513:514], 0.0))
                    bk, bbk = nbank()
                    mm_fm(bk[:, :], bbk, w, bw, 128, 128, t0, 512, hb)
                    P.op("act", [bbk], [btcm], lambda e: e.copy(out=tcm[:, 1:513], in_=bk[:, :]))
                    bkh, bbkh = nbank()
                    if hal_l:
                        mm_fm(bkh[:, 0:1], bbkh, w, bw, 128, 128, t0 - 1, 1, hbn)
                        P.op("act", [bbkh], [btcm], lambda e: e.copy(out=tcm[:, 0:1], in_=bkh[:, 0:1]))
                    if hal_r:
                        mm_fm(bkh[:, 1:2], bbkh, w, bw, 128, 128, t0 + 512, 1, hbn)
                        P.op("act", [bbkh], [btcm], lambda e: e.copy(out=tcm[:, 513:514], in_=bkh[:, 1:2]))
                    bk, bbk = nbank()
                    mm_fm(bk[:, :], bbk, w, bw, 256, 128, t0, 512, hb)
                    P.op("dve", [bbk, btcm], [btcm], lambda e: e.tensor_tensor(out=tcm[:, 1:513], in0=tcm[:, 1:513], in1=bk[:, :], op=ALU.mult))
                    bkh, bbkh = nbank()
                    if hal_l:
                        mm_fm(bkh[:, 0:1], bbkh, w, bw, 256, 128, t0 - 1, 1, hbn)
                        P.op("dve", [bbkh, btcm], [btcm], lambda e: e.tensor_tensor(out=tcm[:, 0:1], in0=tcm[:, 0:1], in1=bkh[:, 0:1], op=ALU.mult))
                    if hal_r:
                        mm_fm(bkh[:, 1:2], bbkh, w, bw, 256, 128, t0 + 512, 1, hbn)
                        P.op("dve", [bbkh, btcm], [btcm], lambda e: e.tensor_tensor(out=tcm[:, 513:514], in0=tcm[:, 513:514], in1=bkh[:, 1:2], op=ALU.mult))
                    w0 = prm[:, PRM_SC + c * 3 + 0:PRM_SC + c * 3 + 1]
                    w1 = prm[:, PRM_SC + c * 3 + 1:PRM_SC + c * 3 + 2]
                    w2 = prm[:, PRM_SC + c * 3 + 2:PRM_SC + c * 3 + 3]
                    P.op("dve", [btcm, b_prm], [b_uconv], lambda e: e.tensor_scalar(out=uconv[:, c, :], in0=tcm[:, 0:512], scalar1=w0, scalar2=None, op0=ALU.mult))
                    P.op("dve", [btcm, b_prm, b_uconv], [b_uconv], lambda e: e.scalar_tensor_tensor(
                        out=uconv[:, c, :], in0=tcm[:, 1:513], scalar=w1, in1=uconv[:, c, :], op0=ALU.mult, op1=ALU.add))
                    P.op("dve", [btcm, b_prm, b_uconv], [b_uconv], lambda e: e.scalar_tensor_tensor(
                        out=uconv[:, c, :], in0=tcm[:, 2:514], scalar=w2, in1=uconv[:, c, :], op0=ALU.mult, op1=ALU.add))
                    bk, bbk = nbank()
                    mm_fm(bk[:, :], bbk, w, bw, 0, 128, t0, 512, hb)
                    P.op("dve", [bbk, b_uconv], [b_uconv], lambda e: e.tensor_tensor(out=uconv[:, c, :], in0=uconv[:, c, :], in1=bk[:, :], op=ALU.mult))
                    bk, bbk = nbank()
                    mm_fm(bk[:, :], bbk, w, bw, 384, 128, t0, 512, hb)
                    zz, bzz = ntA()
                    silu2(zz[:, 0:512], bzz, bk[:, :], bbk)
                    P.op("dve", [bzz, b_uconv], [boT], lambda e: e.scalar_tensor_tensor(
                        out=oT[:, c, :], in0=uconv[:, c, :], scalar=0.5, in1=zz[:, 0:512], op0=ALU.mult, op1=ALU.mult))

                if "B" not in skip and "D" in skip:
                    for c in range(4):
                        b_chunk(c)
                    run(branch_accumulate(1, outsT[1], b_outsT[1]))

                if "D" not in skip:
                    def d_prep():
                        load_rope(t0, "D")
                        wcq, bwcq = ring_load(wcols(l, O_CQ, 256), 256, slot=2)
                        cqf = []
                        for c in range(2):
                            bk, bbk = nbank()
                            mm_fm(bk[:, :], bbk, wcq, bwcq, c * 128, 128, t0, 512, hb)
                            cf, bcf = ntA()
                            P.op("act", [bbk], [bcf], lambda e: e.copy(out=cf[:, 0:512], in_=bk[:, :]))
                            sq_, bsq = ntA()
                            P.op("act", [bcf], [bsq], lambda e: e.activation(out=sq_[:, 0:512], in_=cf[:, 0:512], func=AF.Square))
                            cqf.append((cf, bcf, sq_, bsq))
                        yield
                        bk, bbk = nbank()

                        def emit_q(e, bk=bk, cqf=cqf):
                            ins = None
                            for c in range(2):
                                ins = e.matmul(bk[:, :], lhsT=onesf[:], rhs=cqf[c][2][:, 0:512], start=(c == 0), stop=(c == 1))
                            return ins
                        P.op("pe", [cqf[0][3], cqf[1][3], bf("onesf")], [bbk], emit_q)
                        rs_, brs = cqf[0][2], cqf[0][3]
                        P.op("act", [bbk, cqf[1][3]], [brs], lambda e: e.activation(out=rs_[:, 0:512], in_=bk[:, :], func=AF.Sqrt, bias=float(EPS), scale=1.0 / 256))
                        P.op("dve", [brs], [brs], lambda e: e.reciprocal(out=rs_[:, 0:512], in_=rs_[:, 0:512]))
                        cqn, bcqn = ntB()
                        for c in range(2):
                            P.op("dve", [cqf[c][1], brs, b_prm], [bcqn], lambda e: e.scalar_tensor_tensor(
                                out=cqn[:, c * 512:(c + 1) * 512], in0=cqf[c][0][:, 0:512], scalar=prm[:, PRM_QN + c:PRM_QN + c + 1],
                                in1=rs_[:, 0:512], op0=ALU.mult, op1=ALU.mult))
                        yield
                        for h in range(4):
                            bk1, bbk1 = nbank()
                            bk2, bbk2 = nbank()
                            for (bk_, bbk_, wo_) in ((bk1, bbk1, WUQ), (bk2, bbk2, WUQS)):
                                def emit_uq(e, bk_=bk_, wo_=wo_, h=h):
                                    ins = None
                                    for c in range(2):
                                        ins = e.matmul(bk_[0:96, :], lhsT=wsm[:, wo_ + c * 384 + h * 96:wo_ + c * 384 + (h + 1) * 96],
                                                       rhs=cqn[:, c * 512:(c + 1) * 512], start=(c == 0), stop=(c == 1))
                                    return ins
                                P.op("pe", [bcqn, b_wsm], [bbk_], emit_uq)
                            P.op("act", [bbk1, b_big], [b_qT], lambda e: e.copy(out=qT[0:64, h, :], in_=bk1[0:64, :]))
                            ta, bta = ntA()
                            tb2, btb2 = ntA()
                            P.op("dve", [bbk1, b_rope], [bta], lambda e: e.tensor_tensor(out=ta[64:96, 0:512], in0=bk1[64:96, :], in1=rope[64:96, 2, :], op=ALU.mult))
                            P.op("dve", [bbk2, b_rope], [btb2], lambda e: e.tensor_tensor(out=tb2[64:96, 0:512], in0=bk2[64:96, :], in1=rope[64:96, 3, :], op=ALU.mult))
                            P.op("dve", [bta, btb2, b_big], [b_qT], lambda e: e.tensor_tensor(out=qT[64:96, h, :], in0=ta[64:96, 0:512], in1=tb2[64:96, 0:512], op=ALU.add))
                            yield

                    run(d_prep(), branch_accumulate(0, outsT[0], b_outsT[0], slots=(0, 1)))

                    def evac_plain(tag, accs, baccs):
                        h = tag
                        for j in range(4):
                            c0 = (j % 2) * 129
                            Aj = accs[j // 2]
                            P.op("dve", [baccs[j // 2]], [b_st], lambda e: e.reciprocal(out=st[:, 8:9], in_=Aj[:, c0 + 128:c0 + 129]))
                            P.op("dve", [baccs[j // 2], b_st], [b_otok], lambda e: e.tensor_scalar(
                                out=otok[:, j, h * 128:(h + 1) * 128], in0=Aj[:, c0:c0 + 128], scalar1=st[:, 8:9], scalar2=None, op0=ALU.mult))

                    heads = []
                    for h in range(4):
                        heads.append((qT[0:96, h, :],
                                      (lambda kc, h=h: kdT[0:96, h, kc * 128:(kc + 1) * 128]),
                                      (lambda kc, h=h: vd1[:, kc, h, 0:129]),
                                      h % 2, h))
                    attention(heads, 16, 96 ** -0.5, b_qT, list(b_kdT), list(b_vd1), evac_plain,
                              pre_head=(lambda hi_: b_chunk(hi_) if "B" not in skip else None))
                    run(gate_and_transpose(O_DZ, 0, slot=2), branch_accumulate(1, outsT[1], b_outsT[1], slots=(0, 1)))

                if "E" not in skip:
                    weq, bweq = ring_load(wcols(l, O_EQ, 512), 512)
                    for h in range(4):
                        bk, bbk = nbank()
                        mm_fm(bk[:, :], bbk, weq, bweq, h * 128, 128, t0, 512, hb)
                        P.op("act", [bbk, b_big], [b_qT], lambda e: e.copy(out=qT[:, h, :], in_=bk[:, :]))
                    heads = []
                    for h in range(4):
                        heads.append((qT[:, h, :],
                                      (lambda kc, h=h: kmT[:, h, kc * 128:(kc + 1) * 128]),
                                      (lambda kc, h=h: vm1[:, kc, h, 0:129]),
                                      h % 2, h))
                    attention(heads, 2, 128 ** -0.5, b_qT, [b_kmT], [b_vm1], evac_plain)
                    run(gate_and_transpose(O_EZ, 1, slot=2), branch_accumulate(3, outsT[0], b_outsT[0], slots=(0, 1)))
                    run(branch_accumulate(4, outsT[1], b_outsT[1]))

                for cc in range(8):
                    P.op("act", [b_yacc], [b_ymT], lambda e: e.activation(out=ymT[:, cc, :], in_=yacc[:, cc, :], func=AF.Copy, scale=0.5))
                load_gvec(norm_post[l:l + 1, :])
                yo = yacc[:, :, :].rearrange("p a b -> p (a b)").rearrange("p (j d) -> p j d", j=4)
                for half in range(2):
                    wo_, bwo = ring_load(wo_b[l, :, half * 512:(half + 1) * 512].rearrange("(kc p) n -> p kc n", p=128), 512)
                    for j in range(4):
                        bk, bbk = nbank()

                        def emit_o(e, bk=bk, j=j, wo_=wo_):
                            ins = None
                            for kc in range(8):
                                ins = e.matmul(bk[:, :], lhsT=ymT[:, kc, j * 128:(j + 1) * 128], rhs=wo_[:, kc, 0:512],
                                               start=(kc == 0), stop=(kc == 7))
                            return ins
                        P.op("pe", [b_ymT, bwo], [bbk], emit_o)
                        P.op("act", [bbk, b_ymT], [b_yacc], lambda e: e.copy(out=yo[:, j, half * 512:(half + 1) * 512], in_=bk[:, :]))
                P.op("dve", [], [b_st], lambda e: e.memset(st[:, 16:20], 0.0))
                for j in range(4):
                    tb_, btb = ntB()
                    P.op("act", [b_yacc], [btb, b_st], lambda e: e.activation(out=tb_[:, :], in_=yo[:, j, :], func=AF.Square, accum_out=st[:, 16 + j:17 + j]))
                rsqrt_inplace(st[:, 16:20], b_st, 1.0 / D)
                for j in range(4):
                    P.op("dve", [b_yacc, b_st, b_gvec], [b_yacc], lambda e: e.scalar_tensor_tensor(
                        out=yo[:, j, :], in0=yo[:, j, :], scalar=st[:, 16 + j:17 + j], in1=gvec[:], op0=ALU.mult, op1=ALU.mult))
                ev = P.dma("pool", [b_yacc], [b_dst], b_yacc, x_dst[s, t0:t0 + 512, :].rearrange("(j p) d -> p j d", p=128), yo, accum=True)
                b_yacc.r[ev[0]] = ev[1]
                if x_dst is not xs:
                    out_evs.append(ev)

    P.final_wait("sp", out_evs)
    P.final_wait("pool", out_evs)
    P.emit_all()
    es.close()
    return nc, P


def _rope_tables():
    t = np.arange(S, dtype=np.float32)
    tab = np.zeros((4, 128, S), np.float32)
    tab[0] = 1.0
    tab[2] = 1.0
    inv_a = (np.float32(500000.0) ** (-(np.arange(0, 16, 2, dtype=np.float32) / np.float32(16)))).astype(np.float32)
    ang_a = (t[:, None] * inv_a[None, :]).astype(np.float32)
    ca, sa = np.cos(ang_a).astype(np.float32).T, np.sin(ang_a).astype(np.float32).T
    for base in (0, 64):
        tab[0, base:base + 8] = ca
        tab[0, base + 8:base + 16] = ca
        tab[1, base:base + 8] = -sa
        tab[1, base + 8:base + 16] = sa
    inv_d = (np.float32(10000.0) ** (-(np.arange(0, 32, 2, dtype=np.float32) / np.float32(32)))).astype(np.float32)
    ang_d = (t[:, None] * inv_d[None, :]).astype(np.float32)
    cd, sd = np.cos(ang_d).astype(np.float32).T, np.sin(ang_d).astype(np.float32).T
    tab[2, 64:80] = cd
    tab[2, 80:96] = cd
    tab[3, 64:80] = -sd
    tab[3, 80:96] = sd
    return tab


_CACHE = {}


def kernel(x_prompt, x_sample, mem_prompt, mem_sample, norm_pre, w_in, diff_lambda,
           diff_subln, sconv_w, conf_dw_w, conf_dw_b, conf_ln_g, conf_ln_b,
           mla_q_norm, mla_w_uq, mla_kv_norm, mla_w_ukv, mem_norm, mem_w_kv,
           w_branch, w_out, norm_post):
    f = lambda a: np.ascontiguousarray(np.asarray(a, dtype=np.float32))
    x_prompt, x_sample, mem_prompt, mem_sample = map(f, (x_prompt, x_sample, mem_prompt, mem_sample))
    if "nc" not in _CACHE:
        _CACHE["nc"] = build_program()[0]
    nc = _CACHE["nc"]
    shared = dict(norm_pre=f(norm_pre), w_in=f(w_in), diff_lambda=f(diff_lambda), diff_subln=f(diff_subln),
                  sconv_w=f(sconv_w), conf_dw_w=f(conf_dw_w), conf_dw_b=f(conf_dw_b), conf_ln_g=f(conf_ln_g),
                  conf_ln_b=f(conf_ln_b), mla_q_norm=f(mla_q_norm), mla_w_uq=f(mla_w_uq),
                  mla_kv_norm=f(mla_kv_norm), mla_w_ukv=f(mla_w_ukv), mem_norm=f(mem_norm),
                  mem_w_kv=f(mem_w_kv), w_branch=f(w_branch), w_out=f(w_out), norm_post=f(norm_post),
                  c_ident=np.eye(128, dtype=np.float32), c_rope=_rope_tables())
    in_maps = []
    for c in range(NCORES):
        xa = np.concatenate([x_prompt[2 * c:2 * c + 2], x_sample[4 * c:4 * c + 4]], axis=0)
        ma = np.concatenate([mem_prompt[2 * c:2 * c + 2], mem_sample[4 * c:4 * c + 4]], axis=0)
        d = dict(shared)
        d["x_all"] = np.ascontiguousarray(xa)
        d["mem_all"] = np.ascontiguousarray(ma)
        in_maps.append(d)
    res = run_bass_kernel_spmd(nc, in_maps, core_ids=list(range(NCORES)))
    yp = np.empty_like(x_prompt)
    ysm = np.empty_like(x_sample)
    for c in range(NCORES):
        ya = res.results[c]["y_all"]
        yp[2 * c:2 * c + 2] = ya[0:2]
        ysm[4 * c:4 * c + 4] = ya[2:6]
    return (yp, ysm)
```
